# Optimizing a Trainium2 kernel written in Bass

```python
import math
import jax, jax.numpy as jnp
from jax import lax
import numpy as np

D_MODEL = 1024
BATCH = 8
SEQ = 2048
DEPTH = 4
DEC_BATCH = 128
DEC_SEQ = 1
PAST_LEN = 16384
PAGE_SIZE = 128

D_CONV = 512
H_M = 4
DH_M = 256
D_M = H_M * DH_M
CONV_W = 3
D_FF = 2816
CHUNK = 128
EPS = 1e-5
ALPHA = (2.0 * DEPTH) ** 0.25
BETA = (8.0 * DEPTH) ** -0.25

SPLITS = (D_CONV, D_CONV, D_CONV, D_M, D_M, D_M, D_M, H_M, H_M, D_MODEL, D_MODEL)
D_IN = sum(SPLITS)
SPLIT_IDX = tuple(int(s) for s in np.cumsum(SPLITS)[:-1])

kernel_name = "hybrid_shortconv_mlstm_convffn_deepnorm_step"


def layer_norm(x, g, b):
    xf = x.astype(jnp.float32)
    mu = jnp.mean(xf, axis=-1, keepdims=True)
    var = jnp.mean(jnp.square(xf - mu), axis=-1, keepdims=True)
    y = (xf - mu) * lax.rsqrt(var + EPS) * g.astype(jnp.float32) + b.astype(jnp.float32)
    return y.astype(x.dtype)


def causal_dwconv(u, buf, w):
    t = u.shape[1]
    full = jnp.concatenate([buf.astype(u.dtype), u], axis=1)
    y = full[:, 0:t] * w[0]
    for j in range(1, CONV_W):
        y = y + full[:, j:j + t] * w[j]
    return y, full[:, -(CONV_W - 1):]


def mlstm_chunked(q, k, v, li, lf, c0, n0, m0):
    bsz, t, h, _ = q.shape
    dv = v.shape[-1]
    L = CHUNK if t % CHUNK == 0 else t
    nc = t // L
    f32 = jnp.float32

    def seq_chunks(a):
        return a.astype(f32).reshape(bsz, nc, L, h, -1).transpose(1, 0, 3, 2, 4)

    def gate_chunks(a):
        return a.reshape(bsz, nc, L, h).transpose(1, 0, 3, 2)

    xs = (seq_chunks(q), seq_chunks(k), seq_chunks(v), gate_chunks(li), gate_chunks(lf))
    causal = jnp.tril(jnp.ones((L, L), dtype=bool))

    def step(carry, inp):
        c, n, m = carry
        qc, kc, vc, lic, lfc = inp
        b = jnp.cumsum(lfc, axis=-1)
        d = b[..., :, None] - b[..., None, :] + lic[..., None, :]
        d = jnp.where(causal, d, -jnp.inf)
        m_inter = b + m[..., None]
        m_t = jnp.maximum(m_inter, jnp.max(d, axis=-1))
        s = jnp.einsum('bhtd,bhsd->bhts', qc, kc) * jnp.exp(d - m_t[..., None])
        scale_inter = jnp.exp(m_inter - m_t)
        numer = (jnp.einsum('bhts,bhsv->bhtv', s, vc)
                 + scale_inter[..., None] * jnp.einsum('bhtk,bhkv->bhtv', qc, c))
        den = jnp.sum(s, axis=-1) + scale_inter * jnp.einsum('bhtk,bhk->bht', qc, n)
        hc = numer / jnp.maximum(jnp.abs(den), jnp.exp(-m_t))[..., None]
        b_last = b[..., -1]
        w = b_last[..., None] - b + lic
        m_new = jnp.maximum(b_last + m, jnp.max(w, axis=-1))
        decay = jnp.exp(b_last + m - m_new)
        w = jnp.exp(w - m_new[..., None])
        c_new = decay[..., None, None] * c + jnp.einsum('bhsk,bhsv->bhkv', kc * w[..., None], vc)
        n_new = decay[..., None] * n + jnp.einsum('bhs,bhsk->bhk', w, kc)
        return (c_new, n_new, m_new), hc

    (c, n, m), hs = lax.scan(step, (c0.astype(f32), n0.astype(f32), m0.astype(f32)), xs)
    h_out = hs.transpose(1, 0, 3, 2, 4).reshape(bsz, t, h, dv)
    return h_out, c, n, m


def trunk_layer(x, sconv_buf, c0, n0, m0, ffn_buf,
                w_in, b_igate, b_fgate, w_conv_mix, mhln_g, w_proj_a, w_proj_b, w_mix_out,
                ln1_g, ln1_b, w_ffn_up, w_ffn_conv, w_ffn_down, ln2_g, ln2_b):
    bsz, t, _ = x.shape
    z = x @ w_in
    bg, cg, xv, q, k, v, og, ig, fg, ga, gb = jnp.split(z, SPLIT_IDX, axis=-1)

    conv_out, sconv_new = causal_dwconv(cg * xv, sconv_buf, w_conv_mix)
    ya = (bg * conv_out) @ w_proj_a

    q = q.reshape(bsz, t, H_M, DH_M)
    k = k.reshape(bsz, t, H_M, DH_M) * (DH_M ** -0.5)
    v = v.reshape(bsz, t, H_M, DH_M)
    li = (ig + b_igate).astype(jnp.float32)
    lf = jax.nn.log_sigmoid((fg + b_fgate).astype(jnp.float32))
    hm, c_new, n_new, m_new = mlstm_chunked(q, k, v, li, lf, c0, n0, m0)
    mu = jnp.mean(hm, axis=-1, keepdims=True)
    var = jnp.mean(jnp.square(hm - mu), axis=-1, keepdims=True)
    hm = ((hm - mu) * lax.rsqrt(var + EPS)).reshape(bsz, t, D_M) * mhln_g.astype(jnp.float32)
    hm = jax.nn.sigmoid(og) * hm.astype(x.dtype)
    yb = hm @ w_proj_b

    merged = jax.nn.sigmoid(ga) * ya + jax.nn.sigmoid(gb) * yb
    x = layer_norm(ALPHA * x + merged @ w_mix_out, ln1_g, ln1_b)

    up = x @ w_ffn_up
    up_c, ffn_new = causal_dwconv(up, ffn_buf, w_ffn_conv)
    g, u = jnp.split(up_c, 2, axis=-1)
    x = layer_norm(ALPHA * x + (jax.nn.silu(g) * u) @ w_ffn_down, ln2_g, ln2_b)

    return (x, sconv_new, c_new.astype(c0.dtype), n_new.astype(n0.dtype),
            m_new.astype(m0.dtype), ffn_new)


def setup_inputs(seed: int = 0) -> dict:
    key = jax.random.key(seed)
    ks = jax.random.split(key, 32)
    nrm = jax.random.normal
    f32 = jnp.float32
    x_prompt = nrm(ks[0], (BATCH, SEQ, D_MODEL), f32)
    x_sample = nrm(ks[1], (DEC_BATCH, DEC_SEQ, D_MODEL), f32)
    cache_sconv = 0.5 * nrm(ks[2], (DEPTH, DEC_BATCH, CONV_W - 1, D_CONV), f32)
    state_mlstm_C = 0.1 * nrm(ks[3], (DEPTH, DEC_BATCH, H_M, DH_M, DH_M), f32)
    state_mlstm_n = 0.1 * nrm(ks[4], (DEPTH, DEC_BATCH, H_M, DH_M), f32)
    state_mlstm_m = jax.random.uniform(ks[5], (DEPTH, DEC_BATCH, H_M), f32, 0.0, 5.0)
    cache_ffn_conv = nrm(ks[6], (DEPTH, DEC_BATCH, CONV_W - 1, 2 * D_FF), f32)

    w_in = nrm(ks[7], (DEPTH, D_MODEL, D_IN), f32) * D_MODEL ** -0.5
    b_igate = 0.1 * nrm(ks[8], (DEPTH, H_M), f32)
    b_fgate = jnp.linspace(3.0, 6.0, H_M, dtype=f32)[None, :] + 0.1 * nrm(ks[9], (DEPTH, H_M), f32)
    w_conv_mix = 0.5 * nrm(ks[10], (DEPTH, CONV_W, D_CONV), f32)
    mhln_g = 1.0 + 0.1 * nrm(ks[11], (DEPTH, D_M), f32)
    w_proj_a = nrm(ks[12], (DEPTH, D_CONV, D_MODEL), f32) * D_CONV ** -0.5
    w_proj_b = nrm(ks[13], (DEPTH, D_M, D_MODEL), f32) * D_M ** -0.5
    w_mix_out = nrm(ks[14], (DEPTH, D_MODEL, D_MODEL), f32) * (D_MODEL ** -0.5) * BETA
    ln1_g = 1.0 + 0.05 * nrm(ks[15], (DEPTH, D_MODEL), f32)
    ln1_b = 0.02 * nrm(ks[16], (DEPTH, D_MODEL), f32)
    w_ffn_up = nrm(ks[17], (DEPTH, D_MODEL, 2 * D_FF), f32) * D_MODEL ** -0.5
    w_ffn_conv = 0.5 * nrm(ks[18], (DEPTH, CONV_W, 2 * D_FF), f32)
    w_ffn_down = nrm(ks[19], (DEPTH, D_FF, D_MODEL), f32) * (D_FF ** -0.5) * BETA
    ln2_g = 1.0 + 0.05 * nrm(ks[20], (DEPTH, D_MODEL), f32)
    ln2_b = 0.02 * nrm(ks[21], (DEPTH, D_MODEL), f32)
    return {
        "x_prompt": x_prompt, "x_sample": x_sample,
        "cache_sconv": cache_sconv, "state_mlstm_C": state_mlstm_C,
        "state_mlstm_n": state_mlstm_n, "state_mlstm_m": state_mlstm_m,
        "cache_ffn_conv": cache_ffn_conv,
        "w_in": w_in, "b_igate": b_igate, "b_fgate": b_fgate, "w_conv_mix": w_conv_mix,
        "mhln_g": mhln_g, "w_proj_a": w_proj_a, "w_proj_b": w_proj_b, "w_mix_out": w_mix_out,
        "ln1_g": ln1_g, "ln1_b": ln1_b, "w_ffn_up": w_ffn_up, "w_ffn_conv": w_ffn_conv,
        "w_ffn_down": w_ffn_down, "ln2_g": ln2_g, "ln2_b": ln2_b,
    }


def reference(x_prompt, x_sample, cache_sconv, state_mlstm_C, state_mlstm_n, state_mlstm_m,
              cache_ffn_conv, w_in, b_igate, b_fgate, w_conv_mix, mhln_g, w_proj_a, w_proj_b,
              w_mix_out, ln1_g, ln1_b, w_ffn_up, w_ffn_conv, w_ffn_down, ln2_g, ln2_b):
    dt = x_prompt.dtype
    bsz = x_prompt.shape[0]
    zero_sconv = jnp.zeros((bsz, CONV_W - 1, D_CONV), dt)
    zero_c = jnp.zeros((bsz, H_M, DH_M, DH_M), dt)
    zero_n = jnp.zeros((bsz, H_M, DH_M), dt)
    zero_m = jnp.zeros((bsz, H_M), dt)
    zero_ffn = jnp.zeros((bsz, CONV_W - 1, 2 * D_FF), dt)

    xp, xs = x_prompt, x_sample
    sp_l, ss_l, cp_l, cs_l, np_l, ns_l, mp_l, ms_l, fp_l, fs_l = ([] for _ in range(10))
    for l in range(DEPTH):
        params = (w_in[l], b_igate[l], b_fgate[l], w_conv_mix[l], mhln_g[l], w_proj_a[l],
                  w_proj_b[l], w_mix_out[l], ln1_g[l], ln1_b[l], w_ffn_up[l], w_ffn_conv[l],
                  w_ffn_down[l], ln2_g[l], ln2_b[l])
        xp, sp, cp, np_, mp, fp = trunk_layer(xp, zero_sconv, zero_c, zero_n, zero_m, zero_ffn, *params)
        xs, ss, cs, ns, ms, fs = trunk_layer(xs, cache_sconv[l], state_mlstm_C[l], state_mlstm_n[l],
                                             state_mlstm_m[l], cache_ffn_conv[l], *params)
        sp_l.append(sp); ss_l.append(ss); cp_l.append(cp); cs_l.append(cs)
        np_l.append(np_); ns_l.append(ns); mp_l.append(mp); ms_l.append(ms)
        fp_l.append(fp); fs_l.append(fs)

    return (xp, xs,
            jnp.stack(sp_l), jnp.stack(ss_l),
            jnp.stack(cp_l), jnp.stack(cs_l),
            jnp.stack(np_l), jnp.stack(ns_l),
            jnp.stack(mp_l), jnp.stack(ms_l),
            jnp.stack(fp_l), jnp.stack(fs_l))
```

```python
import numpy as np
import concourse.bass as bass
import concourse.mybir as mybir
from concourse.bass_utils import run_bass_kernel_spmd

F32 = mybir.dt.float32
BF16 = mybir.dt.bfloat16
AF = mybir.ActivationFunctionType
ALU = mybir.AluOpType
AX = mybir.AxisListType

L = 4
D = 1024
SEQ = 2048
NS_ = 16
NT = SEQ + NS_
H = 4
DH = 256
DFF = 2816
DIN = 7688
EPS = 1e-5
ALPHA = (2.0 * L) ** 0.25
TILES = [(0, 512), (512, 512), (1024, 512), (1536, 512), (2048, 16)]
O_B, O_C, O_XV, O_Q, O_K, O_V, O_O, O_IG, O_GA, O_GB = 0, 512, 1024, 1536, 2560, 3584, 4608, 5632, 5640, 6664
NSLOT = 5
NEG = -1.0e30

PP_LN = 0
PP_CM = 128
PP_FC = 176
PP_GB = 704
PP_N = 736
C_ID, C_U, C_MTS, C_S127, C_MST, C_I16, C_N = 0, 128, 256, 384, 512, 640, 896

ENG = ("pe", "act", "dve", "pool", "sp")


class Op:
    __slots__ = ("eng", "fn", "r", "w", "dma", "chain", "waits", "inc", "idx")

    def __init__(self, eng, fn, r, w, dma, chain):
        self.eng, self.fn, self.r, self.w, self.dma, self.chain = eng, fn, r, w, dma, chain
        self.waits = []
        self.inc = None
        self.idx = -1


class Prog:
    def __init__(self, nc):
        self.nc = nc
        self.ops = []

    def op(self, eng, fn, r=(), w=(), dma=False, chain=None):
        o = Op(eng, fn, tuple(r), tuple(w), dma, chain)
        o.idx = len(self.ops)
        self.ops.append(o)
        return o

    def resolve(self):
        last_w, readers = {}, {}
        n = len(self.ops)
        deps = [None] * n
        has_dep = [False] * n
        for o in self.ops:
            d = set()
            for k in o.r:
                lw = last_w.get(k)
                if lw is not None:
                    d.add(lw)
            for k in o.w:
                lw = last_w.get(k)
                if lw is not None:
                    d.add(lw)
                rl = readers.get(k)
                if rl:
                    d.update(rl)
            d.discard(o.idx)
            if o.eng == "pe" and not o.dma:
                d = {x for x in d if not (self.ops[x].eng == "pe" and not self.ops[x].dma)}
            deps[o.idx] = d
            for x in d:
                has_dep[x] = True
            for k in o.w:
                last_w[k] = o.idx
                readers[k] = []
            for k in o.r:
                readers.setdefault(k, []).append(o.idx)
        eng_cnt = {e: 0 for e in ENG}
        chain_cnt = {}
        self.chains = []
        ms = [None] * n
        for o in self.ops:
            if o.dma:
                c = chain_cnt.get(o.chain, 0) + 16
                chain_cnt[o.chain] = c
                if c == 16:
                    self.chains.append(o.chain)
                ms[o.idx] = (("c", o.chain), c)
                o.inc = ("c", o.chain)
            elif has_dep[o.idx]:
                eng_cnt[o.eng] += 1
                ms[o.idx] = (("e", o.eng), eng_cnt[o.eng])
                o.inc = ("e", o.eng)
        self.eng_cnt, self.chain_cnt = eng_cnt, chain_cnt
        seen = {e: {} for e in ENG}
        for o in self.ops:
            need = {}
            for x in deps[o.idx]:
                s, v = ms[x]
                if v > need.get(s, 0):
                    need[s] = v
            sn = seen[o.eng]
            for s, v in need.items():
                if sn.get(s, 0) >= v:
                    continue
                sn[s] = v
                o.waits.append((s, v))
        return self

    def emit(self, block):
        nc = self.nc
        sems = {}
        for e in ENG:
            if self.eng_cnt[e] > 0:
                sems[("e", e)] = nc.alloc_semaphore("se_" + e)
        for i, c in enumerate(self.chains):
            sems[("c", c)] = nc.alloc_semaphore("sc_%d" % i)
        per = {e: [o for o in self.ops if o.eng == e] for e in ENG}

        def run(eh, ename):
            for o in per[ename]:
                for s, v in o.waits:
                    eh.wait_ge(sems[s], v)
                ins = o.fn(eh)
                if o.inc is not None:
                    ins.then_inc(sems[o.inc], 16 if o.dma else 1)
            if ename == "sp":
                for c in self.chains:
                    eh.wait_ge(sems[("c", c)], self.chain_cnt[c])

        @block.tensor
        def _(e):
            run(e, "pe")

        @block.scalar
        def _(e):
            run(e, "act")

        @block.vector
        def _(e):
            run(e, "dve")

        @block.gpsimd
        def _(e):
            run(e, "pool")

        @block.sync
        def _(e):
            run(e, "sp")


def build_nc():
    nc = bass.Bass("TRN2", target_bir_lowering=False)

    def din(name, shape):
        return nc.dram_tensor(name, list(shape), F32, kind="ExternalInput").ap()

    def dout(name, shape):
        return nc.dram_tensor(name, list(shape), F32, kind="ExternalOutput").ap()

    xT = din("xT", [D, NT])
    pp_d = din("pp", [128, PP_N])
    cst_d = din("cst", [128, C_N])
    mh_d = din("mhln_g", [L, D])
    w_in = din("w_in", [L, D, DIN])
    w_pa = din("w_proj_a", [L, 512, D])
    w_pb = din("w_proj_b", [L, D, D])
    w_mx = din("w_mix_out", [L, D, D])
    w_up = din("w_ffn_up", [L, D, 2 * DFF])
    w_dn = din("w_ffn_down", [L, DFF, D])
    cs_in = din("cs_in", [L, 2, 128, 4, NS_])
    cf_in = din("cf_in", [L, 2, 128, 44, NS_])
    Cs_in = din("Cs_in", [L, NS_, H, DH, DH])
    ns_in = din("ns_in", [L, NS_, H * DH])
    ms_in = din("ms_in", [L, NS_, H])

    yT = dout("yT", [D, NT])
    sc_p = dout("sc_p", [L, 128, 4, 2])
    sc_s = dout("sc_s", [L, 2, 128, 4, NS_])
    Cp_o = dout("Cp_o", [L, H, DH, DH])
    Cs_o = dout("Cs_o", [L, NS_, H, DH, DH])
    np_o = dout("np_o", [L, H, 128, 2])
    ns_o = dout("ns_o", [L, NS_, H * DH])
    mp_o = dout("mp_o", [L, H])
    ms_o = dout("ms_o", [L, NS_, H])
    ff_p = dout("ff_p", [L, 128, 44, 2])
    ff_s = dout("ff_s", [L, 2, 128, 44, NS_])

    P = Prog(nc)

    def sb(name, shape, dt=F32):
        return nc.alloc_sbuf_tensor("sb_" + name, list(shape), dt).ap()

    x32 = sb("x32", [128, 8, NT])
    xb = sb("xb", [128, 8, NT], BF16)
    slots = [sb("ws%d" % i, [128, 2048], BF16) for i in range(NSLOT)]
    S = nc.alloc_sbuf_tensor("S", [128, 78 * 1024], mybir.dt.uint8)
    pp = sb("pp", [128, PP_N])
    cst = sb("cst", [128, C_N])
    identb = sb("identb", [128, 128], BF16)
    mstb = sb("mstb", [128, 128], BF16)
    onesb = sb("onesb", [128, 128], BF16)
    ones32 = sb("ones32", [128, 128])
    gbc = sb("gbc", [128, DH])
    psum = [nc.alloc_psum_tensor("ps%d" % i, [128, 512], F32).ap() for i in range(8)]

    class Carver:
        def __init__(self):
            self.off = 0

        def take(self, shape, dt):
            esz = 2 if dt == BF16 else 4
            n = 1
            for s in shape[1:]:
                n *= s
            nbytes = (n * esz + 31) // 32 * 32
            assert self.off + nbytes <= 78 * 1024, (self.off, nbytes)
            v = S[:, self.off:self.off + nbytes].bitcast(dt)[:, 0:n]
            self.off += nbytes
            if len(shape) == 3:
                v = v.rearrange("p (a b) -> p a b", b=shape[2])
            elif len(shape) == 4:
                v = v.rearrange("p (a b c) -> p a b c", b=shape[2], c=shape[3])
            return v[0:shape[0]]

    state = {"ps": 0, "ws": 0, "ev": 0, "tok": None}
    _raw_op = P.op

    def _op(eng, fn, r=(), w=(), dma=False, chain=None):
        r = list(r)
        if state["tok"] is not None and eng != "pool":
            r.append(state["tok"])
        return _raw_op(eng, fn, r, w, dma, chain)

    P.op = _op

    def PS():
        i = state["ps"] % 7
        state["ps"] += 1
        return psum[i], ("ps", i)

    def PSL():
        return psum[7], ("ps", 7)

    def MM(out, lhsT, rhs, start, stop, r, w):
        P.op("pe", lambda e: e.matmul(out, lhsT=lhsT, rhs=rhs, start=start, stop=stop), r=r, w=w)

    def TR(out, in_, ident, r, w):
        P.op("pe", lambda e: e.transpose(out, in_, ident), r=r, w=w)

    def ACT(out, in_, func, r, w, bias=0.0, scale=1.0):
        P.op("act", lambda e: e.activation(out=out, in_=in_, func=func, bias=bias, scale=scale), r=r, w=w)

    def TT(out, in0, in1, op, r, w, eng="dve"):
        P.op(eng, lambda e: e.tensor_tensor(out=out, in0=in0, in1=in1, op=op), r=r, w=w)

    def TS(out, in0, s1, s2, op0, op1, r, w):
        if s2 is None:
            P.op("dve", lambda e: e.tensor_scalar(out=out, in0=in0, scalar1=s1, scalar2=None, op0=op0), r=r, w=w)
        else:
            P.op("dve", lambda e: e.tensor_scalar(out=out, in0=in0, scalar1=s1, scalar2=s2, op0=op0, op1=op1), r=r, w=w)

    def STT(out, in0, scalar, in1, op0, op1, r, w):
        P.op("dve", lambda e: e.scalar_tensor_tensor(out=out, in0=in0, scalar=scalar, in1=in1, op0=op0, op1=op1),
             r=r, w=w)

    def CP(out, in_, r, w, eng=None):
        if eng is None:
            eng = "act" if state["ev"] % 2 == 0 else "dve"
            state["ev"] += 1
        if eng == "act":
            P.op("act", lambda e: e.activation(out=out, in_=in_, func=AF.Copy), r=r, w=w)
        else:
            P.op("dve", lambda e: e.tensor_copy(out=out, in_=in_), r=r, w=w)

    def PTS(out, in0, s1, r, w):
        P.op("act", lambda e: e.activation(out=out, in_=in0, func=AF.Identity, scale=s1), r=r, w=w)

    def PCP(out, in_, r, w):
        P.op("act", lambda e: e.activation(out=out, in_=in_, func=AF.Copy), r=r, w=w)

    def MSET(ap, val, w, eng="dve"):
        P.op(eng, lambda e: e.memset(ap, val), w=w)

    def DMA(q, out, in_, r, w, chain):
        P.op(q, lambda e: e.dma_start(out=out, in_=in_), r=r, w=w, dma=True, chain=chain)

    def WLOAD(src2d, KC, ncols):
        i = state["ws"] % NSLOT
        state["ws"] += 1
        v = slots[i][:, 0:KC * ncols].rearrange("p (k n) -> p k n", n=ncols)
        key = ("ws", i)
        DMA("pool", v, src2d.rearrange("(k p) n -> p k n", p=128), r=[], w=[key], chain=key)
        return v, key

    def xbk(t):
        return [("xb", k, t) for k in range(8)]

    def barrier(newtok, scratch):
        old = state["tok"]
        wk = [newtok] + ([old] if old is not None else [])
        _raw_op("dve", lambda e: e.memset(scratch, 0.0), [], wk, False, None)
        state["tok"] = newtok

    DMA("sp", pp, pp_d, [], ["pp"], "pp")
    DMA("sp", cst, cst_d, [], ["cst"], "cst")
    CP(identb, cst[:, C_ID:C_ID + 128], ["cst"], ["identb"], eng="dve")
    CP(mstb, cst[:, C_MST:C_MST + 128], ["cst"], ["mstb"], eng="dve")
    MSET(onesb, 1.0, ["onesb"])
    MSET(ones32, 1.0, ["ones32"])
    ident32 = cst[:, C_ID:C_ID + 128]
    Utri = cst[:, C_U:C_U + 128]
    mts = cst[:, C_MTS:C_MTS + 128]
    sel127 = cst[:, C_S127:C_S127 + 128]
    I16bc = cst[:, C_I16:C_I16 + 256].rearrange("p (a b) -> p a b", b=16)
    xTv = xT.rearrange("(c p) n -> p c n", p=128)
    for c in range(8):
        DMA("sp", x32[:, c, :], xTv[:, c, :], [], [("x32", c, t) for t in range(5)], ("x32i", c))
        DMA("pool", xb[:, c, :], xTv[:, c, :], [], [("xb", c, t) for t in range(5)], ("xbi", c))
    bscr = sb("bscr", [128, 8])

    def ppc(base, l, i):
        return pp[:, base + l * 8 + i: base + l * 8 + i + 1]

    for l in range(L):
        cv = Carver()
        hmT = cv.take([128, 8, NT], BF16)
        qT = cv.take([128, 2, 512], BF16)
        kT = cv.take([128, 2, 512], BF16)
        ktok = cv.take([128, 4, DH], BF16)
        vtok = cv.take([128, 4, DH + 2], BF16)
        gsig = cv.take([128, 4, DH], F32)
        Et = cv.take([128, 128], F32)
        PTt = cv.take([128, 128], BF16)
        diag = cv.take([128, 128], F32)
        tmp128 = Et
        numB = cv.take([128, DH + 1], F32)
        num = cv.take([128, DH + 1], F32)
        hn = cv.take([128, DH], F32)
        hg = cv.take([128, DH], BF16)
        Kw = cv.take([128, DH], BF16)
        C32 = cv.take([128, 2, DH], F32)
        n32 = cv.take([128, 2], F32)
        Cb = cv.take([128, 2, DH + 2], BF16)
        st6 = cv.take([128, 6], F32)
        mv = cv.take([128, 2], F32)
        sc4 = cv.take([128, 4], F32)
        G_ = {}
        for nm in ("gp", "li", "sp", "bloc", "tot", "ginc", "G", "a", "cm", "cmx", "M", "nM", "fl", "sI", "wS",
                   "dec", "t1"):
            G_[nm] = cv.take([128, 16, 4], F32)
        MT = cv.take([128, 17, 4], F32)
        q_s = cv.take([NS_, DH], F32)
        k_s = cv.take([NS_, DH], F32)
        v_s = cv.take([NS_, DH], F32)
        g_s = cv.take([NS_, DH], F32)
        t_s_full = cv.take([128, DH], F32)[:, 0:128]
        kw_s_full = cv.take([128, DH], F32)[:, 0:128]
        vm_s_full = cv.take([128, DH], F32)[:, 0:128]
        cv.off -= 3 * 1024
        t_s = cv.take([NS_, DH], F32)
        kw_s = cv.take([NS_, DH], F32)
        vm_s = cv.take([NS_, DH], F32)
        nold_s = cv.take([NS_, DH], F32)
        nnew_s = cv.take([NS_, DH], F32)
        num_s = cv.take([NS_, DH], F32)
        hg_s = cv.take([NS_, DH], BF16)
        sg = {}
        for nm in ("gps", "li", "lf", "m", "mn", "dec", "wg", "fl", "qk", "qn", "s", "den", "rdn", "t"):
            sg[nm] = cv.take([NS_, 8], F32)
        Dm = cv.take([NS_, 4, NS_], F32)
        decs_bc = cv.take([128, 4, NS_], F32)
        qTs = cv.take([128, 2, NS_], F32)
        Qm = cv.take([128, 2, NS_, NS_], F32)
        Cst = [cv.take([128, 2, DH], F32) for _ in range(3)]
        st6s = cv.take([NS_, 6], F32)
        mvs = cv.take([NS_, 2], F32)
        RA = ("RA", l)
        barrier(RA, bscr)

        gw, gwk = WLOAD(w_in[l][:, O_IG:O_IG + 8], 8, 8)
        gps, gpk = PS()
        gpsv = gps[:, 0:128].rearrange("p (c g) -> p c g", g=8)
        for c in range(16):
            for k in range(8):
                MM(gpsv[:, c, :], xb[:, k, c * 128:(c + 1) * 128], gw[:, k, :], k == 0, k == 7,
                   [gwk, ("xb", k, c // 4), RA], [gpk])
        bi_bc = pp[:, PP_GB + l * 8:PP_GB + l * 8 + 4]
        bf_bc = pp[:, PP_GB + l * 8 + 4:PP_GB + l * 8 + 8]
        g = G_
        K = lambda nm: ("g", nm)
        TT(g["li"], gpsv[:, :, 0:4], bi_bc.unsqueeze(1).broadcast_to([128, 16, 4]), ALU.add, [gpk, "pp", RA], [K("li")])
        TT(g["t1"], gpsv[:, :, 4:8], bf_bc.unsqueeze(1).broadcast_to([128, 16, 4]), ALU.add, [gpk, "pp", RA], [K("t1")])
        ACT(g["sp"], g["t1"], AF.Exp, [K("t1")], [K("sp")], scale=-1.0)
        ACT(g["sp"], g["sp"], AF.Ln, [K("sp")], [K("sp")], bias=1.0)
        f64 = lambda t: t.rearrange("p c h -> p (c h)")
        ps1, pk1 = PS()
        MM(ps1[:, 0:64], Utri, f64(g["sp"]), True, True, [K("sp"), "cst"], [pk1])
        CP(f64(g["bloc"]), ps1[:, 0:64], [pk1], [K("bloc")], eng="dve")
        ps2, pk2 = PS()
        MM(ps2[:, 0:64], ones32, f64(g["sp"]), True, True, [K("sp"), "ones32"], [pk2])
        CP(f64(g["tot"]), ps2[:, 0:64], [pk2], [K("tot")], eng="dve")
        for h in range(H):
            P.op("dve", lambda e, h=h: e.tensor_tensor_scan(out=g["ginc"][:, :, h], data0=g["tot"][:, :, h],
                                                            data1=g["tot"][:, :, h], initial=0.0,
                                                            op0=ALU.add, op1=ALU.bypass),
                 r=[K("tot")], w=[K("ginc")])
        TT(g["G"], g["ginc"], g["tot"], ALU.subtract, [K("ginc"), K("tot")], [K("G")])
        TT(g["G"], g["G"], g["bloc"], ALU.add, [K("G"), K("bloc")], [K("G")])
        TT(g["a"], g["li"], g["G"], ALU.add, [K("li"), K("G")], [K("a")])
        dgs = [(diag, "diag"), (t_s_full, "t_s"), (kw_s_full, "kw_s"), (vm_s_full, "vm_s")]
        items = [(c, hh_) for c in range(16) for hh_ in range(H)]

        def cm_front(i):
            c, hh_ = items[i]
            dg, dk = dgs[i % 4]
            TS(dg, ident32, g["a"][:, c, hh_:hh_ + 1], None, ALU.mult, None, [K("a"), "cst"], [dk])

        LOOK = 3
        for i in range(min(LOOK, len(items))):
            cm_front(i)
        for i, (c, hh_) in enumerate(items):
            dg, dk = dgs[i % 4]
            psd, pkd = PS()
            MM(psd[:, 0:128], ones32, dg, True, True, [dk, "ones32"], [pkd])
            if i + LOOK < len(items):
                cm_front(i + LOOK)
            TT(tmp128, psd[:, 0:128], mts, ALU.add, [pkd, "cst"], ["Et"])
            P.op("dve", lambda e, c=c, hh_=hh_: e.tensor_reduce(out=g["cm"][:, c, hh_:hh_ + 1], in_=tmp128, axis=AX.X,
                                                                op=ALU.max), r=["Et"], w=[K("cm")])
        ps3, pk3 = PS()
        MM(ps3[:, 0:64], sel127, f64(g["cm"]), True, True, [K("cm"), "cst"], [pk3])
        CP(f64(g["cmx"]), ps3[:, 0:64], [pk3], [K("cmx")], eng="dve")
        MSET(MT[:, 0, :], 0.0, [K("MT")])
        for h in range(H):
            P.op("dve", lambda e, h=h: e.tensor_tensor_scan(out=MT[:, 1:17, h], data0=g["cmx"][:, :, h],
                                                            data1=g["cmx"][:, :, h], initial=0.0,
                                                            op0=ALU.max, op1=ALU.bypass),
                 r=[K("cmx"), K("MT")], w=[K("MT")])
        Mprev = MT[:, 0:16, :]
        Mend = MT[:, 1:17, :]
        TT(g["M"], g["cm"], Mprev, ALU.max, [K("cm"), K("MT")], [K("M")])
        TS(g["nM"], g["M"], -1.0, None, ALU.mult, None, [K("M")], [K("nM")])
        TT(g["t1"], g["G"], g["M"], ALU.subtract, [K("G"), K("M"), K("sp")], [K("t1")])
        ACT(g["fl"], g["t1"], AF.Exp, [K("t1")], [K("fl")])
        TT(g["sI"], Mprev, g["M"], ALU.subtract, [K("MT"), K("M")], [K("sI")])
        ACT(g["sI"], g["sI"], AF.Exp, [K("sI")], [K("sI")])
        TT(g["wS"], g["a"], Mend, ALU.subtract, [K("a"), K("MT")], [K("wS")])
        ACT(g["wS"], g["wS"], AF.Exp, [K("wS")], [K("wS")])
        TT(g["dec"], Mprev, Mend, ALU.subtract, [K("MT")], [K("dec")])
        ACT(g["dec"], g["dec"], AF.Exp, [K("dec")], [K("dec")])
        TT(sc4, MT[:, 16, :], g["ginc"][:, 15, :], ALU.subtract, [K("MT"), K("ginc")], ["sc4"])
        DMA("sp", mp_o[l:l + 1, :], sc4[0:1, :], ["sc4"], [], "mp_o")

        s = sg
        SK = lambda nm: ("sg", nm)
        psg, pkg = PS()
        for k in range(8):
            MM(psg[0:NS_, 0:8], xb[:, k, SEQ:NT], gw[:, k, :], k == 0, k == 7, [gwk, ("xb", k, 4), RA], [pkg])
        DMA("sp", s["m"][:, 0:4], ms_in[l], [RA], [SK("m")], "ms_in")
        TT(s["li"][:, 0:4], psg[0:NS_, 0:4], bi_bc[0:NS_], ALU.add, [pkg, "pp"], [SK("li")])
        TT(s["t"][:, 0:4], psg[0:NS_, 4:8], bf_bc[0:NS_], ALU.add, [pkg, "pp"], [SK("t")])
        ACT(s["lf"][:, 0:4], s["t"][:, 0:4], AF.Exp, [SK("t")], [SK("lf")], scale=-1.0)
        ACT(s["lf"][:, 0:4], s["lf"][:, 0:4], AF.Ln, [SK("lf")], [SK("lf")], bias=1.0)
        TT(s["t"][:, 0:4], s["m"][:, 0:4], s["lf"][:, 0:4], ALU.subtract, [SK("m"), SK("lf"), SK("t")], [SK("t")])
        TT(s["mn"][:, 0:4], s["t"][:, 0:4], s["li"][:, 0:4], ALU.max, [SK("t"), SK("li")], [SK("mn")])
        TT(s["dec"][:, 0:4], s["t"][:, 0:4], s["mn"][:, 0:4], ALU.subtract, [SK("t"), SK("mn")], [SK("dec")])
        ACT(s["dec"][:, 0:4], s["dec"][:, 0:4], AF.Exp, [SK("dec")], [SK("dec")])
        TT(s["wg"][:, 0:4], s["li"][:, 0:4], s["mn"][:, 0:4], ALU.subtract, [SK("li"), SK("mn")], [SK("wg")])
        ACT(s["wg"][:, 0:4], s["wg"][:, 0:4], AF.Exp, [SK("wg")], [SK("wg")])
        ACT(s["fl"][:, 0:4], s["mn"][:, 0:4], AF.Exp, [SK("mn")], [SK("fl")], scale=-1.0)
        DMA("sp", ms_o[l], s["mn"][:, 0:4], [SK("mn")], [], "ms_o")
        TT(Dm, s["dec"][:, 0:4].unsqueeze(2).broadcast_to([NS_, 4, NS_]),
           ident32[0:NS_, 0:NS_].unsqueeze(1).broadcast_to([NS_, 4, NS_]), ALU.mult, [SK("dec"), "cst"], ["Dm"])
        psb, pkb = PS()
        MM(psb[:, 0:64], ones32[0:NS_, :], Dm.rearrange("p h b -> p (h b)"), True, True, ["Dm", "ones32"], [pkb])
        CP(decs_bc.rearrange("p h b -> p (h b)"), psb[:, 0:64], [pkb], ["decs_bc"], eng="dve")

        for h in range(H):
            wq, wqk = WLOAD(w_in[l][:, O_Q + h * DH:O_Q + (h + 1) * DH], 8, DH)
            wk, wkk = WLOAD(w_in[l][:, O_K + h * DH:O_K + (h + 1) * DH], 8, DH)
            wv, wvk = WLOAD(w_in[l][:, O_V + h * DH:O_V + (h + 1) * DH], 8, DH)
            wo, wok = WLOAD(w_in[l][:, O_O + h * DH:O_O + (h + 1) * DH], 8, DH)
            DMA("sp", gbc, mh_d[l:l + 1, h * DH:(h + 1) * DH].broadcast_to([128, DH]), [RA], ["gbc"], "gbc")
            MSET(C32, 0.0, ["C32"])
            MSET(n32, 0.0, ["n32"])
            MSET(Cb, 0.0, ["Cb"], eng="dve")
            MSET(vtok[:, :, DH:DH + 2], 1.0, ["vtok1"])
            for tt in range(4):
                t0 = tt * 512
                for dc in range(2):
                    pq, pqk = PS()
                    for k in range(8):
                        MM(pq, wq[:, k, dc * 128:(dc + 1) * 128], xb[:, k, t0:t0 + 512], k == 0, k == 7,
                           [wqk, ("xb", k, tt), RA], [pqk])
                    CP(qT[:, dc, :], pq, [pqk], [("qT", dc)])
                    pk_, pkk = PS()
                    for k in range(8):
                        MM(pk_, wk[:, k, dc * 128:(dc + 1) * 128], xb[:, k, t0:t0 + 512], k == 0, k == 7,
                           [wkk, ("xb", k, tt), RA], [pkk])
                    ACT(kT[:, dc, :], pk_, AF.Identity, [pkk], [("kT", dc)], scale=1.0 / 16.0)
                for ci in range(4):
                    c0 = t0 + ci * 128
                    pa, pak = PS()
                    for k in range(8):
                        MM(pa[:, 0:DH], xb[:, k, c0:c0 + 128], wk[:, k, :], k == 0, k == 7,
                           [wkk, ("xb", k, tt), RA], [pak])
                    ACT(ktok[:, ci, :], pa[:, 0:DH], AF.Identity, [pak], [("ktok", ci)], scale=1.0 / 16.0)
                    pb_, pbk = PS()
                    for k in range(8):
                        MM(pb_[:, 0:DH], xb[:, k, c0:c0 + 128], wv[:, k, :], k == 0, k == 7,
                           [wvk, ("xb", k, tt), RA], [pbk])
                    CP(vtok[:, ci, 0:DH], pb_[:, 0:DH], [pbk], [("vtok", ci)], eng="dve")
                    pc_, pck = PS()
                    for k in range(8):
                        MM(pc_[:, 0:DH], xb[:, k, c0:c0 + 128], wo[:, k, :], k == 0, k == 7,
                           [wok, ("xb", k, tt), RA], [pck])
                    ACT(gsig[:, ci, :], pc_[:, 0:DH], AF.Sigmoid, [pck], [("gsig", ci)])
                    TT(gsig[:, ci, :], gsig[:, ci, :], gbc, ALU.mult, [("gsig", ci), "gbc"], [("gsig", ci)])
                XB, XK = psum[4], ("ps", 4)
                UB, UK = psum[5], ("ps", 5)
                TB, TK = psum[6], ("ps", 6)
                SB_, SK_ = psum[7], ("ps", 7)

                def front_a(ci):
                    c = tt * 4 + ci
                    cs = slice(ci * 128, (ci + 1) * 128)
                    pB, pBk = psum[2 + c % 2], ("ps", 2 + c % 2)
                    PTS(diag, ident32, g["nM"][:, c, h:h + 1], [K("nM"), "cst"], ["diag"])
                    PTS(Kw, ktok[:, ci, :], g["wS"][:, c, h:h + 1], [("ktok", ci), K("wS")], ["Kw"])
                    for dc in range(2):
                        MM(SB_[:, 0:128], kT[:, dc, cs], qT[:, dc, cs], dc == 0, dc == 1, [("kT", dc), ("qT", dc)], [SK_])
                    MM(XB[:, 128:256], ones32, diag, True, False, ["diag", "ones32"], [XK])
                    MM(XB[:, 128:256], identb, mstb, False, True, ["identb", "mstb"], [XK])
                    for kc in range(2):
                        MM(XB[:, 256 + kc:257 + kc], Kw[:, kc * 128:(kc + 1) * 128], vtok[:, ci, DH:DH + 1], True, True,
                           ["Kw", "vtok1"], [XK])
                    for kc in range(2):
                        MM(UB[:, kc * DH:(kc + 1) * DH], Kw[:, kc * 128:(kc + 1) * 128], vtok[:, ci, 0:DH], True, True,
                           ["Kw", ("vtok", ci)], [UK])
                    for kc in range(2):
                        MM(pB[:, 0:DH + 1], qT[:, kc, cs], Cb[:, kc, 0:DH + 1], kc == 0, kc == 1, [("qT", kc), "Cb"], [pBk])
                    STT(C32.rearrange("p a b -> p (a b)"), C32.rearrange("p a b -> p (a b)"), g["dec"][:, c, h:h + 1],
                        UB, ALU.mult, ALU.add, ["C32", UK, K("dec")], ["C32"])
                    STT(n32, n32, g["dec"][:, c, h:h + 1], XB[:, 256:258], ALU.mult, ALU.add, ["n32", XK, K("dec")], ["n32"])
                    PCP(Cb[:, :, 0:DH], C32, ["C32"], ["Cb"])
                    PCP(Cb[:, :, DH:DH + 1], n32.unsqueeze(2), ["n32", "Cb"], ["Cb"])
                    ACT(Et, XB[:, 128:256], AF.Exp, [XK, K("a")], ["Et"], bias=g["a"][:, c, h:h + 1])

                def front_b(ci):
                    c = tt * 4 + ci
                    pA, pAk = psum[c % 2], ("ps", c % 2)
                    TT(PTt, SB_[:, 0:128], Et, ALU.mult, [SK_, "Et"], ["PTt"])
                    MM(pA[:, 0:DH + 1], PTt, vtok[:, ci, 0:DH + 1], True, True, ["PTt", ("vtok", ci), "vtok1"], [pAk])

                def tail_a(ci):
                    c = tt * 4 + ci
                    pA, pAk = psum[c % 2], ("ps", c % 2)
                    pB, pBk = psum[2 + c % 2], ("ps", 2 + c % 2)
                    ACT(numB, pB[:, 0:DH + 1], AF.Identity, [pBk, K("sI")], ["numB"], scale=g["sI"][:, c, h:h + 1])
                    TT(num, pA[:, 0:DH + 1], numB, ALU.add, [pAk, "numB"], ["num"])
                    STT(mv[:, 0:1], num[:, DH:DH + 1], -1.0, num[:, DH:DH + 1], ALU.mult, ALU.max, ["num"], ["rdn"])
                    TS(mv[:, 0:1], mv[:, 0:1], g["fl"][:, c, h:h + 1], None, ALU.max, None, ["rdn", K("fl")], ["rdn"])
                    P.op("dve", lambda e: e.bn_stats(out=st6, in_=num[:, 0:DH]), r=["num"], w=["st6"])
                    P.op("dve", lambda e: e.bn_aggr(out=sc4[:, 0:2], in_=st6), r=["st6"], w=["sc4"])
                    TT(sc4[:, 2:3], mv[:, 0:1], mv[:, 0:1], ALU.mult, ["rdn", "sc4"], ["sc4b"])
                    STT(sc4[:, 2:3], sc4[:, 2:3], EPS, sc4[:, 1:2], ALU.mult, ALU.add, ["sc4b", "sc4"], ["sc4b"])
                    ACT(sc4[:, 2:3], sc4[:, 2:3], AF.Ln, ["sc4b"], ["sc4b"])
                    ACT(sc4[:, 2:3], sc4[:, 2:3], AF.Exp, ["sc4b"], ["sc4b"], scale=-0.5)

                def tail_b(ci):
                    c = tt * 4 + ci
                    TS(hn, num[:, 0:DH], sc4[:, 0:1], sc4[:, 2:3], ALU.subtract, ALU.mult, ["num", "sc4", "sc4b"], ["hn"])
                    TT(hg, hn, gsig[:, ci, :], ALU.mult, ["hn", ("gsig", ci)], ["hg"])
                    ptb = TB.bitcast(BF16)
                    for dc in range(2):
                        TR(ptb[:, dc * 128:(dc + 1) * 128], hg[:, dc * 128:(dc + 1) * 128], identb, ["hg", "identb"], [TK])
                    CP(hmT[:, 2 * h:2 * h + 2, c * 128:(c + 1) * 128],
                       ptb[:, 0:256].rearrange("p (a b) -> p a b", b=128), [TK, RA],
                       [("hmT", 2 * h, tt), ("hmT", 2 * h + 1, tt)])

                front_a(0)
                front_b(0)
                for ci in range(1, 4):
                    front_a(ci)
                    tail_a(ci - 1)
                    front_b(ci)
                    tail_b(ci - 1)
                tail_a(3)
                tail_b(3)
            DMA("sp", Cp_o[l, h].rearrange("(kc p) v -> p kc v", p=128), C32, ["C32"], [], "Cp_o")
            DMA("sp", np_o[l, h], n32, ["n32"], [], "np_o")

            for (dst, wv_, wk_, sc_) in ((q_s, wq, wqk, 1.0), (k_s, wk, wkk, 1.0 / 16.0), (v_s, wv, wvk, 1.0)):
                pp_, ppk = PS()
                for k in range(8):
                    MM(pp_[0:NS_, 0:DH], xb[:, k, SEQ:NT], wv_[:, k, :], k == 0, k == 7, [wk_, ("xb", k, 4), RA], [ppk])
                ACT(dst, pp_[0:NS_, 0:DH], AF.Identity, [ppk], [("s", id(dst))], scale=sc_)
            pp_, ppk = PS()
            for k in range(8):
                MM(pp_[0:NS_, 0:DH], xb[:, k, SEQ:NT], wo[:, k, :], k == 0, k == 7, [wok, ("xb", k, 4), RA], [ppk])
            ACT(g_s, pp_[0:NS_, 0:DH], AF.Sigmoid, [ppk], ["g_s"])
            TT(g_s, g_s, gbc[0:NS_], ALU.mult, ["g_s", "gbc"], ["g_s"])
            qk_, kk_, vk_ = ("s", id(q_s)), ("s", id(k_s)), ("s", id(v_s))
            for dc in range(2):
                pq, pqk = PS()
                for k in range(8):
                    MM(pq[:, 0:NS_], wq[:, k, dc * 128:(dc + 1) * 128], xb[:, k, SEQ:NT], k == 0, k == 7,
                       [wqk, ("xb", k, 4), RA], [pqk])
                CP(qTs[:, dc, :], pq[:, 0:NS_], [pqk], [("qTs", dc)], eng="dve")
                TT(Qm[:, dc], qTs[:, dc, :].unsqueeze(2).broadcast_to([128, NS_, NS_]), I16bc, ALU.mult,
                   [("qTs", dc), "cst"], [("Qm", dc)])
            DMA("sp", nold_s, ns_in[l][:, h * DH:(h + 1) * DH], [RA], ["nold_s"], "nold_s")
            TT(t_s, q_s, k_s, ALU.mult, [qk_, kk_], ["t_s"])
            P.op("dve", lambda e, h=h: e.tensor_reduce(out=s["qk"][:, h:h + 1], in_=t_s, axis=AX.X, op=ALU.add),
                 r=["t_s"], w=[SK("qk")])
            TT(t_s, q_s, nold_s, ALU.mult, [qk_, "nold_s", SK("qk")], ["t_s"])
            P.op("dve", lambda e, h=h: e.tensor_reduce(out=s["qn"][:, h:h + 1], in_=t_s, axis=AX.X, op=ALU.add),
                 r=["t_s"], w=[SK("qn")])
            hh = slice(h, h + 1)
            TT(s["s"][:, hh], s["qk"][:, hh], s["wg"][:, hh], ALU.mult, [SK("qk"), SK("wg")], [SK("s")])
            TT(s["den"][:, hh], s["dec"][:, hh], s["qn"][:, hh], ALU.mult, [SK("dec"), SK("qn")], [SK("den")])
            TT(s["den"][:, hh], s["den"][:, hh], s["s"][:, hh], ALU.add, [SK("den"), SK("s")], [SK("den")])
            STT(s["rdn"][:, hh], s["den"][:, hh], -1.0, s["den"][:, hh], ALU.mult, ALU.max, [SK("den")], [SK("rdn")])
            TS(s["rdn"][:, hh], s["rdn"][:, hh], s["fl"][:, hh], None, ALU.max, None, [SK("rdn"), SK("fl")], [SK("rdn")])
            P.op("dve", lambda e, hh=hh: e.reciprocal(out=s["rdn"][:, hh], in_=s["rdn"][:, hh]), r=[SK("rdn")], w=[SK("rdn")])
            TS(kw_s, k_s, s["wg"][:, hh], None, ALU.mult, None, [kk_, SK("wg")], ["kw_s"])
            STT(nnew_s, nold_s, s["dec"][:, hh], kw_s, ALU.mult, ALU.add, ["nold_s", "kw_s", SK("dec"), "t_s"], ["nnew_s"])
            DMA("sp", ns_o[l][:, h * DH:(h + 1) * DH], nnew_s, ["nnew_s"], [], "ns_o")
            pqc, pqck = PSL()
            for b in range(NS_):
                ct = Cst[b % 3]
                ck = ("Cst", b % 3)
                if b == 0:
                    for b2 in range(2):
                        DMA("sp", Cst[b2], Cs_in[l, b2, h].rearrange("(kc p) v -> p kc v", p=128), [RA],
                            [("Cst", b2)], ("Cst_i", b2))
                if b + 2 < NS_:
                    b2 = b + 2
                    DMA("sp", Cst[b2 % 3], Cs_in[l, b2, h].rearrange("(kc p) v -> p kc v", p=128), [RA],
                        [("Cst", b2 % 3)], ("Cst_i", b2 % 3))
                for kc in range(2):
                    MM(pqc[0:NS_, 0:DH], Qm[:, kc, b, :], ct[:, kc, :], b == 0 and kc == 0, b == NS_ - 1 and kc == 1,
                       [("Qm", kc), ck], [pqck])
                TS(vm_s, v_s, ident32[0:NS_, b:b + 1], None, ALU.mult, None, [vk_, "cst"], ["vm_s"])
                pU, pUk = PS()
                for kc in range(2):
                    MM(pU[:, kc * DH:(kc + 1) * DH], kw_s[:, kc * 128:(kc + 1) * 128], vm_s, True, True,
                       ["kw_s", "vm_s"], [pUk])
                STT(ct.rearrange("p a b -> p (a b)"), ct.rearrange("p a b -> p (a b)"), decs_bc[:, h, b:b + 1], pU,
                    ALU.mult, ALU.add, [ck, pUk, "decs_bc"], [ck])
                DMA("act", Cs_o[l, b, h].rearrange("(kc p) v -> p kc v", p=128), ct, [ck], [], ("Cst_o", b % 3))
            TS(num_s, v_s, s["s"][:, hh], None, ALU.mult, None, [vk_, SK("s")], ["num_s"])
            STT(num_s, pqc[0:NS_, 0:DH], s["dec"][:, hh], num_s, ALU.mult, ALU.add, [pqck, "num_s", SK("dec")], ["num_s"])
            TS(num_s, num_s, s["rdn"][:, hh], None, ALU.mult, None, ["num_s", SK("rdn")], ["num_s"])
            P.op("dve", lambda e: e.bn_stats(out=st6s, in_=num_s), r=["num_s"], w=["st6s"])
            P.op("dve", lambda e: e.bn_aggr(out=mvs, in_=st6s), r=["st6s"], w=["mvs"])
            ACT(s["t"][:, 4:5], mvs[:, 1:2], AF.Ln, ["mvs"], [SK("t2")], bias=EPS)
            ACT(s["t"][:, 4:5], s["t"][:, 4:5], AF.Exp, [SK("t2")], [SK("t2")], scale=-0.5)
            TS(num_s, num_s, mvs[:, 0:1], s["t"][:, 4:5], ALU.subtract, ALU.mult, ["num_s", "mvs", SK("t2")], ["num_s"])
            TT(hg_s, num_s, g_s, ALU.mult, ["num_s", "g_s"], ["hg_s"])
            ptp, ptpk = PS()
            ptb = ptp.bitcast(BF16)
            for dc in range(2):
                TR(ptb[:, dc * NS_:(dc + 1) * NS_], hg_s[:, dc * 128:(dc + 1) * 128], identb[0:NS_, 0:NS_],
                   ["hg_s", "identb"], [ptpk])
            CP(hmT[:, 2 * h:2 * h + 2, SEQ:NT], ptb[:, 0:2 * NS_].rearrange("p (a b) -> p a b", b=NS_), [ptpk, RA],
               [("hmT", 2 * h, 4), ("hmT", 2 * h + 1, 4)])

        cv = Carver()
        hmT = cv.take([128, 8, NT], BF16)
        u_ = cv.take([128, 4, NT], BF16)
        cur = cv.take([128, 2 + NT], F32)
        cvt = cv.take([128, NT], F32)
        t512 = [cv.take([128, 512], F32) for _ in range(1)]
        csT = cv.take([128, 2, 4, NS_], F32)
        scp = cv.take([128, 4, 2], F32)
        scs = cv.take([128, 4, NS_], F32)
        RB = ("RB", l)
        barrier(RB, bscr)
        DMA("sp", csT[:, 0], cs_in[l, 0], [RB], [("csT", 0)], ("csT", 0))
        DMA("sp", csT[:, 1], cs_in[l, 1], [RB], [("csT", 1)], ("csT", 1))
        DMA("sp", sc_s[l, 0], cs_in[l, 1], [], [], "sc_s0")
        MSET(cur[:, 0:2], 0.0, ["cur0"])
        for jp in range(2):
            wB, wBk = WLOAD(w_in[l][:, O_B + jp * 256:O_B + (jp + 1) * 256], 8, 256)
            wC, wCk = WLOAD(w_in[l][:, O_C + jp * 256:O_C + (jp + 1) * 256], 8, 256)
            wX, wXk = WLOAD(w_in[l][:, O_XV + jp * 256:O_XV + (jp + 1) * 256], 8, 256)
            for jj in range(2):
                j = jp * 2 + jj
                cl = slice(jj * 128, (jj + 1) * 128)
                for t, (t0, tn) in enumerate(TILES):
                    pc_, pck = PS()
                    for k in range(8):
                        MM(pc_[:, 0:tn], wC[:, k, cl], xb[:, k, t0:t0 + tn], k == 0, k == 7, [wCk, ("xb", k, t), RB], [pck])
                    px_, pxk = PS()
                    for k in range(8):
                        MM(px_[:, 0:tn], wX[:, k, cl], xb[:, k, t0:t0 + tn], k == 0, k == 7, [wXk, ("xb", k, t), RB], [pxk])
                    ACT(t512[0][:, 0:tn], pc_[:, 0:tn], AF.Copy, [pck], ["t512_0"])
                    TT(cur[:, 2 + t0:2 + t0 + tn], t512[0][:, 0:tn], px_[:, 0:tn], ALU.mult, ["t512_0", pxk, "cur0", RB],
                       [("cur", t)])
                cm_ = lambda jtap: pp[:, PP_CM + (l * 3 + jtap) * 4 + j:PP_CM + (l * 3 + jtap) * 4 + j + 1]
                curk = [("cur", t) for t in range(5)]
                ACT(cvt[:, 0:SEQ], cur[:, 0:SEQ], AF.Identity, curk + ["cur0", "pp"], ["cvt"], scale=cm_(0))
                STT(cvt[:, 0:SEQ], cur[:, 1:SEQ + 1], cm_(1), cvt[:, 0:SEQ], ALU.mult, ALU.add, curk + ["cvt"], ["cvt"])
                STT(cvt[:, 0:SEQ], cur[:, 2:SEQ + 2], cm_(2), cvt[:, 0:SEQ], ALU.mult, ALU.add, curk + ["cvt"], ["cvt"])
                TS(cvt[:, SEQ:NT], csT[:, 0, j, :], cm_(0), None, ALU.mult, None, [("csT", 0), "cvt"], ["cvts"])
                STT(cvt[:, SEQ:NT], csT[:, 1, j, :], cm_(1), cvt[:, SEQ:NT], ALU.mult, ALU.add, [("csT", 1), "cvts"], ["cvts"])
                STT(cvt[:, SEQ:NT], cur[:, 2 + SEQ:2 + NT], cm_(2), cvt[:, SEQ:NT], ALU.mult, ALU.add, curk + ["cvts"], ["cvts"])
                CP(scp[:, j, :], cur[:, SEQ:SEQ + 2], curk, [("scp", j)], eng="act")
                CP(scs[:, j, :], cur[:, 2 + SEQ:2 + NT], curk, [("scs", j)], eng="act")
                for t, (t0, tn) in enumerate(TILES):
                    pb_, pbk = PS()
                    for k in range(8):
                        MM(pb_[:, 0:tn], wB[:, k, cl], xb[:, k, t0:t0 + tn], k == 0, k == 7, [wBk, ("xb", k, t), RB], [pbk])
                    TT(u_[:, j, t0:t0 + tn], pb_[:, 0:tn], cvt[:, t0:t0 + tn], ALU.mult, [pbk, "cvt", "cvts", RB], [("u", j, t)])
        DMA("sp", sc_p[l], scp, [("scp", j) for j in range(4)], [], "sc_p")
        DMA("sp", sc_s[l, 1], scs, [("scs", j) for j in range(4)], [], "sc_s1")
        cv = Carver()
        hmT = cv.take([128, 8, NT], BF16)
        u_ = cv.take([128, 4, NT], BF16)
        mg = cv.take([128, 2, NT], BF16)
        t512 = [cv.take([128, 512], F32) for _ in range(3)]
        RB = ("RB2", l)
        barrier(RB, bscr)
        for gq in range(4):
            wa, wak = WLOAD(w_pa[l][:, gq * 256:(gq + 1) * 256], 4, 256)
            wb_, wbk = WLOAD(w_pb[l][:, gq * 256:(gq + 1) * 256], 8, 256)
            wga, wgak = WLOAD(w_in[l][:, O_GA + gq * 256:O_GA + (gq + 1) * 256], 8, 256)
            wgb, wgbk = WLOAD(w_in[l][:, O_GB + gq * 256:O_GB + (gq + 1) * 256], 8, 256)
            for oc in range(2):
                cl = slice(oc * 128, (oc + 1) * 128)
                for t, (t0, tn) in enumerate(TILES):
                    p1, p1k = PS()
                    for k in range(8):
                        MM(p1[:, 0:tn], wga[:, k, cl], xb[:, k, t0:t0 + tn], k == 0, k == 7, [wgak, ("xb", k, t), RB], [p1k])
                    ACT(t512[0][:, 0:tn], p1[:, 0:tn], AF.Sigmoid, [p1k], ["t512_0"])
                    p2, p2k = PS()
                    for k in range(4):
                        MM(p2[:, 0:tn], wa[:, k, cl], u_[:, k, t0:t0 + tn], k == 0, k == 3, [wak, ("u", k, t), RB], [p2k])
                    TT(t512[1][:, 0:tn], t512[0][:, 0:tn], p2[:, 0:tn], ALU.mult, ["t512_0", p2k], ["t512_1"])
                    p3, p3k = PS()
                    for k in range(8):
                        MM(p3[:, 0:tn], wgb[:, k, cl], xb[:, k, t0:t0 + tn], k == 0, k == 7, [wgbk, ("xb", k, t), RB], [p3k])
                    ACT(t512[2][:, 0:tn], p3[:, 0:tn], AF.Sigmoid, [p3k], ["t512_2"])
                    p4, p4k = PS()
                    for k in range(8):
                        MM(p4[:, 0:tn], wb_[:, k, cl], hmT[:, k, t0:t0 + tn], k == 0, k == 7, [wbk, ("hmT", k, t), RB], [p4k])
                    TT(t512[2][:, 0:tn], t512[2][:, 0:tn], p4[:, 0:tn], ALU.mult, ["t512_2", p4k], ["t512_2"])
                    TT(mg[:, oc, t0:t0 + tn], t512[1][:, 0:tn], t512[2][:, 0:tn], ALU.add, ["t512_1", "t512_2", RB],
                       [("mg", oc, t)])
            wm, wmk = WLOAD(w_mx[l][gq * 256:(gq + 1) * 256, :], 2, 1024)
            for oc in range(8):
                for t, (t0, tn) in enumerate(TILES):
                    p5, p5k = PS()
                    for k in range(2):
                        MM(p5[:, 0:tn], wm[:, k, oc * 128:(oc + 1) * 128], mg[:, k, t0:t0 + tn], k == 0, k == 1,
                           [wmk, ("mg", k, t), RB], [p5k])
                    xk = ("x32", oc, t)
                    if gq == 0:
                        STT(x32[:, oc, t0:t0 + tn], x32[:, oc, t0:t0 + tn], ALPHA, p5[:, 0:tn], ALU.mult, ALU.add,
                            [xk, p5k], [xk])
                    else:
                        TT(x32[:, oc, t0:t0 + tn], x32[:, oc, t0:t0 + tn], p5[:, 0:tn], ALU.add, [xk, p5k], [xk])

        def layer_norm(goff, boff, RK, t512):
            for t, (t0, tn) in enumerate(TILES):
                ps_s, pssk = PS()
                ps_q, psqk = PS()
                for oc in range(8):
                    xk = ("x32", oc, t)
                    yb_ = t512[0].bitcast(BF16)[:, 0:tn]
                    yq_ = t512[0].bitcast(BF16)[:, 512:512 + tn]
                    ACT(yb_, x32[:, oc, t0:t0 + tn], AF.Copy, [xk, RK], ["lnyb"])
                    ACT(yq_, x32[:, oc, t0:t0 + tn], AF.Square, [xk, RK], ["lnyq"])
                    MM(ps_s[:, 0:tn], onesb, yb_, oc == 0, oc == 7, ["lnyb", "onesb"], [pssk])
                    MM(ps_q[:, 0:tn], onesb, yq_, oc == 0, oc == 7, ["lnyq", "onesb"], [psqk])
                mean = t512[1][:, 0:tn]
                rstd = t512[2][:, 0:tn]
                ACT(mean, ps_s[:, 0:tn], AF.Identity, [pssk], ["lnmean"], scale=1.0 / D)
                TT(rstd, mean, mean, ALU.mult, ["lnmean"], ["lnrstd"])
                STT(rstd, ps_q[:, 0:tn], 1.0 / D, rstd, ALU.mult, ALU.subtract, [psqk, "lnrstd"], ["lnrstd"])
                ACT(rstd, rstd, AF.Ln, ["lnrstd"], ["lnrstd"], bias=EPS)
                ACT(rstd, rstd, AF.Exp, ["lnrstd"], ["lnrstd"], scale=-0.5)
                tmp = t512[0][:, 0:tn]
                for oc in range(8):
                    xk = ("x32", oc, t)
                    TT(tmp, x32[:, oc, t0:t0 + tn], mean, ALU.subtract, [xk, "lnmean"], ["lntmp", "lnyb", "lnyq"])
                    TT(tmp, tmp, rstd, ALU.mult, ["lntmp", "lnrstd"], ["lntmp"])
                    ACT(x32[:, oc, t0:t0 + tn], tmp, AF.Identity, ["lntmp", "lnyb", "lnyq", "pp"], [xk],
                        bias=ppc(boff, l, oc), scale=ppc(goff, l, oc))
                    CP(xb[:, oc, t0:t0 + tn], x32[:, oc, t0:t0 + tn], [xk, RK], [("xb", oc, t)], eng="dve")

        layer_norm(PP_LN, PP_LN + 32, RB, t512)

        cv = Carver()
        hbuf = [cv.take([128, 4, NT], BF16) for _ in range(2)]
        upb = [cv.take([128, 2 + NT], F32) for _ in range(2)]
        cvb = [cv.take([128, NT], F32) for _ in range(2)]
        t512c = [cv.take([128, 512], F32) for _ in range(3)]
        cfT = cv.take([128, 2, 2, NS_], F32)
        ffp = cv.take([128, 44, 2], F32)
        ffs = cv.take([128, 44, NS_], F32)
        RC = ("RC", l)
        barrier(RC, bscr)
        DMA("sp", ff_s[l, 0], cf_in[l, 1], [], [], "ff_s0")
        for gu in range(2):
            MSET(upb[gu][:, 0:2], 0.0, [("upb0", gu)])
        groups = [list(range(a, min(a + 4, 22))) for a in range(0, 22, 4)]
        for gi, grp in enumerate(groups):
            hb = hbuf[gi % 2]
            hk = lambda jj, t: ("h", gi % 2, jj, t)
            for pi in range(0, len(grp), 2):
                prs = grp[pi:pi + 2]
                j0 = prs[0]
                ncol = 128 * len(prs)
                wg_, wgk_ = WLOAD(w_up[l][:, j0 * 128:j0 * 128 + ncol], 8, ncol)
                wu_, wuk_ = WLOAD(w_up[l][:, DFF + j0 * 128:DFF + j0 * 128 + ncol], 8, ncol)
                for pj, j in enumerate(prs):
                    jj = j - grp[0]
                    cl = slice(pj * 128, (pj + 1) * 128)
                    for gu, (ww, wwk) in enumerate(((wg_, wgk_), (wu_, wuk_))):
                        chunk = j + gu * 22
                        ub = upb[gu]
                        cb_ = cvb[gu]
                        DMA("sp", cfT[:, :, gu, :], cf_in[l, :, :, chunk, :].rearrange("r p b -> p r b"), [RC],
                            [("cfT", gu)], ("cfT", gu))
                        for t, (t0, tn) in enumerate(TILES):
                            pu, puk = PS()
                            for k in range(8):
                                MM(pu[:, 0:tn], ww[:, k, cl], xb[:, k, t0:t0 + tn], k == 0, k == 7,
                                   [wwk, ("xb", k, t), RC], [puk])
                            CP(ub[:, 2 + t0:2 + t0 + tn], pu[:, 0:tn], [puk, ("upb0", gu), RC], [("upb", gu, t)])
                        fc = lambda jtap: pp[:, PP_FC + (l * 3 + jtap) * 44 + chunk:PP_FC + (l * 3 + jtap) * 44 + chunk + 1]
                        ubk = [("upb", gu, t) for t in range(5)] + [("upb0", gu)]
                        cvk = ("cvb", gu)
                        ACT(cb_[:, 0:SEQ], ub[:, 0:SEQ], AF.Identity, ubk + ["pp", RC], [cvk], scale=fc(0))
                        STT(cb_[:, 0:SEQ], ub[:, 1:SEQ + 1], fc(1), cb_[:, 0:SEQ], ALU.mult, ALU.add, ubk + [cvk], [cvk])
                        STT(cb_[:, 0:SEQ], ub[:, 2:SEQ + 2], fc(2), cb_[:, 0:SEQ], ALU.mult, ALU.add, ubk + [cvk], [cvk])
                        cvks = ("cvbs", gu)
                        TS(cb_[:, SEQ:NT], cfT[:, 0, gu, :], fc(0), None, ALU.mult, None, [("cfT", gu), cvk], [cvks])
                        STT(cb_[:, SEQ:NT], cfT[:, 1, gu, :], fc(1), cb_[:, SEQ:NT], ALU.mult, ALU.add, [("cfT", gu), cvks], [cvks])
                        STT(cb_[:, SEQ:NT], ub[:, 2 + SEQ:2 + NT], fc(2), cb_[:, SEQ:NT], ALU.mult, ALU.add, ubk + [cvks], [cvks])
                        CP(ffp[:, chunk, :], ub[:, SEQ:SEQ + 2], ubk, [("ffp", chunk)], eng="act")
                        CP(ffs[:, chunk, :], ub[:, 2 + SEQ:2 + NT], ubk, [("ffs", chunk)], eng="act")
                    ACT(cvb[0], cvb[0], AF.Silu, [("cvb", 0), ("cvbs", 0)], [("cvb", 0), ("cvbs", 0)])
                    for t, (t0, tn) in enumerate(TILES):
                        TT(hb[:, jj, t0:t0 + tn], cvb[0][:, t0:t0 + tn], cvb[1][:, t0:t0 + tn], ALU.mult,
                           [("cvb", 0), ("cvbs", 0), ("cvb", 1), ("cvbs", 1), RC], [hk(jj, t)])
            kcn = len(grp)
            for half in range(2):
                wd, wdk = WLOAD(w_dn[l][grp[0] * 128:(grp[0] + kcn) * 128, half * 512:(half + 1) * 512], kcn, 512)
                for o4 in range(4):
                    oc = half * 4 + o4
                    for t, (t0, tn) in enumerate(TILES):
                        p6, p6k = PS()
                        for k in range(kcn):
                            MM(p6[:, 0:tn], wd[:, k, o4 * 128:(o4 + 1) * 128], hb[:, k, t0:t0 + tn], k == 0, k == kcn - 1,
                               [wdk, hk(k, t), RC], [p6k])
                        xk = ("x32", oc, t)
                        if gi == 0:
                            STT(x32[:, oc, t0:t0 + tn], x32[:, oc, t0:t0 + tn], ALPHA, p6[:, 0:tn], ALU.mult, ALU.add,
                                [xk, p6k], [xk])
                        else:
                            TT(x32[:, oc, t0:t0 + tn], x32[:, oc, t0:t0 + tn], p6[:, 0:tn], ALU.add, [xk, p6k], [xk])
        DMA("sp", ff_p[l], ffp, [("ffp", c) for c in range(44)], [], "ff_p")
        DMA("sp", ff_s[l, 1], ffs, [("ffs", c) for c in range(44)], [], "ff_s1")
        layer_norm(PP_LN + 64, PP_LN + 96, RC, t512c)

    yTv = yT.rearrange("(c p) n -> p c n", p=128)
    for c in range(8):
        DMA("sp", yTv[:, c, :], x32[:, c, :], [("x32", c, t) for t in range(5)], [], ("yT", c))

    P.resolve()
    with nc.Block() as block:
        P.emit(block)
    return nc


_NC_CACHE = {}


def _consts():
    c = np.zeros((128, C_N), np.float32)
    idx = np.arange(128)
    c[:, C_ID:C_ID + 128] = np.eye(128, dtype=np.float32)
    c[:, C_U:C_U + 128] = (idx[:, None] <= idx[None, :]).astype(np.float32)
    c[:, C_MTS:C_MTS + 128] = np.where(idx[None, :] <= idx[:, None], 0.0, NEG)
    c[127, C_S127:C_S127 + 128] = 1.0
    c[:, C_MST:C_MST + 128] = np.where(idx[:, None] <= idx[None, :], 0.0, NEG)
    c[:, C_I16:C_I16 + 256] = np.eye(16, dtype=np.float32).reshape(1, 256)
    return c


def kernel(x_prompt, x_sample, cache_sconv, state_mlstm_C, state_mlstm_n, state_mlstm_m, cache_ffn_conv,
           w_in, b_igate, b_fgate, w_conv_mix, mhln_g, w_proj_a, w_proj_b, w_mix_out, ln1_g, ln1_b,
           w_ffn_up, w_ffn_conv, w_ffn_down, ln2_g, ln2_b):
    f = lambda a: np.ascontiguousarray(np.asarray(a, dtype=np.float32))
    x_prompt, x_sample = f(x_prompt), f(x_sample)
    cache_sconv, cache_ffn_conv = f(cache_sconv), f(cache_ffn_conv)
    state_mlstm_C, state_mlstm_n, state_mlstm_m = f(state_mlstm_C), f(state_mlstm_n), f(state_mlstm_m)
    NCORE = 8
    pp = np.zeros((128, PP_N), np.float32)
    for i, a in enumerate((ln1_g, ln1_b, ln2_g, ln2_b)):
        pp[:, PP_LN + i * 32:PP_LN + (i + 1) * 32] = f(a).reshape(L, 8, 128).transpose(2, 0, 1).reshape(128, 32)
    pp[:, PP_CM:PP_CM + 48] = f(w_conv_mix).reshape(L, 3, 4, 128).transpose(3, 0, 1, 2).reshape(128, 48)
    pp[:, PP_FC:PP_FC + 528] = f(w_ffn_conv).reshape(L, 3, 44, 128).transpose(3, 0, 1, 2).reshape(128, 528)
    gb = np.concatenate([f(b_igate), f(b_fgate)], axis=1).reshape(1, L * 8)
    pp[:, PP_GB:PP_GB + 32] = np.broadcast_to(gb, (128, 32))
    cst = _consts()
    shared = {"pp": pp, "cst": cst, "mhln_g": f(mhln_g), "w_in": f(w_in), "w_proj_a": f(w_proj_a),
              "w_proj_b": f(w_proj_b), "w_mix_out": f(w_mix_out), "w_ffn_up": f(w_ffn_up), "w_ffn_down": f(w_ffn_down)}
    in_maps = []
    for c in range(NCORE):
        sl = slice(c * NS_, (c + 1) * NS_)
        xT = np.ascontiguousarray(np.concatenate([x_prompt[c].T, x_sample[sl, 0, :].T], axis=1))
        cs = np.ascontiguousarray(cache_sconv[:, sl].reshape(L, NS_, 2, 4, 128).transpose(0, 2, 4, 3, 1))
        cf = np.ascontiguousarray(cache_ffn_conv[:, sl].reshape(L, NS_, 2, 44, 128).transpose(0, 2, 4, 3, 1))
        m = dict(shared)
        m.update({"xT": xT, "cs_in": cs, "cf_in": cf,
                  "Cs_in": np.ascontiguousarray(state_mlstm_C[:, sl]),
                  "ns_in": np.ascontiguousarray(state_mlstm_n[:, sl].reshape(L, NS_, H * DH)),
                  "ms_in": np.ascontiguousarray(state_mlstm_m[:, sl])})
        in_maps.append(m)
    if "nc" not in _NC_CACHE:
        _NC_CACHE["nc"] = build_nc()
    nc = _NC_CACHE["nc"]
    res = run_bass_kernel_spmd(nc, in_maps, core_ids=list(range(NCORE)))
    R = res.results
    y_p = np.stack([R[c]["yT"][:, :SEQ].T for c in range(NCORE)])
    y_s = np.concatenate([R[c]["yT"][:, SEQ:].T for c in range(NCORE)])[:, None, :]
    def conv_p(key, C):
        return np.stack([R[c][key].transpose(0, 3, 2, 1).reshape(L, 2, C * 128) for c in range(NCORE)], axis=1)

    def conv_s(key, C):
        return np.concatenate([R[c][key].transpose(0, 4, 1, 3, 2).reshape(L, NS_, 2, C * 128) for c in range(NCORE)], axis=1)

    sp_ = conv_p("sc_p", 4)
    ss_ = conv_s("sc_s", 4)
    fp_ = conv_p("ff_p", 44)
    fs_ = conv_s("ff_s", 44)
    Cp = np.stack([R[c]["Cp_o"] for c in range(NCORE)], axis=1)
    Cs = np.concatenate([R[c]["Cs_o"] for c in range(NCORE)], axis=1)
    np_ = np.stack([R[c]["np_o"].transpose(0, 1, 3, 2).reshape(L, H, DH) for c in range(NCORE)], axis=1)
    ns_ = np.concatenate([R[c]["ns_o"].reshape(L, NS_, H, DH) for c in range(NCORE)], axis=1)
    mp_ = np.stack([R[c]["mp_o"] for c in range(NCORE)], axis=1)
    ms_ = np.concatenate([R[c]["ms_o"] for c in range(NCORE)], axis=1)
    out = (y_p, y_s, sp_, ss_, Cp, Cs, np_, ns_, mp_, ms_, fp_, fs_)
    return tuple(np.ascontiguousarray(o, dtype=np.float32) for o in out)
```

```python
import numpy as np
import concourse.bass as bass
import concourse.mybir as mybir
from concourse.bass_utils import run_bass_kernel_spmd

F32 = mybir.dt.float32
BF16 = mybir.dt.bfloat16
AF = mybir.ActivationFunctionType
ALU = mybir.AluOpType
AX = mybir.AxisListType

L = 4
D = 1024
SEQ = 2048
NS_ = 16
NT = SEQ + NS_
H = 4
DH = 256
DFF = 2816
DIN = 7688
EPS = 1e-5
ALPHA = (2.0 * L) ** 0.25
TILES = [(0, 512), (512, 512), (1024, 512), (1536, 512), (2048, 16)]
O_B, O_C, O_XV, O_Q, O_K, O_V, O_O, O_IG, O_GA, O_GB = 0, 512, 1024, 1536, 2560, 3584, 4608, 5632, 5640, 6664
NSLOT = 5
NEG = -1.0e30

PP_LN = 0
PP_CM = 128
PP_FC = 176
PP_GB = 704
PP_N = 736
C_ID, C_U, C_MTS, C_S127, C_MST, C_I16, C_N = 0, 128, 256, 384, 512, 640, 896

ENG = ("pe", "act", "dve", "pool", "sp")


class Op:
    __slots__ = ("eng", "fn", "r", "w", "dma", "chain", "waits", "inc", "idx")

    def __init__(self, eng, fn, r, w, dma, chain):
        self.eng, self.fn, self.r, self.w, self.dma, self.chain = eng, fn, r, w, dma, chain
        self.waits = []
        self.inc = None
        self.idx = -1


class Prog:
    def __init__(self, nc):
        self.nc = nc
        self.ops = []

    def op(self, eng, fn, r=(), w=(), dma=False, chain=None):
        o = Op(eng, fn, tuple(r), tuple(w), dma, chain)
        o.idx = len(self.ops)
        self.ops.append(o)
        return o

    def resolve(self):
        last_w, readers = {}, {}
        n = len(self.ops)
        deps = [None] * n
        has_dep = [False] * n
        for o in self.ops:
            d = set()
            for k in o.r:
                lw = last_w.get(k)
                if lw is not None:
                    d.add(lw)
            for k in o.w:
                lw = last_w.get(k)
                if lw is not None:
                    d.add(lw)
                rl = readers.get(k)
                if rl:
                    d.update(rl)
            d.discard(o.idx)
            if o.eng == "pe" and not o.dma:
                d = {x for x in d if not (self.ops[x].eng == "pe" and not self.ops[x].dma)}
            latest = {}
            for x in d:
                ox = self.ops[x]
                sk = ("c", ox.chain) if ox.dma else ("e", ox.eng)
                if x > latest.get(sk, -1):
                    latest[sk] = x
            d = set(latest.values())
            deps[o.idx] = d
            for x in d:
                has_dep[x] = True
            for k in o.w:
                last_w[k] = o.idx
                readers[k] = []
            for k in o.r:
                readers.setdefault(k, []).append(o.idx)
        eng_cnt = {e: 0 for e in ENG}
        chain_cnt = {}
        self.chains = []
        ms = [None] * n
        for o in self.ops:
            if o.dma:
                c = chain_cnt.get(o.chain, 0) + 16
                chain_cnt[o.chain] = c
                if c == 16:
                    self.chains.append(o.chain)
                ms[o.idx] = (("c", o.chain), c)
                o.inc = ("c", o.chain)
            elif has_dep[o.idx]:
                eng_cnt[o.eng] += 1
                ms[o.idx] = (("e", o.eng), eng_cnt[o.eng])
                o.inc = ("e", o.eng)
        self.eng_cnt, self.chain_cnt = eng_cnt, chain_cnt
        seen = {e: {} for e in ENG}
        for o in self.ops:
            need = {}
            for x in deps[o.idx]:
                s, v = ms[x]
                if v > need.get(s, 0):
                    need[s] = v
            sn = seen[o.eng]
            for s, v in need.items():
                if sn.get(s, 0) >= v:
                    continue
                sn[s] = v
                o.waits.append((s, v))
        return self

    def emit(self, block):
        nc = self.nc
        sems = {}
        for e in ENG:
            if self.eng_cnt[e] > 0:
                sems[("e", e)] = nc.alloc_semaphore("se_" + e)
        for i, c in enumerate(self.chains):
            sems[("c", c)] = nc.alloc_semaphore("sc_%d" % i)
        per = {e: [o for o in self.ops if o.eng == e] for e in ENG}

        def run(eh, ename):
            for o in per[ename]:
                for s, v in o.waits:
                    eh.wait_ge(sems[s], v)
                ins = o.fn(eh)
                if o.inc is not None:
                    ins.then_inc(sems[o.inc], 16 if o.dma else 1)
            if ename == "sp":
                for c in self.chains:
                    eh.wait_ge(sems[("c", c)], self.chain_cnt[c])

        @block.tensor
        def _(e):
            run(e, "pe")

        @block.scalar
        def _(e):
            run(e, "act")

        @block.vector
        def _(e):
            run(e, "dve")

        @block.gpsimd
        def _(e):
            run(e, "pool")

        @block.sync
        def _(e):
            run(e, "sp")


def build_nc():
    nc = bass.Bass("TRN2", target_bir_lowering=False)

    def din(name, shape):
        return nc.dram_tensor(name, list(shape), F32, kind="ExternalInput").ap()

    def dout(name, shape):
        return nc.dram_tensor(name, list(shape), F32, kind="ExternalOutput").ap()

    xT = din("xT", [D, NT])
    pp_d = din("pp", [128, PP_N])
    cst_d = din("cst", [128, C_N])
    mh_d = din("mhln_g", [L, D])
    w_in = din("w_in", [L, D, DIN])
    w_pa = din("w_proj_a", [L, 512, D])
    w_pb = din("w_proj_b", [L, D, D])
    w_mx = din("w_mix_out", [L, D, D])
    w_up = din("w_ffn_up", [L, D, 2 * DFF])
    w_dn = din("w_ffn_down", [L, DFF, D])
    cs_in = din("cs_in", [L, 2, 128, 4, NS_])
    cf_in = din("cf_in", [L, 2, 128, 44, NS_])
    Cs_in = din("Cs_in", [L, NS_, H, DH, DH])
    ns_in = din("ns_in", [L, NS_, H * DH])
    ms_in = din("ms_in", [L, NS_, H])

    yT = dout("yT", [D, NT])
    sc_p = dout("sc_p", [L, 128, 4, 2])
    sc_s = dout("sc_s", [L, 2, 128, 4, NS_])
    Cp_o = dout("Cp_o", [L, H, DH, DH])
    Cs_o = dout("Cs_o", [L, NS_, H, DH, DH])
    np_o = dout("np_o", [L, H, 128, 2])
    ns_o = dout("ns_o", [L, NS_, H * DH])
    mp_o = dout("mp_o", [L, H])
    ms_o = dout("ms_o", [L, NS_, H])
    ff_p = dout("ff_p", [L, 128, 44, 2])
    ff_s = dout("ff_s", [L, 2, 128, 44, NS_])

    P = Prog(nc)

    def sb(name, shape, dt=F32):
        return nc.alloc_sbuf_tensor("sb_" + name, list(shape), dt).ap()

    x32 = sb("x32", [128, 8, NT])
    xb = sb("xb", [128, 8, NT], BF16)
    slots = [sb("ws%d" % i, [128, 2048], BF16) for i in range(NSLOT)]
    S = nc.alloc_sbuf_tensor("S", [128, 79 * 1024], mybir.dt.uint8)
    pp = sb("pp", [128, PP_N])
    cst = sb("cst", [128, C_N])
    identb = sb("identb", [128, 128], BF16)
    mstb = sb("mstb", [128, 128], BF16)
    onesb = sb("onesb", [128, 128], BF16)
    ones32 = sb("ones32", [128, 128])
    gbc = sb("gbc", [128, DH])
    psum = [nc.alloc_psum_tensor("ps%d" % i, [128, 512], F32).ap() for i in range(8)]

    class Carver:
        def __init__(self):
            self.off = 0

        def take(self, shape, dt):
            esz = 2 if dt == BF16 else 4
            n = 1
            for s in shape[1:]:
                n *= s
            nbytes = (n * esz + 31) // 32 * 32
            assert self.off + nbytes <= 79 * 1024, (self.off, nbytes)
            v = S[:, self.off:self.off + nbytes].bitcast(dt)[:, 0:n]
            self.off += nbytes
            if len(shape) == 3:
                v = v.rearrange("p (a b) -> p a b", b=shape[2])
            elif len(shape) == 4:
                v = v.rearrange("p (a b c) -> p a b c", b=shape[2], c=shape[3])
            return v[0:shape[0]]

    state = {"ps": 0, "ws": 0, "ev": 0, "tok": None}
    _raw_op = P.op

    def _op(eng, fn, r=(), w=(), dma=False, chain=None):
        r = list(r)
        if state["tok"] is not None and eng != "pool":
            r.append(state["tok"])
        return _raw_op(eng, fn, r, w, dma, chain)

    P.op = _op

    def PS():
        i = state["ps"] % 7
        state["ps"] += 1
        return psum[i], ("ps", i)

    def PSL():
        return psum[7], ("ps", 7)

    def MM(out, lhsT, rhs, start, stop, r, w):
        P.op("pe", lambda e: e.matmul(out, lhsT=lhsT, rhs=rhs, start=start, stop=stop), r=r, w=w)

    def TR(out, in_, ident, r, w):
        P.op("pe", lambda e: e.transpose(out, in_, ident), r=r, w=w)

    def ACT(out, in_, func, r, w, bias=0.0, scale=1.0):
        P.op("act", lambda e: e.activation(out=out, in_=in_, func=func, bias=bias, scale=scale), r=r, w=w)

    def TT(out, in0, in1, op, r, w, eng="dve"):
        P.op(eng, lambda e: e.tensor_tensor(out=out, in0=in0, in1=in1, op=op), r=r, w=w)

    def TS(out, in0, s1, s2, op0, op1, r, w):
        if s2 is None:
            P.op("dve", lambda e: e.tensor_scalar(out=out, in0=in0, scalar1=s1, scalar2=None, op0=op0), r=r, w=w)
        else:
            P.op("dve", lambda e: e.tensor_scalar(out=out, in0=in0, scalar1=s1, scalar2=s2, op0=op0, op1=op1), r=r, w=w)

    def STT(out, in0, scalar, in1, op0, op1, r, w):
        P.op("dve", lambda e: e.scalar_tensor_tensor(out=out, in0=in0, scalar=scalar, in1=in1, op0=op0, op1=op1),
             r=r, w=w)

    def CP(out, in_, r, w, eng=None):
        if eng is None:
            eng = "act" if state["ev"] % 2 == 0 else "dve"
            state["ev"] += 1
        if eng == "act":
            P.op("act", lambda e: e.activation(out=out, in_=in_, func=AF.Copy), r=r, w=w)
        else:
            P.op("dve", lambda e: e.tensor_copy(out=out, in_=in_), r=r, w=w)

    def PTS(out, in0, s1, r, w):
        P.op("act", lambda e: e.activation(out=out, in_=in0, func=AF.Identity, scale=s1), r=r, w=w)

    def PCP(out, in_, r, w):
        P.op("act", lambda e: e.activation(out=out, in_=in_, func=AF.Copy), r=r, w=w)

    def MSET(ap, val, w, eng="dve"):
        P.op(eng, lambda e: e.memset(ap, val), w=w)

    def DMA(q, out, in_, r, w, chain):
        P.op(q, lambda e: e.dma_start(out=out, in_=in_), r=r, w=w, dma=True, chain=chain)

    def WLOAD(src2d, KC, ncols):
        i = state["ws"] % NSLOT
        state["ws"] += 1
        v = slots[i][:, 0:KC * ncols].rearrange("p (k n) -> p k n", n=ncols)
        key = ("ws", i)
        DMA("pool", v, src2d.rearrange("(k p) n -> p k n", p=128), r=[], w=[key], chain=key)
        return v, key

    def xbk(t):
        return [("xb", k, t) for k in range(8)]

    def barrier(newtok, scratch):
        old = state["tok"]
        wk = [newtok] + ([old] if old is not None else [])
        _raw_op("dve", lambda e: e.memset(scratch, 0.0), [], wk, False, None)
        state["tok"] = newtok

    DMA("sp", pp, pp_d, [], ["pp"], "pp")
    DMA("sp", cst, cst_d, [], ["cst"], "cst")
    CP(identb, cst[:, C_ID:C_ID + 128], ["cst"], ["identb"], eng="dve")
    CP(mstb, cst[:, C_MST:C_MST + 128], ["cst"], ["mstb"], eng="dve")
    MSET(onesb, 1.0, ["onesb"])
    MSET(ones32, 1.0, ["ones32"])
    ident32 = cst[:, C_ID:C_ID + 128]
    Utri = cst[:, C_U:C_U + 128]
    mts = cst[:, C_MTS:C_MTS + 128]
    sel127 = cst[:, C_S127:C_S127 + 128]
    I16bc = cst[:, C_I16:C_I16 + 256].rearrange("p (a b) -> p a b", b=16)
    xTv = xT.rearrange("(c p) n -> p c n", p=128)
    for c in range(8):
        DMA("sp", x32[:, c, :], xTv[:, c, :], [], [("x32", c, t) for t in range(5)], ("x32i", c))
        DMA("pool", xb[:, c, :], xTv[:, c, :], [], [("xb", c, t) for t in range(5)], ("xbi", c))
    bscr = sb("bscr", [128, 8])

    def ppc(base, l, i):
        return pp[:, base + l * 8 + i: base + l * 8 + i + 1]

    for l in range(L):
        cv = Carver()
        hmT = cv.take([128, 8, NT], BF16)
        qT = cv.take([128, 2, 512], BF16)
        kT = cv.take([128, 2, 512], BF16)
        ktok = cv.take([128, 4, DH], BF16)
        vtok = cv.take([128, 4, DH + 2], BF16)
        gsig = cv.take([128, 4, DH], F32)
        Et = cv.take([128, 128], F32)
        PTt = cv.take([128, 128], BF16)
        diag = cv.take([128, 128], F32)
        tmp128 = Et
        numB = cv.take([128, DH + 1], F32)
        num = cv.take([128, DH + 1], F32)
        hn = cv.take([128, DH], F32)
        hg = cv.take([128, DH], BF16)
        Kw = cv.take([128, DH], BF16)
        C32 = cv.take([128, 2, DH], F32)
        n32 = cv.take([128, 2], F32)
        Cb = cv.take([128, 2, DH + 2], BF16)
        st6 = cv.take([128, 6], F32)
        mv = cv.take([128, 2], F32)
        sc4 = cv.take([128, 4], F32)
        G_ = {}
        for nm in ("gp", "li", "sp", "bloc", "tot", "ginc", "G", "a", "cm", "cmx", "M", "nM", "fl", "sI", "wS",
                   "dec", "t1"):
            G_[nm] = cv.take([128, 16, 4], F32)
        MT = cv.take([128, 17, 4], F32)
        q_s = cv.take([NS_, DH], F32)
        k_s = cv.take([NS_, DH], F32)
        v_s = cv.take([NS_, DH], F32)
        g_s = cv.take([NS_, DH], F32)
        t_s_full = cv.take([128, DH], F32)[:, 0:128]
        kw_s_full = cv.take([128, DH], F32)[:, 0:128]
        vm_s_full = cv.take([128, DH], F32)[:, 0:128]
        cv.off -= 3 * 1024
        t_s = cv.take([NS_, DH], F32)
        kw_s = cv.take([NS_, DH], F32)
        vm_s = cv.take([NS_, DH], F32)
        nold_s = cv.take([NS_, DH], F32)
        nnew_s = cv.take([NS_, DH], F32)
        num_s = cv.take([NS_, DH], F32)
        hg_s = cv.take([NS_, DH], BF16)
        sg = {}
        for nm in ("gps", "li", "lf", "m", "mn", "dec", "wg", "fl", "qk", "qn", "s", "den", "rdn", "t"):
            sg[nm] = cv.take([NS_, 8], F32)
        Dm = cv.take([NS_, 4, NS_], F32)
        decs_bc = cv.take([128, 4, NS_], F32)
        qTs = cv.take([128, 2, NS_], F32)
        Qm = cv.take([128, 2, NS_, NS_], F32)
        Cst = [cv.take([128, 2, DH], F32) for _ in range(3)]
        st6s = cv.take([NS_, 6], F32)
        mvs = cv.take([NS_, 2], F32)
        RA = ("RA", l)
        barrier(RA, bscr)

        gw, gwk = WLOAD(w_in[l][:, O_IG:O_IG + 8], 8, 8)
        gps, gpk = PS()
        gpsv = gps[:, 0:128].rearrange("p (c g) -> p c g", g=8)
        for c in range(16):
            for k in range(8):
                MM(gpsv[:, c, :], xb[:, k, c * 128:(c + 1) * 128], gw[:, k, :], k == 0, k == 7,
                   [gwk, ("xb", k, c // 4), RA], [gpk])
        bi_bc = pp[:, PP_GB + l * 8:PP_GB + l * 8 + 4]
        bf_bc = pp[:, PP_GB + l * 8 + 4:PP_GB + l * 8 + 8]
        g = G_
        K = lambda nm: ("g", nm)
        TT(g["li"], gpsv[:, :, 0:4], bi_bc.unsqueeze(1).broadcast_to([128, 16, 4]), ALU.add, [gpk, "pp", RA], [K("li")])
        TT(g["t1"], gpsv[:, :, 4:8], bf_bc.unsqueeze(1).broadcast_to([128, 16, 4]), ALU.add, [gpk, "pp", RA], [K("t1")])
        ACT(g["sp"], g["t1"], AF.Exp, [K("t1")], [K("sp")], scale=-1.0)
        ACT(g["sp"], g["sp"], AF.Ln, [K("sp")], [K("sp")], bias=1.0)
        f64 = lambda t: t.rearrange("p c h -> p (c h)")
        ps1, pk1 = PS()
        MM(ps1[:, 0:64], Utri, f64(g["sp"]), True, True, [K("sp"), "cst"], [pk1])
        CP(f64(g["bloc"]), ps1[:, 0:64], [pk1], [K("bloc")], eng="dve")
        ps2, pk2 = PS()
        MM(ps2[:, 0:64], ones32, f64(g["sp"]), True, True, [K("sp"), "ones32"], [pk2])
        CP(f64(g["tot"]), ps2[:, 0:64], [pk2], [K("tot")], eng="dve")
        for h in range(H):
            P.op("dve", lambda e, h=h: e.tensor_tensor_scan(out=g["ginc"][:, :, h], data0=g["tot"][:, :, h],
                                                            data1=g["tot"][:, :, h], initial=0.0,
                                                            op0=ALU.add, op1=ALU.bypass),
                 r=[K("tot")], w=[K("ginc")])
        TT(g["G"], g["ginc"], g["tot"], ALU.subtract, [K("ginc"), K("tot")], [K("G")])
        TT(g["G"], g["G"], g["bloc"], ALU.add, [K("G"), K("bloc")], [K("G")])
        TT(g["a"], g["li"], g["G"], ALU.add, [K("li"), K("G")], [K("a")])
        dgs = [(diag, "diag"), (t_s_full, "t_s"), (kw_s_full, "kw_s"), (vm_s_full, "vm_s")]
        items = [(c, hh_) for c in range(16) for hh_ in range(H)]

        def cm_front(i):
            c, hh_ = items[i]
            dg, dk = dgs[i % 4]
            TS(dg, ident32, g["a"][:, c, hh_:hh_ + 1], None, ALU.mult, None, [K("a"), "cst"], [dk])

        LOOK = 3
        for i in range(min(LOOK, len(items))):
            cm_front(i)
        for i, (c, hh_) in enumerate(items):
            dg, dk = dgs[i % 4]
            psd, pkd = PS()
            MM(psd[:, 0:128], ones32, dg, True, True, [dk, "ones32"], [pkd])
            if i + LOOK < len(items):
                cm_front(i + LOOK)
            TT(tmp128, psd[:, 0:128], mts, ALU.add, [pkd, "cst"], ["Et"])
            P.op("dve", lambda e, c=c, hh_=hh_: e.tensor_reduce(out=g["cm"][:, c, hh_:hh_ + 1], in_=tmp128, axis=AX.X,
                                                                op=ALU.max), r=["Et"], w=[K("cm")])
        ps3, pk3 = PS()
        MM(ps3[:, 0:64], sel127, f64(g["cm"]), True, True, [K("cm"), "cst"], [pk3])
        CP(f64(g["cmx"]), ps3[:, 0:64], [pk3], [K("cmx")], eng="dve")
        MSET(MT[:, 0, :], 0.0, [K("MT")])
        for h in range(H):
            P.op("dve", lambda e, h=h: e.tensor_tensor_scan(out=MT[:, 1:17, h], data0=g["cmx"][:, :, h],
                                                            data1=g["cmx"][:, :, h], initial=0.0,
                                                            op0=ALU.max, op1=ALU.bypass),
                 r=[K("cmx"), K("MT")], w=[K("MT")])
        Mprev = MT[:, 0:16, :]
        Mend = MT[:, 1:17, :]
        TT(g["M"], g["cm"], Mprev, ALU.max, [K("cm"), K("MT")], [K("M")])
        TS(g["nM"], g["M"], -1.0, None, ALU.mult, None, [K("M")], [K("nM")])
        TT(g["t1"], g["G"], g["M"], ALU.subtract, [K("G"), K("M"), K("sp")], [K("t1")])
        ACT(g["fl"], g["t1"], AF.Exp, [K("t1")], [K("fl")])
        TT(g["sI"], Mprev, g["M"], ALU.subtract, [K("MT"), K("M")], [K("sI")])
        ACT(g["sI"], g["sI"], AF.Exp, [K("sI")], [K("sI")])
        TT(g["wS"], g["a"], Mend, ALU.subtract, [K("a"), K("MT")], [K("wS")])
        ACT(g["wS"], g["wS"], AF.Exp, [K("wS")], [K("wS")])
        TT(g["dec"], Mprev, Mend, ALU.subtract, [K("MT")], [K("dec")])
        ACT(g["dec"], g["dec"], AF.Exp, [K("dec")], [K("dec")])
        TT(sc4, MT[:, 16, :], g["ginc"][:, 15, :], ALU.subtract, [K("MT"), K("ginc")], ["sc4"])
        DMA("sp", mp_o[l:l + 1, :], sc4[0:1, :], ["sc4"], [], "mp_o")

        s = sg
        SK = lambda nm: ("sg", nm)
        psg, pkg = PS()
        for k in range(8):
            MM(psg[0:NS_, 0:8], xb[:, k, SEQ:NT], gw[:, k, :], k == 0, k == 7, [gwk, ("xb", k, 4), RA], [pkg])
        DMA("sp", s["m"][:, 0:4], ms_in[l], [RA], [SK("m")], "ms_in")
        TT(s["li"][:, 0:4], psg[0:NS_, 0:4], bi_bc[0:NS_], ALU.add, [pkg, "pp"], [SK("li")])
        TT(s["t"][:, 0:4], psg[0:NS_, 4:8], bf_bc[0:NS_], ALU.add, [pkg, "pp"], [SK("t")])
        ACT(s["lf"][:, 0:4], s["t"][:, 0:4], AF.Exp, [SK("t")], [SK("lf")], scale=-1.0)
        ACT(s["lf"][:, 0:4], s["lf"][:, 0:4], AF.Ln, [SK("lf")], [SK("lf")], bias=1.0)
        TT(s["t"][:, 0:4], s["m"][:, 0:4], s["lf"][:, 0:4], ALU.subtract, [SK("m"), SK("lf"), SK("t")], [SK("t")])
        TT(s["mn"][:, 0:4], s["t"][:, 0:4], s["li"][:, 0:4], ALU.max, [SK("t"), SK("li")], [SK("mn")])
        TT(s["dec"][:, 0:4], s["t"][:, 0:4], s["mn"][:, 0:4], ALU.subtract, [SK("t"), SK("mn")], [SK("dec")])
        ACT(s["dec"][:, 0:4], s["dec"][:, 0:4], AF.Exp, [SK("dec")], [SK("dec")])
        TT(s["wg"][:, 0:4], s["li"][:, 0:4], s["mn"][:, 0:4], ALU.subtract, [SK("li"), SK("mn")], [SK("wg")])
        ACT(s["wg"][:, 0:4], s["wg"][:, 0:4], AF.Exp, [SK("wg")], [SK("wg")])
        ACT(s["fl"][:, 0:4], s["mn"][:, 0:4], AF.Exp, [SK("mn")], [SK("fl")], scale=-1.0)
        DMA("sp", ms_o[l], s["mn"][:, 0:4], [SK("mn")], [], "ms_o")
        TT(Dm, s["dec"][:, 0:4].unsqueeze(2).broadcast_to([NS_, 4, NS_]),
           ident32[0:NS_, 0:NS_].unsqueeze(1).broadcast_to([NS_, 4, NS_]), ALU.mult, [SK("dec"), "cst"], ["Dm"])
        psb, pkb = PS()
        MM(psb[:, 0:64], ones32[0:NS_, :], Dm.rearrange("p h b -> p (h b)"), True, True, ["Dm", "ones32"], [pkb])
        CP(decs_bc.rearrange("p h b -> p (h b)"), psb[:, 0:64], [pkb], ["decs_bc"], eng="dve")

        for h in range(H):
            wq, wqk = WLOAD(w_in[l][:, O_Q + h * DH:O_Q + (h + 1) * DH], 8, DH)
            wk, wkk = WLOAD(w_in[l][:, O_K + h * DH:O_K + (h + 1) * DH], 8, DH)
            wv, wvk = WLOAD(w_in[l][:, O_V + h * DH:O_V + (h + 1) * DH], 8, DH)
            wo, wok = WLOAD(w_in[l][:, O_O + h * DH:O_O + (h + 1) * DH], 8, DH)
            DMA("sp", gbc, mh_d[l:l + 1, h * DH:(h + 1) * DH].broadcast_to([128, DH]), [RA], ["gbc"], "gbc")
            MSET(C32, 0.0, ["C32"])
            MSET(n32, 0.0, ["n32"])
            MSET(Cb, 0.0, ["Cb"], eng="dve")
            MSET(vtok[:, :, DH:DH + 2], 1.0, ["vtok1"])
            for tt in range(4):
                t0 = tt * 512
                for dc in range(2):
                    pq, pqk = PS()
                    for k in range(8):
                        MM(pq, wq[:, k, dc * 128:(dc + 1) * 128], xb[:, k, t0:t0 + 512], k == 0, k == 7,
                           [wqk, ("xb", k, tt), RA], [pqk])
                    CP(qT[:, dc, :], pq, [pqk], [("qT", dc)])
                    pk_, pkk = PS()
                    for k in range(8):
                        MM(pk_, wk[:, k, dc * 128:(dc + 1) * 128], xb[:, k, t0:t0 + 512], k == 0, k == 7,
                           [wkk, ("xb", k, tt), RA], [pkk])
                    ACT(kT[:, dc, :], pk_, AF.Identity, [pkk], [("kT", dc)], scale=1.0 / 16.0)
                for ci in range(4):
                    c0 = t0 + ci * 128
                    pa, pak = PS()
                    for k in range(8):
                        MM(pa[:, 0:DH], xb[:, k, c0:c0 + 128], wk[:, k, :], k == 0, k == 7,
                           [wkk, ("xb", k, tt), RA], [pak])
                    ACT(ktok[:, ci, :], pa[:, 0:DH], AF.Identity, [pak], [("ktok", ci)], scale=1.0 / 16.0)
                    pb_, pbk = PS()
                    for k in range(8):
                        MM(pb_[:, 0:DH], xb[:, k, c0:c0 + 128], wv[:, k, :], k == 0, k == 7,
                           [wvk, ("xb", k, tt), RA], [pbk])
                    CP(vtok[:, ci, 0:DH], pb_[:, 0:DH], [pbk], [("vtok", ci)], eng="dve")
                    pc_, pck = PS()
                    for k in range(8):
                        MM(pc_[:, 0:DH], xb[:, k, c0:c0 + 128], wo[:, k, :], k == 0, k == 7,
                           [wok, ("xb", k, tt), RA], [pck])
                    ACT(gsig[:, ci, :], pc_[:, 0:DH], AF.Sigmoid, [pck], [("gsig", ci)])
                    TT(gsig[:, ci, :], gsig[:, ci, :], gbc, ALU.mult, [("gsig", ci), "gbc"], [("gsig", ci)])
                XB, XK = psum[4], ("ps", 4)
                UB, UK = psum[5], ("ps", 5)
                TB, TK = psum[6], ("ps", 6)
                SB_, SK_ = psum[7], ("ps", 7)

                def front_a(ci):
                    c = tt * 4 + ci
                    cs = slice(ci * 128, (ci + 1) * 128)
                    pB, pBk = psum[2 + c % 2], ("ps", 2 + c % 2)
                    PTS(diag, ident32, g["nM"][:, c, h:h + 1], [K("nM"), "cst"], ["diag"])
                    PTS(Kw, ktok[:, ci, :], g["wS"][:, c, h:h + 1], [("ktok", ci), K("wS")], ["Kw"])
                    for dc in range(2):
                        MM(SB_[:, 0:128], kT[:, dc, cs], qT[:, dc, cs], dc == 0, dc == 1, [("kT", dc), ("qT", dc)], [SK_])
                    MM(XB[:, 128:256], ones32, diag, True, False, ["diag", "ones32"], [XK])
                    MM(XB[:, 128:256], identb, mstb, False, True, ["identb", "mstb"], [XK])
                    for kc in range(2):
                        MM(XB[:, 256 + kc:257 + kc], Kw[:, kc * 128:(kc + 1) * 128], vtok[:, ci, DH:DH + 1], True, True,
                           ["Kw", "vtok1"], [XK])
                    for kc in range(2):
                        MM(UB[:, kc * DH:(kc + 1) * DH], Kw[:, kc * 128:(kc + 1) * 128], vtok[:, ci, 0:DH], True, True,
                           ["Kw", ("vtok", ci)], [UK])

                def front_a2(ci):
                    c = tt * 4 + ci
                    cs = slice(ci * 128, (ci + 1) * 128)
                    pB, pBk = psum[2 + c % 2], ("ps", 2 + c % 2)
                    for kc in range(2):
                        MM(pB[:, 0:DH + 1], qT[:, kc, cs], Cb[:, kc, 0:DH + 1], kc == 0, kc == 1, [("qT", kc), "Cb"], [pBk])
                    STT(C32.rearrange("p a b -> p (a b)"), C32.rearrange("p a b -> p (a b)"), g["dec"][:, c, h:h + 1],
                        UB, ALU.mult, ALU.add, ["C32", UK, K("dec")], ["C32"])
                    STT(n32, n32, g["dec"][:, c, h:h + 1], XB[:, 256:258], ALU.mult, ALU.add, ["n32", XK, K("dec")], ["n32"])
                    PCP(Cb[:, :, 0:DH], C32, ["C32"], ["Cb"])
                    PCP(Cb[:, :, DH:DH + 1], n32.unsqueeze(2), ["n32", "Cb"], ["Cb"])
                    ACT(Et, XB[:, 128:256], AF.Exp, [XK, K("a")], ["Et"], bias=g["a"][:, c, h:h + 1])

                def front_b(ci):
                    c = tt * 4 + ci
                    pA, pAk = psum[c % 2], ("ps", c % 2)
                    TT(PTt, SB_[:, 0:128], Et, ALU.mult, [SK_, "Et"], ["PTt"])
                    MM(pA[:, 0:DH + 1], PTt, vtok[:, ci, 0:DH + 1], True, True, ["PTt", ("vtok", ci), "vtok1"], [pAk])

                def tail_a(ci):
                    c = tt * 4 + ci
                    pA, pAk = psum[c % 2], ("ps", c % 2)
                    pB, pBk = psum[2 + c % 2], ("ps", 2 + c % 2)
                    ACT(numB, pB[:, 0:DH + 1], AF.Identity, [pBk, K("sI")], ["numB"], scale=g["sI"][:, c, h:h + 1])
                    TT(num, pA[:, 0:DH + 1], numB, ALU.add, [pAk, "numB"], ["num"])
                    STT(mv[:, 0:1], num[:, DH:DH + 1], -1.0, num[:, DH:DH + 1], ALU.mult, ALU.max, ["num"], ["rdn"])
                    TS(mv[:, 0:1], mv[:, 0:1], g["fl"][:, c, h:h + 1], None, ALU.max, None, ["rdn", K("fl")], ["rdn"])
                    P.op("dve", lambda e: e.bn_stats(out=st6, in_=num[:, 0:DH]), r=["num"], w=["st6"])
                    P.op("dve", lambda e: e.bn_aggr(out=sc4[:, 0:2], in_=st6), r=["st6"], w=["sc4"])
                    TT(sc4[:, 2:3], mv[:, 0:1], mv[:, 0:1], ALU.mult, ["rdn", "sc4"], ["sc4b"])
                    STT(sc4[:, 2:3], sc4[:, 2:3], EPS, sc4[:, 1:2], ALU.mult, ALU.add, ["sc4b", "sc4"], ["sc4b"])
                    ACT(sc4[:, 2:3], sc4[:, 2:3], AF.Ln, ["sc4b"], ["sc4b"])
                    ACT(sc4[:, 2:3], sc4[:, 2:3], AF.Exp, ["sc4b"], ["sc4b"], scale=-0.5)

                def tail_b(ci):
                    c = tt * 4 + ci
                    TS(hn, num[:, 0:DH], sc4[:, 0:1], sc4[:, 2:3], ALU.subtract, ALU.mult, ["num", "sc4", "sc4b"], ["hn"])
                    TT(hg, hn, gsig[:, ci, :], ALU.mult, ["hn", ("gsig", ci)], ["hg"])
                    ptb = TB.bitcast(BF16)
                    for dc in range(2):
                        TR(ptb[:, dc * 128:(dc + 1) * 128], hg[:, dc * 128:(dc + 1) * 128], identb, ["hg", "identb"], [TK])
                    CP(hmT[:, 2 * h:2 * h + 2, c * 128:(c + 1) * 128],
                       ptb[:, 0:256].rearrange("p (a b) -> p a b", b=128), [TK, RA],
                       [("hmT", 2 * h, tt), ("hmT", 2 * h + 1, tt)])

                front_a(0)
                front_a2(0)
                front_b(0)
                for ci in range(1, 4):
                    front_a(ci)
                    tail_a(ci - 1)
                    front_a2(ci)
                    front_b(ci)
                    tail_b(ci - 1)
                tail_a(3)
                tail_b(3)
            DMA("sp", Cp_o[l, h].rearrange("(kc p) v -> p kc v", p=128), C32, ["C32"], [], "Cp_o")
            DMA("sp", np_o[l, h], n32, ["n32"], [], "np_o")

            for (dst, wv_, wk_, sc_) in ((q_s, wq, wqk, 1.0), (k_s, wk, wkk, 1.0 / 16.0), (v_s, wv, wvk, 1.0)):
                pp_, ppk = PS()
                for k in range(8):
                    MM(pp_[0:NS_, 0:DH], xb[:, k, SEQ:NT], wv_[:, k, :], k == 0, k == 7, [wk_, ("xb", k, 4), RA], [ppk])
                ACT(dst, pp_[0:NS_, 0:DH], AF.Identity, [ppk], [("s", id(dst))], scale=sc_)
            pp_, ppk = PS()
            for k in range(8):
                MM(pp_[0:NS_, 0:DH], xb[:, k, SEQ:NT], wo[:, k, :], k == 0, k == 7, [wok, ("xb", k, 4), RA], [ppk])
            ACT(g_s, pp_[0:NS_, 0:DH], AF.Sigmoid, [ppk], ["g_s"])
            TT(g_s, g_s, gbc[0:NS_], ALU.mult, ["g_s", "gbc"], ["g_s"])
            qk_, kk_, vk_ = ("s", id(q_s)), ("s", id(k_s)), ("s", id(v_s))
            for dc in range(2):
                pq, pqk = PS()
                for k in range(8):
                    MM(pq[:, 0:NS_], wq[:, k, dc * 128:(dc + 1) * 128], xb[:, k, SEQ:NT], k == 0, k == 7,
                       [wqk, ("xb", k, 4), RA], [pqk])
                CP(qTs[:, dc, :], pq[:, 0:NS_], [pqk], [("qTs", dc)], eng="dve")
                TT(Qm[:, dc], qTs[:, dc, :].unsqueeze(2).broadcast_to([128, NS_, NS_]), I16bc, ALU.mult,
                   [("qTs", dc), "cst"], [("Qm", dc)])
            DMA("sp", nold_s, ns_in[l][:, h * DH:(h + 1) * DH], [RA], ["nold_s"], "nold_s")
            TT(t_s, q_s, k_s, ALU.mult, [qk_, kk_], ["t_s"])
            P.op("dve", lambda e, h=h: e.tensor_reduce(out=s["qk"][:, h:h + 1], in_=t_s, axis=AX.X, op=ALU.add),
                 r=["t_s"], w=[SK("qk")])
            TT(t_s, q_s, nold_s, ALU.mult, [qk_, "nold_s", SK("qk")], ["t_s"])
            P.op("dve", lambda e, h=h: e.tensor_reduce(out=s["qn"][:, h:h + 1], in_=t_s, axis=AX.X, op=ALU.add),
                 r=["t_s"], w=[SK("qn")])
            hh = slice(h, h + 1)
            TT(s["s"][:, hh], s["qk"][:, hh], s["wg"][:, hh], ALU.mult, [SK("qk"), SK("wg")], [SK("s")])
            TT(s["den"][:, hh], s["dec"][:, hh], s["qn"][:, hh], ALU.mult, [SK("dec"), SK("qn")], [SK("den")])
            TT(s["den"][:, hh], s["den"][:, hh], s["s"][:, hh], ALU.add, [SK("den"), SK("s")], [SK("den")])
            STT(s["rdn"][:, hh], s["den"][:, hh], -1.0, s["den"][:, hh], ALU.mult, ALU.max, [SK("den")], [SK("rdn")])
            TS(s["rdn"][:, hh], s["rdn"][:, hh], s["fl"][:, hh], None, ALU.max, None, [SK("rdn"), SK("fl")], [SK("rdn")])
            P.op("dve", lambda e, hh=hh: e.reciprocal(out=s["rdn"][:, hh], in_=s["rdn"][:, hh]), r=[SK("rdn")], w=[SK("rdn")])
            TS(kw_s, k_s, s["wg"][:, hh], None, ALU.mult, None, [kk_, SK("wg")], ["kw_s"])
            STT(nnew_s, nold_s, s["dec"][:, hh], kw_s, ALU.mult, ALU.add, ["nold_s", "kw_s", SK("dec"), "t_s"], ["nnew_s"])
            DMA("sp", ns_o[l][:, h * DH:(h + 1) * DH], nnew_s, ["nnew_s"], [], "ns_o")
            pqc, pqck = PSL()
            for b in range(NS_):
                ct = Cst[b % 3]
                ck = ("Cst", b % 3)
                if b == 0:
                    for b2 in range(2):
                        DMA("sp", Cst[b2], Cs_in[l, b2, h].rearrange("(kc p) v -> p kc v", p=128), [RA],
                            [("Cst", b2)], ("Cst_i", b2))
                if b + 2 < NS_:
                    b2 = b + 2
                    DMA("sp", Cst[b2 % 3], Cs_in[l, b2, h].rearrange("(kc p) v -> p kc v", p=128), [RA],
                        [("Cst", b2 % 3)], ("Cst_i", b2 % 3))
                for kc in range(2):
                    MM(pqc[0:NS_, 0:DH], Qm[:, kc, b, :], ct[:, kc, :], b == 0 and kc == 0, b == NS_ - 1 and kc == 1,
                       [("Qm", kc), ck], [pqck])
                TS(vm_s, v_s, ident32[0:NS_, b:b + 1], None, ALU.mult, None, [vk_, "cst"], ["vm_s"])
                pU, pUk = PS()
                for kc in range(2):
                    MM(pU[:, kc * DH:(kc + 1) * DH], kw_s[:, kc * 128:(kc + 1) * 128], vm_s, True, True,
                       ["kw_s", "vm_s"], [pUk])
                STT(ct.rearrange("p a b -> p (a b)"), ct.rearrange("p a b -> p (a b)"), decs_bc[:, h, b:b + 1], pU,
                    ALU.mult, ALU.add, [ck, pUk, "decs_bc"], [ck])
                DMA("act", Cs_o[l, b, h].rearrange("(kc p) v -> p kc v", p=128), ct, [ck], [], ("Cst_o", b % 3))
            TS(num_s, v_s, s["s"][:, hh], None, ALU.mult, None, [vk_, SK("s")], ["num_s"])
            STT(num_s, pqc[0:NS_, 0:DH], s["dec"][:, hh], num_s, ALU.mult, ALU.add, [pqck, "num_s", SK("dec")], ["num_s"])
            TS(num_s, num_s, s["rdn"][:, hh], None, ALU.mult, None, ["num_s", SK("rdn")], ["num_s"])
            P.op("dve", lambda e: e.bn_stats(out=st6s, in_=num_s), r=["num_s"], w=["st6s"])
            P.op("dve", lambda e: e.bn_aggr(out=mvs, in_=st6s), r=["st6s"], w=["mvs"])
            ACT(s["t"][:, 4:5], mvs[:, 1:2], AF.Ln, ["mvs"], [SK("t2")], bias=EPS)
            ACT(s["t"][:, 4:5], s["t"][:, 4:5], AF.Exp, [SK("t2")], [SK("t2")], scale=-0.5)
            TS(num_s, num_s, mvs[:, 0:1], s["t"][:, 4:5], ALU.subtract, ALU.mult, ["num_s", "mvs", SK("t2")], ["num_s"])
            TT(hg_s, num_s, g_s, ALU.mult, ["num_s", "g_s"], ["hg_s"])
            ptp, ptpk = PS()
            ptb = ptp.bitcast(BF16)
            for dc in range(2):
                TR(ptb[:, dc * NS_:(dc + 1) * NS_], hg_s[:, dc * 128:(dc + 1) * 128], identb[0:NS_, 0:NS_],
                   ["hg_s", "identb"], [ptpk])
            CP(hmT[:, 2 * h:2 * h + 2, SEQ:NT], ptb[:, 0:2 * NS_].rearrange("p (a b) -> p a b", b=NS_), [ptpk, RA],
               [("hmT", 2 * h, 4), ("hmT", 2 * h + 1, 4)])

        cv = Carver()
        hmT = cv.take([128, 8, NT], BF16)
        u_ = cv.take([128, 4, NT], BF16)
        cur = cv.take([128, 2 + NT], F32)
        cvt = cv.take([128, NT], F32)
        t512 = [cv.take([128, 512], F32) for _ in range(1)]
        csT = cv.take([128, 2, 4, NS_], F32)
        scp = cv.take([128, 4, 2], F32)
        scs = cv.take([128, 4, NS_], F32)
        RB = ("RB", l)
        barrier(RB, bscr)
        DMA("sp", csT[:, 0], cs_in[l, 0], [RB], [("csT", 0)], ("csT", 0))
        DMA("sp", csT[:, 1], cs_in[l, 1], [RB], [("csT", 1)], ("csT", 1))
        DMA("sp", sc_s[l, 0], cs_in[l, 1], [], [], "sc_s0")
        MSET(cur[:, 0:2], 0.0, ["cur0"])
        for jp in range(2):
            wB, wBk = WLOAD(w_in[l][:, O_B + jp * 256:O_B + (jp + 1) * 256], 8, 256)
            wC, wCk = WLOAD(w_in[l][:, O_C + jp * 256:O_C + (jp + 1) * 256], 8, 256)
            wX, wXk = WLOAD(w_in[l][:, O_XV + jp * 256:O_XV + (jp + 1) * 256], 8, 256)
            for jj in range(2):
                j = jp * 2 + jj
                cl = slice(jj * 128, (jj + 1) * 128)
                for t, (t0, tn) in enumerate(TILES):
                    pc_, pck = PS()
                    for k in range(8):
                        MM(pc_[:, 0:tn], wC[:, k, cl], xb[:, k, t0:t0 + tn], k == 0, k == 7, [wCk, ("xb", k, t), RB], [pck])
                    px_, pxk = PS()
                    for k in range(8):
                        MM(px_[:, 0:tn], wX[:, k, cl], xb[:, k, t0:t0 + tn], k == 0, k == 7, [wXk, ("xb", k, t), RB], [pxk])
                    ACT(t512[0][:, 0:tn], pc_[:, 0:tn], AF.Copy, [pck], ["t512_0"])
                    TT(cur[:, 2 + t0:2 + t0 + tn], t512[0][:, 0:tn], px_[:, 0:tn], ALU.mult, ["t512_0", pxk, "cur0", RB],
                       [("cur", t)])
                cm_ = lambda jtap: pp[:, PP_CM + (l * 3 + jtap) * 4 + j:PP_CM + (l * 3 + jtap) * 4 + j + 1]
                curk = [("cur", t) for t in range(5)]
                ACT(cvt[:, 0:SEQ], cur[:, 0:SEQ], AF.Identity, curk + ["cur0", "pp"], ["cvt"], scale=cm_(0))
                STT(cvt[:, 0:SEQ], cur[:, 1:SEQ + 1], cm_(1), cvt[:, 0:SEQ], ALU.mult, ALU.add, curk + ["cvt"], ["cvt"])
                STT(cvt[:, 0:SEQ], cur[:, 2:SEQ + 2], cm_(2), cvt[:, 0:SEQ], ALU.mult, ALU.add, curk + ["cvt"], ["cvt"])
                TS(cvt[:, SEQ:NT], csT[:, 0, j, :], cm_(0), None, ALU.mult, None, [("csT", 0), "cvt"], ["cvts"])
                STT(cvt[:, SEQ:NT], csT[:, 1, j, :], cm_(1), cvt[:, SEQ:NT], ALU.mult, ALU.add, [("csT", 1), "cvts"], ["cvts"])
                STT(cvt[:, SEQ:NT], cur[:, 2 + SEQ:2 + NT], cm_(2), cvt[:, SEQ:NT], ALU.mult, ALU.add, curk + ["cvts"], ["cvts"])
                CP(scp[:, j, :], cur[:, SEQ:SEQ + 2], curk, [("scp", j)], eng="act")
                CP(scs[:, j, :], cur[:, 2 + SEQ:2 + NT], curk, [("scs", j)], eng="act")
                for t, (t0, tn) in enumerate(TILES):
                    pb_, pbk = PS()
                    for k in range(8):
                        MM(pb_[:, 0:tn], wB[:, k, cl], xb[:, k, t0:t0 + tn], k == 0, k == 7, [wBk, ("xb", k, t), RB], [pbk])
                    TT(u_[:, j, t0:t0 + tn], pb_[:, 0:tn], cvt[:, t0:t0 + tn], ALU.mult, [pbk, "cvt", "cvts", RB], [("u", j, t)])
        DMA("sp", sc_p[l], scp, [("scp", j) for j in range(4)], [], "sc_p")
        DMA("sp", sc_s[l, 1], scs, [("scs", j) for j in range(4)], [], "sc_s1")
        cv = Carver()
        hmT = cv.take([128, 8, NT], BF16)
        u_ = cv.take([128, 4, NT], BF16)
        mg = cv.take([128, 2, NT], BF16)
        t512 = [cv.take([128, 512], F32) for _ in range(3)]
        lnsc = [cv.take([128, 512], F32) for _ in range(2)]
        RB = ("RB2", l)
        barrier(RB, bscr)
        for gq in range(4):
            wa, wak = WLOAD(w_pa[l][:, gq * 256:(gq + 1) * 256], 4, 256)
            wb_, wbk = WLOAD(w_pb[l][:, gq * 256:(gq + 1) * 256], 8, 256)
            wga, wgak = WLOAD(w_in[l][:, O_GA + gq * 256:O_GA + (gq + 1) * 256], 8, 256)
            wgb, wgbk = WLOAD(w_in[l][:, O_GB + gq * 256:O_GB + (gq + 1) * 256], 8, 256)
            for oc in range(2):
                cl = slice(oc * 128, (oc + 1) * 128)
                for t, (t0, tn) in enumerate(TILES):
                    p1, p1k = PS()
                    for k in range(8):
                        MM(p1[:, 0:tn], wga[:, k, cl], xb[:, k, t0:t0 + tn], k == 0, k == 7, [wgak, ("xb", k, t), RB], [p1k])
                    ACT(t512[0][:, 0:tn], p1[:, 0:tn], AF.Sigmoid, [p1k], ["t512_0"])
                    p2, p2k = PS()
                    for k in range(4):
                        MM(p2[:, 0:tn], wa[:, k, cl], u_[:, k, t0:t0 + tn], k == 0, k == 3, [wak, ("u", k, t), RB], [p2k])
                    TT(t512[1][:, 0:tn], t512[0][:, 0:tn], p2[:, 0:tn], ALU.mult, ["t512_0", p2k], ["t512_1"])
                    p3, p3k = PS()
                    for k in range(8):
                        MM(p3[:, 0:tn], wgb[:, k, cl], xb[:, k, t0:t0 + tn], k == 0, k == 7, [wgbk, ("xb", k, t), RB], [p3k])
                    ACT(t512[2][:, 0:tn], p3[:, 0:tn], AF.Sigmoid, [p3k], ["t512_2"])
                    p4, p4k = PS()
                    for k in range(8):
                        MM(p4[:, 0:tn], wb_[:, k, cl], hmT[:, k, t0:t0 + tn], k == 0, k == 7, [wbk, ("hmT", k, t), RB], [p4k])
                    TT(t512[2][:, 0:tn], t512[2][:, 0:tn], p4[:, 0:tn], ALU.mult, ["t512_2", p4k], ["t512_2"])
                    TT(mg[:, oc, t0:t0 + tn], t512[1][:, 0:tn], t512[2][:, 0:tn], ALU.add, ["t512_1", "t512_2", RB],
                       [("mg", oc, t)])
            wm, wmk = WLOAD(w_mx[l][gq * 256:(gq + 1) * 256, :], 2, 1024)
            for oc in range(8):
                for t, (t0, tn) in enumerate(TILES):
                    p5, p5k = PS()
                    for k in range(2):
                        MM(p5[:, 0:tn], wm[:, k, oc * 128:(oc + 1) * 128], mg[:, k, t0:t0 + tn], k == 0, k == 1,
                           [wmk, ("mg", k, t), RB], [p5k])
                    xk = ("x32", oc, t)
                    if gq == 0:
                        STT(x32[:, oc, t0:t0 + tn], x32[:, oc, t0:t0 + tn], ALPHA, p5[:, 0:tn], ALU.mult, ALU.add,
                            [xk, p5k], [xk])
                    else:
                        TT(x32[:, oc, t0:t0 + tn], x32[:, oc, t0:t0 + tn], p5[:, 0:tn], ALU.add, [xk, p5k], [xk])

        def layer_norm(goff, boff, RK, t512, lnsc):
            for t, (t0, tn) in enumerate(TILES):
                ps_s, pssk = PS()
                ps_q, psqk = PS()
                for oc in range(8):
                    xk = ("x32", oc, t)
                    bsel = oc % 2
                    yb_ = lnsc[bsel].bitcast(BF16)[:, 0:tn]
                    yq_ = lnsc[bsel].bitcast(BF16)[:, 512:512 + tn]
                    CP(yb_, x32[:, oc, t0:t0 + tn], [xk, RK], [("lnyb", bsel)], eng="dve")
                    ACT(yq_, x32[:, oc, t0:t0 + tn], AF.Square, [xk, RK], [("lnyq", bsel)])
                    MM(ps_s[:, 0:tn], onesb, yb_, oc == 0, oc == 7, [("lnyb", bsel), "onesb"], [pssk])
                    MM(ps_q[:, 0:tn], onesb, yq_, oc == 0, oc == 7, [("lnyq", bsel), "onesb"], [psqk])
                mean = t512[1][:, 0:tn]
                rstd = t512[2][:, 0:tn]
                ACT(mean, ps_s[:, 0:tn], AF.Identity, [pssk], ["lnmean"], scale=1.0 / D)
                TT(rstd, mean, mean, ALU.mult, ["lnmean"], ["lnrstd"])
                STT(rstd, ps_q[:, 0:tn], 1.0 / D, rstd, ALU.mult, ALU.subtract, [psqk, "lnrstd"], ["lnrstd"])
                ACT(rstd, rstd, AF.Ln, ["lnrstd"], ["lnrstd"], bias=EPS)
                ACT(rstd, rstd, AF.Exp, ["lnrstd"], ["lnrstd"], scale=-0.5)
                tmp = t512[0][:, 0:tn]
                for oc in range(8):
                    xk = ("x32", oc, t)
                    TT(tmp, x32[:, oc, t0:t0 + tn], mean, ALU.subtract, [xk, "lnmean"], ["lntmp"])
                    TT(tmp, tmp, rstd, ALU.mult, ["lntmp", "lnrstd"], ["lntmp"])
                    ACT(x32[:, oc, t0:t0 + tn], tmp, AF.Identity, ["lntmp", "pp"], [xk],
                        bias=ppc(boff, l, oc), scale=ppc(goff, l, oc))
                    CP(xb[:, oc, t0:t0 + tn], x32[:, oc, t0:t0 + tn], [xk, RK], [("xb", oc, t)], eng="dve")

        layer_norm(PP_LN, PP_LN + 32, RB, t512, lnsc)

        cv = Carver()
        hbuf = [cv.take([128, 4, NT], BF16) for _ in range(2)]
        upb = [cv.take([128, 2 + NT], F32) for _ in range(2)]
        cvb = [cv.take([128, NT], F32) for _ in range(2)]
        t512c = [cv.take([128, 512], F32) for _ in range(3)]
        lnscc = [cv.take([128, 512], F32) for _ in range(2)]
        cfT = cv.take([128, 2, 2, NS_], F32)
        ffp = cv.take([128, 44, 2], F32)
        ffs = cv.take([128, 44, NS_], F32)
        RC = ("RC", l)
        barrier(RC, bscr)
        DMA("sp", ff_s[l, 0], cf_in[l, 1], [], [], "ff_s0")
        for gu in range(2):
            MSET(upb[gu][:, 0:2], 0.0, [("upb0", gu)])
        groups = [list(range(a, min(a + 4, 22))) for a in range(0, 22, 4)]
        for gi, grp in enumerate(groups):
            hb = hbuf[gi % 2]
            hk = lambda jj, t: ("h", gi % 2, jj, t)
            for pi in range(0, len(grp), 2):
                prs = grp[pi:pi + 2]
                j0 = prs[0]
                ncol = 128 * len(prs)
                wg_, wgk_ = WLOAD(w_up[l][:, j0 * 128:j0 * 128 + ncol], 8, ncol)
                wu_, wuk_ = WLOAD(w_up[l][:, DFF + j0 * 128:DFF + j0 * 128 + ncol], 8, ncol)
                for pj, j in enumerate(prs):
                    jj = j - grp[0]
                    cl = slice(pj * 128, (pj + 1) * 128)
                    for gu, (ww, wwk) in enumerate(((wg_, wgk_), (wu_, wuk_))):
                        chunk = j + gu * 22
                        ub = upb[gu]
                        cb_ = cvb[gu]
                        DMA("sp", cfT[:, :, gu, :], cf_in[l, :, :, chunk, :].rearrange("r p b -> p r b"), [RC],
                            [("cfT", gu)], ("cfT", gu))
                        for t, (t0, tn) in enumerate(TILES):
                            pu, puk = PS()
                            for k in range(8):
                                MM(pu[:, 0:tn], ww[:, k, cl], xb[:, k, t0:t0 + tn], k == 0, k == 7,
                                   [wwk, ("xb", k, t), RC], [puk])
                            CP(ub[:, 2 + t0:2 + t0 + tn], pu[:, 0:tn], [puk, ("upb0", gu), RC], [("upb", gu, t)])
                        fc = lambda jtap: pp[:, PP_FC + (l * 3 + jtap) * 44 + chunk:PP_FC + (l * 3 + jtap) * 44 + chunk + 1]
                        ubk = [("upb", gu, t) for t in range(5)] + [("upb0", gu)]
                        cvk = ("cvb", gu)
                        ACT(cb_[:, 0:SEQ], ub[:, 0:SEQ], AF.Identity, ubk + ["pp", RC], [cvk], scale=fc(0))
                        STT(cb_[:, 0:SEQ], ub[:, 1:SEQ + 1], fc(1), cb_[:, 0:SEQ], ALU.mult, ALU.add, ubk + [cvk], [cvk])
                        STT(cb_[:, 0:SEQ], ub[:, 2:SEQ + 2], fc(2), cb_[:, 0:SEQ], ALU.mult, ALU.add, ubk + [cvk], [cvk])
                        cvks = ("cvbs", gu)
                        TS(cb_[:, SEQ:NT], cfT[:, 0, gu, :], fc(0), None, ALU.mult, None, [("cfT", gu), cvk], [cvks])
                        STT(cb_[:, SEQ:NT], cfT[:, 1, gu, :], fc(1), cb_[:, SEQ:NT], ALU.mult, ALU.add, [("cfT", gu), cvks], [cvks])
                        STT(cb_[:, SEQ:NT], ub[:, 2 + SEQ:2 + NT], fc(2), cb_[:, SEQ:NT], ALU.mult, ALU.add, ubk + [cvks], [cvks])
                        CP(ffp[:, chunk, :], ub[:, SEQ:SEQ + 2], ubk, [("ffp", chunk)], eng="act")
                        CP(ffs[:, chunk, :], ub[:, 2 + SEQ:2 + NT], ubk, [("ffs", chunk)], eng="act")
                    ACT(cvb[0], cvb[0], AF.Silu, [("cvb", 0), ("cvbs", 0)], [("cvb", 0), ("cvbs", 0)])
                    for t, (t0, tn) in enumerate(TILES):
                        TT(hb[:, jj, t0:t0 + tn], cvb[0][:, t0:t0 + tn], cvb[1][:, t0:t0 + tn], ALU.mult,
                           [("cvb", 0), ("cvbs", 0), ("cvb", 1), ("cvbs", 1), RC], [hk(jj, t)])
            kcn = len(grp)
            for half in range(2):
                wd, wdk = WLOAD(w_dn[l][grp[0] * 128:(grp[0] + kcn) * 128, half * 512:(half + 1) * 512], kcn, 512)
                for o4 in range(4):
                    oc = half * 4 + o4
                    for t, (t0, tn) in enumerate(TILES):
                        p6, p6k = PS()
                        for k in range(kcn):
                            MM(p6[:, 0:tn], wd[:, k, o4 * 128:(o4 + 1) * 128], hb[:, k, t0:t0 + tn], k == 0, k == kcn - 1,
                               [wdk, hk(k, t), RC], [p6k])
                        xk = ("x32", oc, t)
                        if gi == 0:
                            STT(x32[:, oc, t0:t0 + tn], x32[:, oc, t0:t0 + tn], ALPHA, p6[:, 0:tn], ALU.mult, ALU.add,
                                [xk, p6k], [xk])
                        else:
                            TT(x32[:, oc, t0:t0 + tn], x32[:, oc, t0:t0 + tn], p6[:, 0:tn], ALU.add, [xk, p6k], [xk])
        DMA("sp", ff_p[l], ffp, [("ffp", c) for c in range(44)], [], "ff_p")
        DMA("sp", ff_s[l, 1], ffs, [("ffs", c) for c in range(44)], [], "ff_s1")
        layer_norm(PP_LN + 64, PP_LN + 96, RC, t512c, lnscc)

    yTv = yT.rearrange("(c p) n -> p c n", p=128)
    for c in range(8):
        DMA("sp", yTv[:, c, :], x32[:, c, :], [("x32", c, t) for t in range(5)], [], ("yT", c))

    P.resolve()
    with nc.Block() as block:
        P.emit(block)
    return nc


_NC_CACHE = {}


def _consts():
    c = np.zeros((128, C_N), np.float32)
    idx = np.arange(128)
    c[:, C_ID:C_ID + 128] = np.eye(128, dtype=np.float32)
    c[:, C_U:C_U + 128] = (idx[:, None] <= idx[None, :]).astype(np.float32)
    c[:, C_MTS:C_MTS + 128] = np.where(idx[None, :] <= idx[:, None], 0.0, NEG)
    c[127, C_S127:C_S127 + 128] = 1.0
    c[:, C_MST:C_MST + 128] = np.where(idx[:, None] <= idx[None, :], 0.0, NEG)
    c[:, C_I16:C_I16 + 256] = np.eye(16, dtype=np.float32).reshape(1, 256)
    return c


def kernel(x_prompt, x_sample, cache_sconv, state_mlstm_C, state_mlstm_n, state_mlstm_m, cache_ffn_conv,
           w_in, b_igate, b_fgate, w_conv_mix, mhln_g, w_proj_a, w_proj_b, w_mix_out, ln1_g, ln1_b,
           w_ffn_up, w_ffn_conv, w_ffn_down, ln2_g, ln2_b):
    f = lambda a: np.ascontiguousarray(np.asarray(a, dtype=np.float32))
    x_prompt, x_sample = f(x_prompt), f(x_sample)
    cache_sconv, cache_ffn_conv = f(cache_sconv), f(cache_ffn_conv)
    state_mlstm_C, state_mlstm_n, state_mlstm_m = f(state_mlstm_C), f(state_mlstm_n), f(state_mlstm_m)
    NCORE = 8
    pp = np.zeros((128, PP_N), np.float32)
    for i, a in enumerate((ln1_g, ln1_b, ln2_g, ln2_b)):
        pp[:, PP_LN + i * 32:PP_LN + (i + 1) * 32] = f(a).reshape(L, 8, 128).transpose(2, 0, 1).reshape(128, 32)
    pp[:, PP_CM:PP_CM + 48] = f(w_conv_mix).reshape(L, 3, 4, 128).transpose(3, 0, 1, 2).reshape(128, 48)
    pp[:, PP_FC:PP_FC + 528] = f(w_ffn_conv).reshape(L, 3, 44, 128).transpose(3, 0, 1, 2).reshape(128, 528)
    gb = np.concatenate([f(b_igate), f(b_fgate)], axis=1).reshape(1, L * 8)
    pp[:, PP_GB:PP_GB + 32] = np.broadcast_to(gb, (128, 32))
    cst = _consts()
    shared = {"pp": pp, "cst": cst, "mhln_g": f(mhln_g), "w_in": f(w_in), "w_proj_a": f(w_proj_a),
              "w_proj_b": f(w_proj_b), "w_mix_out": f(w_mix_out), "w_ffn_up": f(w_ffn_up), "w_ffn_down": f(w_ffn_down)}
    in_maps = []
    for c in range(NCORE):
        sl = slice(c * NS_, (c + 1) * NS_)
        xT = np.ascontiguousarray(np.concatenate([x_prompt[c].T, x_sample[sl, 0, :].T], axis=1))
        cs = np.ascontiguousarray(cache_sconv[:, sl].reshape(L, NS_, 2, 4, 128).transpose(0, 2, 4, 3, 1))
        cf = np.ascontiguousarray(cache_ffn_conv[:, sl].reshape(L, NS_, 2, 44, 128).transpose(0, 2, 4, 3, 1))
        m = dict(shared)
        m.update({"xT": xT, "cs_in": cs, "cf_in": cf,
                  "Cs_in": np.ascontiguousarray(state_mlstm_C[:, sl]),
                  "ns_in": np.ascontiguousarray(state_mlstm_n[:, sl].reshape(L, NS_, H * DH)),
                  "ms_in": np.ascontiguousarray(state_mlstm_m[:, sl])})
        in_maps.append(m)
    if "nc" not in _NC_CACHE:
        _NC_CACHE["nc"] = build_nc()
    nc = _NC_CACHE["nc"]
    res = run_bass_kernel_spmd(nc, in_maps, core_ids=list(range(NCORE)))
    R = res.results
    y_p = np.stack([R[c]["yT"][:, :SEQ].T for c in range(NCORE)])
    y_s = np.concatenate([R[c]["yT"][:, SEQ:].T for c in range(NCORE)])[:, None, :]
    def conv_p(key, C):
        return np.stack([R[c][key].transpose(0, 3, 2, 1).reshape(L, 2, C * 128) for c in range(NCORE)], axis=1)

    def conv_s(key, C):
        return np.concatenate([R[c][key].transpose(0, 4, 1, 3, 2).reshape(L, NS_, 2, C * 128) for c in range(NCORE)], axis=1)

    sp_ = conv_p("sc_p", 4)
    ss_ = conv_s("sc_s", 4)
    fp_ = conv_p("ff_p", 44)
    fs_ = conv_s("ff_s", 44)
    Cp = np.stack([R[c]["Cp_o"] for c in range(NCORE)], axis=1)
    Cs = np.concatenate([R[c]["Cs_o"] for c in range(NCORE)], axis=1)
    np_ = np.stack([R[c]["np_o"].transpose(0, 1, 3, 2).reshape(L, H, DH) for c in range(NCORE)], axis=1)
    ns_ = np.concatenate([R[c]["ns_o"].reshape(L, NS_, H, DH) for c in range(NCORE)], axis=1)
    mp_ = np.stack([R[c]["mp_o"] for c in range(NCORE)], axis=1)
    ms_ = np.concatenate([R[c]["ms_o"] for c in range(NCORE)], axis=1)
    out = (y_p, y_s, sp_, ss_, Cp, Cs, np_, ns_, mp_, ms_, fp_, fs_)
    return tuple(np.ascontiguousarray(o, dtype=np.float32) for o in out)
```

```python
import numpy as np
import concourse.bass as bass
import concourse.mybir as mybir
from concourse.bass_utils import run_bass_kernel_spmd

F32 = mybir.dt.float32
BF16 = mybir.dt.bfloat16
AF = mybir.ActivationFunctionType
ALU = mybir.AluOpType
AX = mybir.AxisListType

L = 4
D = 1024
SEQ = 2048
NS_ = 16
NT = SEQ + NS_
H = 4
DH = 256
DFF = 2816
DIN = 7688
EPS = 1e-5
ALPHA = (2.0 * L) ** 0.25
TILES = [(0, 512), (512, 512), (1024, 512), (1536, 512), (2048, 16)]
O_B, O_C, O_XV, O_Q, O_K, O_V, O_O, O_IG, O_GA, O_GB = 0, 512, 1024, 1536, 2560, 3584, 4608, 5632, 5640, 6664
NSLOT = 5
NEG = -1.0e30

PP_LN = 0
PP_CM = 128
PP_FC = 176
PP_GB = 704
PP_N = 736
C_ID, C_U, C_MTS, C_S127, C_MST, C_I16, C_N = 0, 128, 256, 384, 512, 640, 896

ENG = ("pe", "act", "dve", "pool", "sp")


class Op:
    __slots__ = ("eng", "fn", "r", "w", "dma", "chain", "waits", "inc", "idx")

    def __init__(self, eng, fn, r, w, dma, chain):
        self.eng, self.fn, self.r, self.w, self.dma, self.chain = eng, fn, r, w, dma, chain
        self.waits = []
        self.inc = None
        self.idx = -1


class Prog:
    def __init__(self, nc):
        self.nc = nc
        self.ops = []

    def op(self, eng, fn, r=(), w=(), dma=False, chain=None):
        o = Op(eng, fn, tuple(r), tuple(w), dma, chain)
        o.idx = len(self.ops)
        self.ops.append(o)
        return o

    def resolve(self):
        last_w, readers = {}, {}
        n = len(self.ops)
        deps = [None] * n
        has_dep = [False] * n
        for o in self.ops:
            d = set()
            for k in o.r:
                lw = last_w.get(k)
                if lw is not None:
                    d.add(lw)
            for k in o.w:
                lw = last_w.get(k)
                if lw is not None:
                    d.add(lw)
                rl = readers.get(k)
                if rl:
                    d.update(rl)
            d.discard(o.idx)
            if o.eng == "pe" and not o.dma:
                d = {x for x in d if not (self.ops[x].eng == "pe" and not self.ops[x].dma)}
            latest = {}
            for x in d:
                ox = self.ops[x]
                sk = ("c", ox.chain) if ox.dma else ("e", ox.eng)
                if x > latest.get(sk, -1):
                    latest[sk] = x
            d = set(latest.values())
            deps[o.idx] = d
            for x in d:
                has_dep[x] = True
            for k in o.w:
                last_w[k] = o.idx
                readers[k] = []
            for k in o.r:
                readers.setdefault(k, []).append(o.idx)
        eng_cnt = {e: 0 for e in ENG}
        chain_cnt = {}
        self.chains = []
        ms = [None] * n
        for o in self.ops:
            if o.dma:
                c = chain_cnt.get(o.chain, 0) + 16
                chain_cnt[o.chain] = c
                if c == 16:
                    self.chains.append(o.chain)
                ms[o.idx] = (("c", o.chain), c)
                o.inc = ("c", o.chain)
            elif has_dep[o.idx]:
                eng_cnt[o.eng] += 1
                ms[o.idx] = (("e", o.eng), eng_cnt[o.eng])
                o.inc = ("e", o.eng)
        self.eng_cnt, self.chain_cnt = eng_cnt, chain_cnt
        seen = {e: {} for e in ENG}
        for o in self.ops:
            need = {}
            for x in deps[o.idx]:
                s, v = ms[x]
                if v > need.get(s, 0):
                    need[s] = v
            sn = seen[o.eng]
            for s, v in need.items():
                if sn.get(s, 0) >= v:
                    continue
                sn[s] = v
                o.waits.append((s, v))
        return self

    def emit(self, block):
        nc = self.nc
        sems = {}
        for e in ENG:
            if self.eng_cnt[e] > 0:
                sems[("e", e)] = nc.alloc_semaphore("se_" + e)
        for i, c in enumerate(self.chains):
            sems[("c", c)] = nc.alloc_semaphore("sc_%d" % i)
        per = {e: [o for o in self.ops if o.eng == e] for e in ENG}

        def run(eh, ename):
            for o in per[ename]:
                for s, v in o.waits:
                    eh.wait_ge(sems[s], v)
                ins = o.fn(eh)
                if o.inc is not None:
                    ins.then_inc(sems[o.inc], 16 if o.dma else 1)
            if ename == "sp":
                for c in self.chains:
                    eh.wait_ge(sems[("c", c)], self.chain_cnt[c])

        @block.tensor
        def _(e):
            run(e, "pe")

        @block.scalar
        def _(e):
            run(e, "act")

        @block.vector
        def _(e):
            run(e, "dve")

        @block.gpsimd
        def _(e):
            run(e, "pool")

        @block.sync
        def _(e):
            run(e, "sp")


def build_nc():
    nc = bass.Bass("TRN2", target_bir_lowering=False)

    def din(name, shape):
        return nc.dram_tensor(name, list(shape), F32, kind="ExternalInput").ap()

    def dout(name, shape):
        return nc.dram_tensor(name, list(shape), F32, kind="ExternalOutput").ap()

    xT = din("xT", [D, NT])
    pp_d = din("pp", [128, PP_N])
    cst_d = din("cst", [128, C_N])
    mh_d = din("mhln_g", [L, D])
    w_in = din("w_in", [L, D, DIN])
    w_pa = din("w_proj_a", [L, 512, D])
    w_pb = din("w_proj_b", [L, D, D])
    w_mx = din("w_mix_out", [L, D, D])
    w_up = din("w_ffn_up", [L, D, 2 * DFF])
    w_dn = din("w_ffn_down", [L, DFF, D])
    cs_in = din("cs_in", [L, 2, 128, 4, NS_])
    cf_in = din("cf_in", [L, 2, 128, 44, NS_])
    Cs_in = din("Cs_in", [L, NS_, H, DH, DH])
    ns_in = din("ns_in", [L, NS_, H * DH])
    ms_in = din("ms_in", [L, NS_, H])

    yT = dout("yT", [D, NT])
    sc_p = dout("sc_p", [L, 128, 4, 2])
    sc_s = dout("sc_s", [L, 2, 128, 4, NS_])
    Cp_o = dout("Cp_o", [L, H, DH, DH])
    Cs_o = dout("Cs_o", [L, NS_, H, DH, DH])
    np_o = dout("np_o", [L, H, 128, 2])
    ns_o = dout("ns_o", [L, NS_, H * DH])
    mp_o = dout("mp_o", [L, H])
    ms_o = dout("ms_o", [L, NS_, H])
    ff_p = dout("ff_p", [L, 128, 44, 2])
    ff_s = dout("ff_s", [L, 2, 128, 44, NS_])

    P = Prog(nc)

    def sb(name, shape, dt=F32):
        return nc.alloc_sbuf_tensor("sb_" + name, list(shape), dt).ap()

    x32 = sb("x32", [128, 8, NT])
    xb = sb("xb", [128, 8, NT], BF16)
    slots = [sb("ws%d" % i, [128, 2048], BF16) for i in range(NSLOT)]
    S = nc.alloc_sbuf_tensor("S", [128, 79 * 1024], mybir.dt.uint8)
    pp = sb("pp", [128, PP_N])
    cst = sb("cst", [128, C_N])
    identb = sb("identb", [128, 128], BF16)
    mstb = sb("mstb", [128, 128], BF16)
    onesb = sb("onesb", [128, 128], BF16)
    ones32 = sb("ones32", [128, 128])
    gbc = sb("gbc", [128, DH])
    psum = [nc.alloc_psum_tensor("ps%d" % i, [128, 512], F32).ap() for i in range(8)]

    class Carver:
        def __init__(self):
            self.off = 0

        def take(self, shape, dt):
            esz = 2 if dt == BF16 else 4
            n = 1
            for s in shape[1:]:
                n *= s
            nbytes = (n * esz + 31) // 32 * 32
            assert self.off + nbytes <= 79 * 1024, (self.off, nbytes)
            v = S[:, self.off:self.off + nbytes].bitcast(dt)[:, 0:n]
            self.off += nbytes
            if len(shape) == 3:
                v = v.rearrange("p (a b) -> p a b", b=shape[2])
            elif len(shape) == 4:
                v = v.rearrange("p (a b c) -> p a b c", b=shape[2], c=shape[3])
            return v[0:shape[0]]

    state = {"ps": 0, "ws": 0, "ev": 0, "tok": None}
    _raw_op = P.op

    def _op(eng, fn, r=(), w=(), dma=False, chain=None):
        r = list(r)
        if state["tok"] is not None and eng != "pool":
            r.append(state["tok"])
        return _raw_op(eng, fn, r, w, dma, chain)

    P.op = _op

    def PS():
        i = state["ps"] % 7
        state["ps"] += 1
        return psum[i], ("ps", i)

    def PSL():
        return psum[7], ("ps", 7)

    def MM(out, lhsT, rhs, start, stop, r, w):
        P.op("pe", lambda e: e.matmul(out, lhsT=lhsT, rhs=rhs, start=start, stop=stop), r=r, w=w)

    def TR(out, in_, ident, r, w):
        P.op("pe", lambda e: e.transpose(out, in_, ident), r=r, w=w)

    def ACT(out, in_, func, r, w, bias=0.0, scale=1.0):
        P.op("act", lambda e: e.activation(out=out, in_=in_, func=func, bias=bias, scale=scale), r=r, w=w)

    def TT(out, in0, in1, op, r, w, eng="dve"):
        P.op(eng, lambda e: e.tensor_tensor(out=out, in0=in0, in1=in1, op=op), r=r, w=w)

    def TS(out, in0, s1, s2, op0, op1, r, w):
        if s2 is None:
            P.op("dve", lambda e: e.tensor_scalar(out=out, in0=in0, scalar1=s1, scalar2=None, op0=op0), r=r, w=w)
        else:
            P.op("dve", lambda e: e.tensor_scalar(out=out, in0=in0, scalar1=s1, scalar2=s2, op0=op0, op1=op1), r=r, w=w)

    def STT(out, in0, scalar, in1, op0, op1, r, w):
        P.op("dve", lambda e: e.scalar_tensor_tensor(out=out, in0=in0, scalar=scalar, in1=in1, op0=op0, op1=op1),
             r=r, w=w)

    def CP(out, in_, r, w, eng=None):
        if eng is None:
            eng = "act" if state["ev"] % 2 == 0 else "dve"
            state["ev"] += 1
        if eng == "act":
            P.op("act", lambda e: e.activation(out=out, in_=in_, func=AF.Copy), r=r, w=w)
        else:
            P.op("dve", lambda e: e.tensor_copy(out=out, in_=in_), r=r, w=w)

    def PTS(out, in0, s1, r, w):
        P.op("act", lambda e: e.activation(out=out, in_=in0, func=AF.Identity, scale=s1), r=r, w=w)

    def PCP(out, in_, r, w):
        P.op("act", lambda e: e.activation(out=out, in_=in_, func=AF.Copy), r=r, w=w)

    def MSET(ap, val, w, eng="dve"):
        P.op(eng, lambda e: e.memset(ap, val), w=w)

    def DMA(q, out, in_, r, w, chain):
        P.op(q, lambda e: e.dma_start(out=out, in_=in_), r=r, w=w, dma=True, chain=chain)

    def WLOAD(src2d, KC, ncols):
        i = state["ws"] % NSLOT
        state["ws"] += 1
        v = slots[i][:, 0:KC * ncols].rearrange("p (k n) -> p k n", n=ncols)
        key = ("ws", i)
        DMA("pool", v, src2d.rearrange("(k p) n -> p k n", p=128), r=[], w=[key], chain=key)
        return v, key

    def xbk(t):
        return [("xb", k, t) for k in range(8)]

    def barrier(newtok, scratch):
        old = state["tok"]
        wk = [newtok] + ([old] if old is not None else [])
        _raw_op("dve", lambda e: e.memset(scratch, 0.0), [], wk, False, None)
        state["tok"] = newtok

    DMA("sp", pp, pp_d, [], ["pp"], "pp")
    DMA("sp", cst, cst_d, [], ["cst"], "cst")
    CP(identb, cst[:, C_ID:C_ID + 128], ["cst"], ["identb"], eng="dve")
    CP(mstb, cst[:, C_MST:C_MST + 128], ["cst"], ["mstb"], eng="dve")
    MSET(onesb, 1.0, ["onesb"])
    MSET(ones32, 1.0, ["ones32"])
    ident32 = cst[:, C_ID:C_ID + 128]
    Utri = cst[:, C_U:C_U + 128]
    mts = cst[:, C_MTS:C_MTS + 128]
    sel127 = cst[:, C_S127:C_S127 + 128]
    I16bc = cst[:, C_I16:C_I16 + 256].rearrange("p (a b) -> p a b", b=16)
    xTv = xT.rearrange("(c p) n -> p c n", p=128)
    for c in range(8):
        DMA("sp", x32[:, c, :], xTv[:, c, :], [], [("x32", c, t) for t in range(5)], ("x32i", c))
        DMA("pool", xb[:, c, :], xTv[:, c, :], [], [("xb", c, t) for t in range(5)], ("xbi", c))
    bscr = sb("bscr", [128, 8])

    def ppc(base, l, i):
        return pp[:, base + l * 8 + i: base + l * 8 + i + 1]

    for l in range(L):
        cv = Carver()
        hmT = cv.take([128, 8, NT], BF16)
        qT = cv.take([128, 2, 512], BF16)
        kT = cv.take([128, 2, 512], BF16)
        ktok = cv.take([128, 4, DH], BF16)
        vtok = cv.take([128, 4, DH + 2], BF16)
        gsig = cv.take([128, 4, DH], F32)
        Et = cv.take([128, 128], F32)
        PTt = cv.take([128, 128], BF16)
        diag = cv.take([128, 128], F32)
        tmp128 = Et
        numB = cv.take([128, DH + 1], F32)
        num = cv.take([128, DH + 1], F32)
        hn = cv.take([128, DH], F32)
        hg = cv.take([128, DH], BF16)
        Kw = cv.take([128, DH], BF16)
        C32 = cv.take([128, 2, DH], F32)
        n32 = cv.take([128, 2], F32)
        Cb = cv.take([128, 2, DH + 2], BF16)
        st6 = cv.take([128, 6], F32)
        mv = cv.take([128, 2], F32)
        sc4 = cv.take([128, 4], F32)
        G_ = {}
        for nm in ("gp", "li", "sp", "bloc", "tot", "ginc", "G", "a", "cm", "cmx", "M", "nM", "fl", "sI", "wS",
                   "dec", "t1"):
            G_[nm] = cv.take([128, 16, 4], F32)
        MT = cv.take([128, 17, 4], F32)
        q_s = cv.take([NS_, DH], F32)
        k_s = cv.take([NS_, DH], F32)
        v_s = cv.take([NS_, DH], F32)
        g_s = cv.take([NS_, DH], F32)
        t_s_full = cv.take([128, DH], F32)[:, 0:128]
        kw_s_full = cv.take([128, DH], F32)[:, 0:128]
        vm_s_full = cv.take([128, DH], F32)[:, 0:128]
        cv.off -= 3 * 1024
        t_s = cv.take([NS_, DH], F32)
        kw_s = cv.take([NS_, DH], F32)
        vm_s = cv.take([NS_, DH], BF16)
        kwb_s = cv.take([NS_, DH], BF16)
        nold_s = cv.take([NS_, DH], F32)
        nnew_s = cv.take([NS_, DH], F32)
        num_s = cv.take([NS_, DH], F32)
        hg_s = cv.take([NS_, DH], BF16)
        sg = {}
        for nm in ("gps", "li", "lf", "m", "mn", "dec", "wg", "fl", "qk", "qn", "s", "den", "rdn", "t"):
            sg[nm] = cv.take([NS_, 8], F32)
        Dm = cv.take([NS_, 4, NS_], F32)
        decs_bc = cv.take([128, 4, NS_], F32)
        qTs = cv.take([128, 2, NS_], F32)
        Qm = cv.take([128, 2, NS_, NS_], F32)
        Cst = [cv.take([128, 2, DH], F32) for _ in range(3)]
        st6s = cv.take([NS_, 6], F32)
        mvs = cv.take([NS_, 2], F32)
        RA = ("RA", l)
        barrier(RA, bscr)

        gw, gwk = WLOAD(w_in[l][:, O_IG:O_IG + 8], 8, 8)
        gps, gpk = PS()
        gpsv = gps[:, 0:128].rearrange("p (c g) -> p c g", g=8)
        for c in range(16):
            for k in range(8):
                MM(gpsv[:, c, :], xb[:, k, c * 128:(c + 1) * 128], gw[:, k, :], k == 0, k == 7,
                   [gwk, ("xb", k, c // 4), RA], [gpk])
        bi_bc = pp[:, PP_GB + l * 8:PP_GB + l * 8 + 4]
        bf_bc = pp[:, PP_GB + l * 8 + 4:PP_GB + l * 8 + 8]
        g = G_
        K = lambda nm: ("g", nm)
        TT(g["li"], gpsv[:, :, 0:4], bi_bc.unsqueeze(1).broadcast_to([128, 16, 4]), ALU.add, [gpk, "pp", RA], [K("li")])
        TT(g["t1"], gpsv[:, :, 4:8], bf_bc.unsqueeze(1).broadcast_to([128, 16, 4]), ALU.add, [gpk, "pp", RA], [K("t1")])
        ACT(g["sp"], g["t1"], AF.Exp, [K("t1")], [K("sp")], scale=-1.0)
        ACT(g["sp"], g["sp"], AF.Ln, [K("sp")], [K("sp")], bias=1.0)
        f64 = lambda t: t.rearrange("p c h -> p (c h)")
        ps1, pk1 = PS()
        MM(ps1[:, 0:64], Utri, f64(g["sp"]), True, True, [K("sp"), "cst"], [pk1])
        CP(f64(g["bloc"]), ps1[:, 0:64], [pk1], [K("bloc")], eng="dve")
        ps2, pk2 = PS()
        MM(ps2[:, 0:64], ones32, f64(g["sp"]), True, True, [K("sp"), "ones32"], [pk2])
        CP(f64(g["tot"]), ps2[:, 0:64], [pk2], [K("tot")], eng="dve")
        for h in range(H):
            P.op("dve", lambda e, h=h: e.tensor_tensor_scan(out=g["ginc"][:, :, h], data0=g["tot"][:, :, h],
                                                            data1=g["tot"][:, :, h], initial=0.0,
                                                            op0=ALU.add, op1=ALU.bypass),
                 r=[K("tot")], w=[K("ginc")])
        TT(g["G"], g["ginc"], g["tot"], ALU.subtract, [K("ginc"), K("tot")], [K("G")])
        TT(g["G"], g["G"], g["bloc"], ALU.add, [K("G"), K("bloc")], [K("G")])
        TT(g["a"], g["li"], g["G"], ALU.add, [K("li"), K("G")], [K("a")])
        dgs = [(diag, "diag"), (t_s_full, "t_s"), (kw_s_full, "kw_s"), (vm_s_full, "vm_s")]
        items = [(c, hh_) for c in range(16) for hh_ in range(H)]

        def cm_front(i):
            c, hh_ = items[i]
            dg, dk = dgs[i % 4]
            TS(dg, ident32, g["a"][:, c, hh_:hh_ + 1], None, ALU.mult, None, [K("a"), "cst"], [dk])

        LOOK = 3
        for i in range(min(LOOK, len(items))):
            cm_front(i)
        for i, (c, hh_) in enumerate(items):
            dg, dk = dgs[i % 4]
            psd, pkd = PS()
            MM(psd[:, 0:128], ones32, dg, True, True, [dk, "ones32"], [pkd])
            if i + LOOK < len(items):
                cm_front(i + LOOK)
            TT(tmp128, psd[:, 0:128], mts, ALU.add, [pkd, "cst"], ["Et"])
            P.op("dve", lambda e, c=c, hh_=hh_: e.tensor_reduce(out=g["cm"][:, c, hh_:hh_ + 1], in_=tmp128, axis=AX.X,
                                                                op=ALU.max), r=["Et"], w=[K("cm")])
        ps3, pk3 = PS()
        MM(ps3[:, 0:64], sel127, f64(g["cm"]), True, True, [K("cm"), "cst"], [pk3])
        CP(f64(g["cmx"]), ps3[:, 0:64], [pk3], [K("cmx")], eng="dve")
        MSET(MT[:, 0, :], 0.0, [K("MT")])
        for h in range(H):
            P.op("dve", lambda e, h=h: e.tensor_tensor_scan(out=MT[:, 1:17, h], data0=g["cmx"][:, :, h],
                                                            data1=g["cmx"][:, :, h], initial=0.0,
                                                            op0=ALU.max, op1=ALU.bypass),
                 r=[K("cmx"), K("MT")], w=[K("MT")])
        Mprev = MT[:, 0:16, :]
        Mend = MT[:, 1:17, :]
        TT(g["M"], g["cm"], Mprev, ALU.max, [K("cm"), K("MT")], [K("M")])
        TS(g["nM"], g["M"], -1.0, None, ALU.mult, None, [K("M")], [K("nM")])
        TT(g["t1"], g["G"], g["M"], ALU.subtract, [K("G"), K("M"), K("sp")], [K("t1")])
        ACT(g["fl"], g["t1"], AF.Exp, [K("t1")], [K("fl")])
        TT(g["sI"], Mprev, g["M"], ALU.subtract, [K("MT"), K("M")], [K("sI")])
        ACT(g["sI"], g["sI"], AF.Exp, [K("sI")], [K("sI")])
        TT(g["wS"], g["a"], Mend, ALU.subtract, [K("a"), K("MT")], [K("wS")])
        ACT(g["wS"], g["wS"], AF.Exp, [K("wS")], [K("wS")])
        TT(g["dec"], Mprev, Mend, ALU.subtract, [K("MT")], [K("dec")])
        ACT(g["dec"], g["dec"], AF.Exp, [K("dec")], [K("dec")])
        TT(sc4, MT[:, 16, :], g["ginc"][:, 15, :], ALU.subtract, [K("MT"), K("ginc")], ["sc4"])
        DMA("sp", mp_o[l:l + 1, :], sc4[0:1, :], ["sc4"], [], "mp_o")

        s = sg
        SK = lambda nm: ("sg", nm)
        psg, pkg = PS()
        for k in range(8):
            MM(psg[0:NS_, 0:8], xb[:, k, SEQ:NT], gw[:, k, :], k == 0, k == 7, [gwk, ("xb", k, 4), RA], [pkg])
        DMA("sp", s["m"][:, 0:4], ms_in[l], [RA], [SK("m")], "ms_in")
        TT(s["li"][:, 0:4], psg[0:NS_, 0:4], bi_bc[0:NS_], ALU.add, [pkg, "pp"], [SK("li")])
        TT(s["t"][:, 0:4], psg[0:NS_, 4:8], bf_bc[0:NS_], ALU.add, [pkg, "pp"], [SK("t")])
        ACT(s["lf"][:, 0:4], s["t"][:, 0:4], AF.Exp, [SK("t")], [SK("lf")], scale=-1.0)
        ACT(s["lf"][:, 0:4], s["lf"][:, 0:4], AF.Ln, [SK("lf")], [SK("lf")], bias=1.0)
        TT(s["t"][:, 0:4], s["m"][:, 0:4], s["lf"][:, 0:4], ALU.subtract, [SK("m"), SK("lf"), SK("t")], [SK("t")])
        TT(s["mn"][:, 0:4], s["t"][:, 0:4], s["li"][:, 0:4], ALU.max, [SK("t"), SK("li")], [SK("mn")])
        TT(s["dec"][:, 0:4], s["t"][:, 0:4], s["mn"][:, 0:4], ALU.subtract, [SK("t"), SK("mn")], [SK("dec")])
        ACT(s["dec"][:, 0:4], s["dec"][:, 0:4], AF.Exp, [SK("dec")], [SK("dec")])
        TT(s["wg"][:, 0:4], s["li"][:, 0:4], s["mn"][:, 0:4], ALU.subtract, [SK("li"), SK("mn")], [SK("wg")])
        ACT(s["wg"][:, 0:4], s["wg"][:, 0:4], AF.Exp, [SK("wg")], [SK("wg")])
        ACT(s["fl"][:, 0:4], s["mn"][:, 0:4], AF.Exp, [SK("mn")], [SK("fl")], scale=-1.0)
        DMA("sp", ms_o[l], s["mn"][:, 0:4], [SK("mn")], [], "ms_o")
        TT(Dm, s["dec"][:, 0:4].unsqueeze(2).broadcast_to([NS_, 4, NS_]),
           ident32[0:NS_, 0:NS_].unsqueeze(1).broadcast_to([NS_, 4, NS_]), ALU.mult, [SK("dec"), "cst"], ["Dm"])
        psb, pkb = PS()
        MM(psb[:, 0:64], ones32[0:NS_, :], Dm.rearrange("p h b -> p (h b)"), True, True, ["Dm", "ones32"], [pkb])
        CP(decs_bc.rearrange("p h b -> p (h b)"), psb[:, 0:64], [pkb], ["decs_bc"], eng="dve")

        for h in range(H):
            wq, wqk = WLOAD(w_in[l][:, O_Q + h * DH:O_Q + (h + 1) * DH], 8, DH)
            wk, wkk = WLOAD(w_in[l][:, O_K + h * DH:O_K + (h + 1) * DH], 8, DH)
            wv, wvk = WLOAD(w_in[l][:, O_V + h * DH:O_V + (h + 1) * DH], 8, DH)
            wo, wok = WLOAD(w_in[l][:, O_O + h * DH:O_O + (h + 1) * DH], 8, DH)
            DMA("sp", gbc, mh_d[l:l + 1, h * DH:(h + 1) * DH].broadcast_to([128, DH]), [RA], ["gbc"], "gbc")
            MSET(C32, 0.0, ["C32"])
            MSET(n32, 0.0, ["n32"])
            MSET(Cb, 0.0, ["Cb"], eng="dve")
            MSET(vtok[:, :, DH:DH + 2], 1.0, ["vtok1"])
            hh = slice(h, h + 1)
            qk_, kk_, vk_ = ("s", id(q_s)), ("s", id(k_s)), ("s", id(v_s))
            pqc, pqck = PSL()

            def sample_prep():
                for (dst, wv_, wk_, sc_) in ((q_s, wq, wqk, 1.0), (k_s, wk, wkk, 1.0 / 16.0), (v_s, wv, wvk, 1.0)):
                    pp_, ppk = PS()
                    for k in range(8):
                        MM(pp_[0:NS_, 0:DH], xb[:, k, SEQ:NT], wv_[:, k, :], k == 0, k == 7, [wk_, ("xb", k, 4), RA], [ppk])
                    ACT(dst, pp_[0:NS_, 0:DH], AF.Identity, [ppk], [("s", id(dst))], scale=sc_)
                pp_, ppk = PS()
                for k in range(8):
                    MM(pp_[0:NS_, 0:DH], xb[:, k, SEQ:NT], wo[:, k, :], k == 0, k == 7, [wok, ("xb", k, 4), RA], [ppk])
                ACT(g_s, pp_[0:NS_, 0:DH], AF.Sigmoid, [ppk], ["g_s"])
                TT(g_s, g_s, gbc[0:NS_], ALU.mult, ["g_s", "gbc"], ["g_s"])
                qk_, kk_, vk_ = ("s", id(q_s)), ("s", id(k_s)), ("s", id(v_s))
                for dc in range(2):
                    pq, pqk = PS()
                    for k in range(8):
                        MM(pq[:, 0:NS_], wq[:, k, dc * 128:(dc + 1) * 128], xb[:, k, SEQ:NT], k == 0, k == 7,
                           [wqk, ("xb", k, 4), RA], [pqk])
                    CP(qTs[:, dc, :], pq[:, 0:NS_], [pqk], [("qTs", dc)], eng="dve")
                    TT(Qm[:, dc], qTs[:, dc, :].unsqueeze(2).broadcast_to([128, NS_, NS_]), I16bc, ALU.mult,
                       [("qTs", dc), "cst"], [("Qm", dc)])
                DMA("sp", nold_s, ns_in[l][:, h * DH:(h + 1) * DH], [RA], ["nold_s"], "nold_s")
                TT(t_s, q_s, k_s, ALU.mult, [qk_, kk_], ["t_s"])
                P.op("dve", lambda e, h=h: e.tensor_reduce(out=s["qk"][:, h:h + 1], in_=t_s, axis=AX.X, op=ALU.add),
                     r=["t_s"], w=[SK("qk")])
                TT(t_s, q_s, nold_s, ALU.mult, [qk_, "nold_s", SK("qk")], ["t_s"])
                P.op("dve", lambda e, h=h: e.tensor_reduce(out=s["qn"][:, h:h + 1], in_=t_s, axis=AX.X, op=ALU.add),
                     r=["t_s"], w=[SK("qn")])
                hh = slice(h, h + 1)
                TT(s["s"][:, hh], s["qk"][:, hh], s["wg"][:, hh], ALU.mult, [SK("qk"), SK("wg")], [SK("s")])
                TT(s["den"][:, hh], s["dec"][:, hh], s["qn"][:, hh], ALU.mult, [SK("dec"), SK("qn")], [SK("den")])
                TT(s["den"][:, hh], s["den"][:, hh], s["s"][:, hh], ALU.add, [SK("den"), SK("s")], [SK("den")])
                STT(s["rdn"][:, hh], s["den"][:, hh], -1.0, s["den"][:, hh], ALU.mult, ALU.max, [SK("den")], [SK("rdn")])
                TS(s["rdn"][:, hh], s["rdn"][:, hh], s["fl"][:, hh], None, ALU.max, None, [SK("rdn"), SK("fl")], [SK("rdn")])
                P.op("dve", lambda e, hh=hh: e.reciprocal(out=s["rdn"][:, hh], in_=s["rdn"][:, hh]), r=[SK("rdn")], w=[SK("rdn")])
                TS(kw_s, k_s, s["wg"][:, hh], None, ALU.mult, None, [kk_, SK("wg")], ["kw_s"])
                CP(kwb_s, kw_s, ["kw_s"], ["kwb_s"], eng="act")
                STT(nnew_s, nold_s, s["dec"][:, hh], kw_s, ALU.mult, ALU.add, ["nold_s", "kw_s", SK("dec"), "t_s"], ["nnew_s"])
                DMA("sp", ns_o[l][:, h * DH:(h + 1) * DH], nnew_s, ["nnew_s"], [], "ns_o")


            def sample_step(b):
                ct = Cst[b % 3]
                ck = ("Cst", b % 3)
                if b == 0:
                    for b2 in range(2):
                        DMA("sp", Cst[b2], Cs_in[l, b2, h].rearrange("(kc p) v -> p kc v", p=128), [RA],
                            [("Cst", b2)], ("Cst_i", b2))
                if b + 2 < NS_:
                    b2 = b + 2
                    DMA("sp", Cst[b2 % 3], Cs_in[l, b2, h].rearrange("(kc p) v -> p kc v", p=128), [RA],
                        [("Cst", b2 % 3)], ("Cst_i", b2 % 3))
                for kc in range(2):
                    MM(pqc[0:NS_, 0:DH], Qm[:, kc, b, :], ct[:, kc, :], b == 0 and kc == 0, b == NS_ - 1 and kc == 1,
                       [("Qm", kc), ck], [pqck])
                TS(vm_s, v_s, ident32[0:NS_, b:b + 1], None, ALU.mult, None, [vk_, "cst"], ["vm_s"])
                pU, pUk = PS()
                for kc in range(2):
                    MM(pU[:, kc * DH:(kc + 1) * DH], kwb_s[:, kc * 128:(kc + 1) * 128], vm_s, True, True,
                       ["kwb_s", "vm_s"], [pUk])
                STT(ct.rearrange("p a b -> p (a b)"), ct.rearrange("p a b -> p (a b)"), decs_bc[:, h, b:b + 1], pU,
                    ALU.mult, ALU.add, [ck, pUk, "decs_bc"], [ck])
                DMA("act", Cs_o[l, b, h].rearrange("(kc p) v -> p kc v", p=128), ct, [ck], [], ("Cst_o", b % 3))

            def sample_tail():
                TS(num_s, v_s, s["s"][:, hh], None, ALU.mult, None, [vk_, SK("s")], ["num_s"])
                STT(num_s, pqc[0:NS_, 0:DH], s["dec"][:, hh], num_s, ALU.mult, ALU.add, [pqck, "num_s", SK("dec")], ["num_s"])
                TS(num_s, num_s, s["rdn"][:, hh], None, ALU.mult, None, ["num_s", SK("rdn")], ["num_s"])
                P.op("dve", lambda e: e.bn_stats(out=st6s, in_=num_s), r=["num_s"], w=["st6s"])
                P.op("dve", lambda e: e.bn_aggr(out=mvs, in_=st6s), r=["st6s"], w=["mvs"])
                ACT(s["t"][:, 4:5], mvs[:, 1:2], AF.Ln, ["mvs"], [SK("t2")], bias=EPS)
                ACT(s["t"][:, 4:5], s["t"][:, 4:5], AF.Exp, [SK("t2")], [SK("t2")], scale=-0.5)
                TS(num_s, num_s, mvs[:, 0:1], s["t"][:, 4:5], ALU.subtract, ALU.mult, ["num_s", "mvs", SK("t2")], ["num_s"])
                TT(hg_s, num_s, g_s, ALU.mult, ["num_s", "g_s"], ["hg_s"])
                ptp, ptpk = PS()
                ptb = ptp.bitcast(BF16)
                for dc in range(2):
                    TR(ptb[:, dc * NS_:(dc + 1) * NS_], hg_s[:, dc * 128:(dc + 1) * 128], identb[0:NS_, 0:NS_],
                       ["hg_s", "identb"], [ptpk])
                CP(hmT[:, 2 * h:2 * h + 2, SEQ:NT], ptb[:, 0:2 * NS_].rearrange("p (a b) -> p a b", b=NS_), [ptpk, RA],
                   [("hmT", 2 * h, 4), ("hmT", 2 * h + 1, 4)])
            sample_prep()
            for tt in range(4):
                t0 = tt * 512
                for dc in range(2):
                    pq, pqk = PS()
                    for k in range(8):
                        MM(pq, wq[:, k, dc * 128:(dc + 1) * 128], xb[:, k, t0:t0 + 512], k == 0, k == 7,
                           [wqk, ("xb", k, tt), RA], [pqk])
                    CP(qT[:, dc, :], pq, [pqk], [("qT", dc)])
                    pk_, pkk = PS()
                    for k in range(8):
                        MM(pk_, wk[:, k, dc * 128:(dc + 1) * 128], xb[:, k, t0:t0 + 512], k == 0, k == 7,
                           [wkk, ("xb", k, tt), RA], [pkk])
                    ACT(kT[:, dc, :], pk_, AF.Identity, [pkk], [("kT", dc)], scale=1.0 / 16.0)
                for ci in range(4):
                    c0 = t0 + ci * 128
                    pa, pak = PS()
                    for k in range(8):
                        MM(pa[:, 0:DH], xb[:, k, c0:c0 + 128], wk[:, k, :], k == 0, k == 7,
                           [wkk, ("xb", k, tt), RA], [pak])
                    ACT(ktok[:, ci, :], pa[:, 0:DH], AF.Identity, [pak], [("ktok", ci)], scale=1.0 / 16.0)
                    pb_, pbk = PS()
                    for k in range(8):
                        MM(pb_[:, 0:DH], xb[:, k, c0:c0 + 128], wv[:, k, :], k == 0, k == 7,
                           [wvk, ("xb", k, tt), RA], [pbk])
                    CP(vtok[:, ci, 0:DH], pb_[:, 0:DH], [pbk], [("vtok", ci)], eng="dve")
                    pc_, pck = PS()
                    for k in range(8):
                        MM(pc_[:, 0:DH], xb[:, k, c0:c0 + 128], wo[:, k, :], k == 0, k == 7,
                           [wok, ("xb", k, tt), RA], [pck])
                    ACT(gsig[:, ci, :], pc_[:, 0:DH], AF.Sigmoid, [pck], [("gsig", ci)])
                    TT(gsig[:, ci, :], gsig[:, ci, :], gbc, ALU.mult, [("gsig", ci), "gbc"], [("gsig", ci)])
                XB, XK = psum[4], ("ps", 4)
                UB, UK = psum[5], ("ps", 5)
                TB, TK = psum[6], ("ps", 6)
                SB_, SK_ = XB, XK

                def front_a(ci):
                    c = tt * 4 + ci
                    cs = slice(ci * 128, (ci + 1) * 128)
                    pB, pBk = psum[2 + c % 2], ("ps", 2 + c % 2)
                    PTS(diag, ident32, g["nM"][:, c, h:h + 1], [K("nM"), "cst"], ["diag"])
                    PTS(Kw, ktok[:, ci, :], g["wS"][:, c, h:h + 1], [("ktok", ci), K("wS")], ["Kw"])
                    for dc in range(2):
                        MM(SB_[:, 0:128], kT[:, dc, cs], qT[:, dc, cs], dc == 0, dc == 1, [("kT", dc), ("qT", dc)], [SK_])
                    MM(XB[:, 128:256], ones32, diag, True, False, ["diag", "ones32"], [XK])
                    MM(XB[:, 128:256], identb, mstb, False, True, ["identb", "mstb"], [XK])
                    for kc in range(2):
                        MM(XB[:, 256 + kc:257 + kc], Kw[:, kc * 128:(kc + 1) * 128], vtok[:, ci, DH:DH + 1], True, True,
                           ["Kw", "vtok1"], [XK])
                    for kc in range(2):
                        MM(UB[:, kc * DH:(kc + 1) * DH], Kw[:, kc * 128:(kc + 1) * 128], vtok[:, ci, 0:DH], True, True,
                           ["Kw", ("vtok", ci)], [UK])

                def front_a2(ci):
                    c = tt * 4 + ci
                    cs = slice(ci * 128, (ci + 1) * 128)
                    pB, pBk = psum[2 + c % 2], ("ps", 2 + c % 2)
                    for kc in range(2):
                        MM(pB[:, 0:DH + 1], qT[:, kc, cs], Cb[:, kc, 0:DH + 1], kc == 0, kc == 1, [("qT", kc), "Cb"], [pBk])
                    STT(C32.rearrange("p a b -> p (a b)"), C32.rearrange("p a b -> p (a b)"), g["dec"][:, c, h:h + 1],
                        UB, ALU.mult, ALU.add, ["C32", UK, K("dec")], ["C32"])
                    STT(n32, n32, g["dec"][:, c, h:h + 1], XB[:, 256:258], ALU.mult, ALU.add, ["n32", XK, K("dec")], ["n32"])
                    PCP(Cb[:, :, 0:DH], C32, ["C32"], ["Cb"])
                    PCP(Cb[:, :, DH:DH + 1], n32.unsqueeze(2), ["n32", "Cb"], ["Cb"])
                    ACT(Et, XB[:, 128:256], AF.Exp, [XK, K("a")], ["Et"], bias=g["a"][:, c, h:h + 1])

                def front_b(ci):
                    c = tt * 4 + ci
                    pA, pAk = psum[c % 2], ("ps", c % 2)
                    TT(PTt, SB_[:, 0:128], Et, ALU.mult, [SK_, "Et"], ["PTt"])
                    MM(pA[:, 0:DH + 1], PTt, vtok[:, ci, 0:DH + 1], True, True, ["PTt", ("vtok", ci), "vtok1"], [pAk])

                def tail_a(ci):
                    c = tt * 4 + ci
                    pA, pAk = psum[c % 2], ("ps", c % 2)
                    pB, pBk = psum[2 + c % 2], ("ps", 2 + c % 2)
                    ACT(numB, pB[:, 0:DH + 1], AF.Identity, [pBk, K("sI")], ["numB"], scale=g["sI"][:, c, h:h + 1])
                    TT(num, pA[:, 0:DH + 1], numB, ALU.add, [pAk, "numB"], ["num"])
                    STT(mv[:, 0:1], num[:, DH:DH + 1], -1.0, num[:, DH:DH + 1], ALU.mult, ALU.max, ["num"], ["rdn"])
                    TS(mv[:, 0:1], mv[:, 0:1], g["fl"][:, c, h:h + 1], None, ALU.max, None, ["rdn", K("fl")], ["rdn"])
                    P.op("dve", lambda e: e.bn_stats(out=st6, in_=num[:, 0:DH]), r=["num"], w=["st6"])
                    P.op("dve", lambda e: e.bn_aggr(out=sc4[:, 0:2], in_=st6), r=["st6"], w=["sc4"])
                    TT(sc4[:, 2:3], mv[:, 0:1], mv[:, 0:1], ALU.mult, ["rdn", "sc4"], ["sc4b"])
                    STT(sc4[:, 2:3], sc4[:, 2:3], EPS, sc4[:, 1:2], ALU.mult, ALU.add, ["sc4b", "sc4"], ["sc4b"])
                    ACT(sc4[:, 2:3], sc4[:, 2:3], AF.Ln, ["sc4b"], ["sc4b"])
                    ACT(sc4[:, 2:3], sc4[:, 2:3], AF.Exp, ["sc4b"], ["sc4b"], scale=-0.5)

                def tail_b(ci):
                    c = tt * 4 + ci
                    TS(hn, num[:, 0:DH], sc4[:, 0:1], sc4[:, 2:3], ALU.subtract, ALU.mult, ["num", "sc4", "sc4b"], ["hn"])
                    TT(hg, hn, gsig[:, ci, :], ALU.mult, ["hn", ("gsig", ci)], ["hg"])
                    ptb = TB.bitcast(BF16)
                    for dc in range(2):
                        TR(ptb[:, dc * 128:(dc + 1) * 128], hg[:, dc * 128:(dc + 1) * 128], identb, ["hg", "identb"], [TK])
                    CP(hmT[:, 2 * h:2 * h + 2, c * 128:(c + 1) * 128],
                       ptb[:, 0:256].rearrange("p (a b) -> p a b", b=128), [TK, RA],
                       [("hmT", 2 * h, tt), ("hmT", 2 * h + 1, tt)])

                front_a(0)
                front_a2(0)
                front_b(0)
                for ci in range(1, 4):
                    front_a(ci)
                    tail_a(ci - 1)
                    front_a2(ci)
                    front_b(ci)
                    tail_b(ci - 1)
                tail_a(3)
                tail_b(3)
                for b_ in range(4 * tt, 4 * tt + 4):
                    sample_step(b_)
            DMA("sp", Cp_o[l, h].rearrange("(kc p) v -> p kc v", p=128), C32, ["C32"], [], "Cp_o")
            DMA("sp", np_o[l, h], n32, ["n32"], [], "np_o")
            sample_tail()


        cv = Carver()
        hmT = cv.take([128, 8, NT], BF16)
        u_ = cv.take([128, 4, NT], BF16)
        cur = cv.take([128, 2 + NT], F32)
        cvt = cv.take([128, NT], F32)
        t512 = [cv.take([128, 512], F32) for _ in range(1)]
        csT = cv.take([128, 2, 4, NS_], F32)
        scp = cv.take([128, 4, 2], F32)
        scs = cv.take([128, 4, NS_], F32)
        RB = ("RB", l)
        barrier(RB, bscr)
        DMA("sp", csT[:, 0], cs_in[l, 0], [RB], [("csT", 0)], ("csT", 0))
        DMA("sp", csT[:, 1], cs_in[l, 1], [RB], [("csT", 1)], ("csT", 1))
        DMA("sp", sc_s[l, 0], cs_in[l, 1], [], [], "sc_s0")
        MSET(cur[:, 0:2], 0.0, ["cur0"])
        for jp in range(2):
            wB, wBk = WLOAD(w_in[l][:, O_B + jp * 256:O_B + (jp + 1) * 256], 8, 256)
            wC, wCk = WLOAD(w_in[l][:, O_C + jp * 256:O_C + (jp + 1) * 256], 8, 256)
            wX, wXk = WLOAD(w_in[l][:, O_XV + jp * 256:O_XV + (jp + 1) * 256], 8, 256)
            for jj in range(2):
                j = jp * 2 + jj
                cl = slice(jj * 128, (jj + 1) * 128)
                for t, (t0, tn) in enumerate(TILES):
                    pc_, pck = PS()
                    for k in range(8):
                        MM(pc_[:, 0:tn], wC[:, k, cl], xb[:, k, t0:t0 + tn], k == 0, k == 7, [wCk, ("xb", k, t), RB], [pck])
                    px_, pxk = PS()
                    for k in range(8):
                        MM(px_[:, 0:tn], wX[:, k, cl], xb[:, k, t0:t0 + tn], k == 0, k == 7, [wXk, ("xb", k, t), RB], [pxk])
                    ACT(t512[0][:, 0:tn], pc_[:, 0:tn], AF.Copy, [pck], ["t512_0"])
                    TT(cur[:, 2 + t0:2 + t0 + tn], t512[0][:, 0:tn], px_[:, 0:tn], ALU.mult, ["t512_0", pxk, "cur0", RB],
                       [("cur", t)])
                cm_ = lambda jtap: pp[:, PP_CM + (l * 3 + jtap) * 4 + j:PP_CM + (l * 3 + jtap) * 4 + j + 1]
                curk = [("cur", t) for t in range(5)]
                ACT(cvt[:, 0:SEQ], cur[:, 0:SEQ], AF.Identity, curk + ["cur0", "pp"], ["cvt"], scale=cm_(0))
                STT(cvt[:, 0:SEQ], cur[:, 1:SEQ + 1], cm_(1), cvt[:, 0:SEQ], ALU.mult, ALU.add, curk + ["cvt"], ["cvt"])
                STT(cvt[:, 0:SEQ], cur[:, 2:SEQ + 2], cm_(2), cvt[:, 0:SEQ], ALU.mult, ALU.add, curk + ["cvt"], ["cvt"])
                TS(cvt[:, SEQ:NT], csT[:, 0, j, :], cm_(0), None, ALU.mult, None, [("csT", 0), "cvt"], ["cvts"])
                STT(cvt[:, SEQ:NT], csT[:, 1, j, :], cm_(1), cvt[:, SEQ:NT], ALU.mult, ALU.add, [("csT", 1), "cvts"], ["cvts"])
                STT(cvt[:, SEQ:NT], cur[:, 2 + SEQ:2 + NT], cm_(2), cvt[:, SEQ:NT], ALU.mult, ALU.add, curk + ["cvts"], ["cvts"])
                CP(scp[:, j, :], cur[:, SEQ:SEQ + 2], curk, [("scp", j)], eng="act")
                CP(scs[:, j, :], cur[:, 2 + SEQ:2 + NT], curk, [("scs", j)], eng="act")
                for t, (t0, tn) in enumerate(TILES):
                    pb_, pbk = PS()
                    for k in range(8):
                        MM(pb_[:, 0:tn], wB[:, k, cl], xb[:, k, t0:t0 + tn], k == 0, k == 7, [wBk, ("xb", k, t), RB], [pbk])
                    TT(u_[:, j, t0:t0 + tn], pb_[:, 0:tn], cvt[:, t0:t0 + tn], ALU.mult, [pbk, "cvt", "cvts", RB], [("u", j, t)])
        DMA("sp", sc_p[l], scp, [("scp", j) for j in range(4)], [], "sc_p")
        DMA("sp", sc_s[l, 1], scs, [("scs", j) for j in range(4)], [], "sc_s1")
        cv = Carver()
        hmT = cv.take([128, 8, NT], BF16)
        u_ = cv.take([128, 4, NT], BF16)
        mg = cv.take([128, 2, NT], BF16)
        t512 = [cv.take([128, 512], F32) for _ in range(3)]
        lnsc = [cv.take([128, 512], F32) for _ in range(2)]
        RB = ("RB2", l)
        barrier(RB, bscr)
        for gq in range(4):
            wa, wak = WLOAD(w_pa[l][:, gq * 256:(gq + 1) * 256], 4, 256)
            wb_, wbk = WLOAD(w_pb[l][:, gq * 256:(gq + 1) * 256], 8, 256)
            wga, wgak = WLOAD(w_in[l][:, O_GA + gq * 256:O_GA + (gq + 1) * 256], 8, 256)
            wgb, wgbk = WLOAD(w_in[l][:, O_GB + gq * 256:O_GB + (gq + 1) * 256], 8, 256)
            for oc in range(2):
                cl = slice(oc * 128, (oc + 1) * 128)
                for t, (t0, tn) in enumerate(TILES):
                    p1, p1k = PS()
                    for k in range(8):
                        MM(p1[:, 0:tn], wga[:, k, cl], xb[:, k, t0:t0 + tn], k == 0, k == 7, [wgak, ("xb", k, t), RB], [p1k])
                    ACT(t512[0][:, 0:tn], p1[:, 0:tn], AF.Sigmoid, [p1k], ["t512_0"])
                    p2, p2k = PS()
                    for k in range(4):
                        MM(p2[:, 0:tn], wa[:, k, cl], u_[:, k, t0:t0 + tn], k == 0, k == 3, [wak, ("u", k, t), RB], [p2k])
                    TT(t512[1][:, 0:tn], t512[0][:, 0:tn], p2[:, 0:tn], ALU.mult, ["t512_0", p2k], ["t512_1"])
                    p3, p3k = PS()
                    for k in range(8):
                        MM(p3[:, 0:tn], wgb[:, k, cl], xb[:, k, t0:t0 + tn], k == 0, k == 7, [wgbk, ("xb", k, t), RB], [p3k])
                    ACT(t512[2][:, 0:tn], p3[:, 0:tn], AF.Sigmoid, [p3k], ["t512_2"])
                    p4, p4k = PS()
                    for k in range(8):
                        MM(p4[:, 0:tn], wb_[:, k, cl], hmT[:, k, t0:t0 + tn], k == 0, k == 7, [wbk, ("hmT", k, t), RB], [p4k])
                    TT(t512[2][:, 0:tn], t512[2][:, 0:tn], p4[:, 0:tn], ALU.mult, ["t512_2", p4k], ["t512_2"])
                    TT(mg[:, oc, t0:t0 + tn], t512[1][:, 0:tn], t512[2][:, 0:tn], ALU.add, ["t512_1", "t512_2", RB],
                       [("mg", oc, t)])
            wm, wmk = WLOAD(w_mx[l][gq * 256:(gq + 1) * 256, :], 2, 1024)
            for oc in range(8):
                for t, (t0, tn) in enumerate(TILES):
                    p5, p5k = PS()
                    for k in range(2):
                        MM(p5[:, 0:tn], wm[:, k, oc * 128:(oc + 1) * 128], mg[:, k, t0:t0 + tn], k == 0, k == 1,
                           [wmk, ("mg", k, t), RB], [p5k])
                    xk = ("x32", oc, t)
                    if gq == 0:
                        STT(x32[:, oc, t0:t0 + tn], x32[:, oc, t0:t0 + tn], ALPHA, p5[:, 0:tn], ALU.mult, ALU.add,
                            [xk, p5k], [xk])
                    else:
                        TT(x32[:, oc, t0:t0 + tn], x32[:, oc, t0:t0 + tn], p5[:, 0:tn], ALU.add, [xk, p5k], [xk])

        def layer_norm(goff, boff, RK, t512, lnsc):
            for t, (t0, tn) in enumerate(TILES):
                ps_s, pssk = PS()
                ps_q, psqk = PS()
                for oc in range(8):
                    xk = ("x32", oc, t)
                    bsel = oc % 2
                    yb_ = lnsc[bsel].bitcast(BF16)[:, 0:tn]
                    yq_ = lnsc[bsel].bitcast(BF16)[:, 512:512 + tn]
                    CP(yb_, x32[:, oc, t0:t0 + tn], [xk, RK], [("lnyb", bsel)], eng="dve")
                    ACT(yq_, x32[:, oc, t0:t0 + tn], AF.Square, [xk, RK], [("lnyq", bsel)])
                    MM(ps_s[:, 0:tn], onesb, yb_, oc == 0, oc == 7, [("lnyb", bsel), "onesb"], [pssk])
                    MM(ps_q[:, 0:tn], onesb, yq_, oc == 0, oc == 7, [("lnyq", bsel), "onesb"], [psqk])
                mean = t512[1][:, 0:tn]
                rstd = t512[2][:, 0:tn]
                ACT(mean, ps_s[:, 0:tn], AF.Identity, [pssk], ["lnmean"], scale=1.0 / D)
                TT(rstd, mean, mean, ALU.mult, ["lnmean"], ["lnrstd"])
                STT(rstd, ps_q[:, 0:tn], 1.0 / D, rstd, ALU.mult, ALU.subtract, [psqk, "lnrstd"], ["lnrstd"])
                ACT(rstd, rstd, AF.Ln, ["lnrstd"], ["lnrstd"], bias=EPS)
                ACT(rstd, rstd, AF.Exp, ["lnrstd"], ["lnrstd"], scale=-0.5)
                tmp = t512[0][:, 0:tn]
                for oc in range(8):
                    xk = ("x32", oc, t)
                    TT(tmp, x32[:, oc, t0:t0 + tn], mean, ALU.subtract, [xk, "lnmean"], ["lntmp"])
                    TT(tmp, tmp, rstd, ALU.mult, ["lntmp", "lnrstd"], ["lntmp"])
                    ACT(x32[:, oc, t0:t0 + tn], tmp, AF.Identity, ["lntmp", "pp"], [xk],
                        bias=ppc(boff, l, oc), scale=ppc(goff, l, oc))
                    ACT(xb[:, oc, t0:t0 + tn], tmp, AF.Identity, ["lntmp", "pp"], [("xb", oc, t)],
                        bias=ppc(boff, l, oc), scale=ppc(goff, l, oc))

        layer_norm(PP_LN, PP_LN + 32, RB, t512, lnsc)

        cv = Carver()
        hbuf = [cv.take([128, 4, NT], BF16) for _ in range(2)]
        upb = [cv.take([128, 2 + NT], F32) for _ in range(2)]
        cvb = [cv.take([128, NT], F32) for _ in range(2)]
        t512c = [cv.take([128, 512], F32) for _ in range(3)]
        lnscc = [cv.take([128, 512], F32) for _ in range(2)]
        cfT = cv.take([128, 2, 2, NS_], F32)
        ffp = cv.take([128, 44, 2], F32)
        ffs = cv.take([128, 44, NS_], F32)
        RC = ("RC", l)
        barrier(RC, bscr)
        DMA("sp", ff_s[l, 0], cf_in[l, 1], [], [], "ff_s0")
        for gu in range(2):
            MSET(upb[gu][:, 0:2], 0.0, [("upb0", gu)])
        groups = [list(range(a, min(a + 4, 22))) for a in range(0, 22, 4)]
        for gi, grp in enumerate(groups):
            hb = hbuf[gi % 2]
            hk = lambda jj, t: ("h", gi % 2, jj, t)
            for pi in range(0, len(grp), 2):
                prs = grp[pi:pi + 2]
                j0 = prs[0]
                ncol = 128 * len(prs)
                wg_, wgk_ = WLOAD(w_up[l][:, j0 * 128:j0 * 128 + ncol], 8, ncol)
                wu_, wuk_ = WLOAD(w_up[l][:, DFF + j0 * 128:DFF + j0 * 128 + ncol], 8, ncol)
                for pj, j in enumerate(prs):
                    jj = j - grp[0]
                    cl = slice(pj * 128, (pj + 1) * 128)
                    for gu, (ww, wwk) in enumerate(((wg_, wgk_), (wu_, wuk_))):
                        chunk = j + gu * 22
                        ub = upb[gu]
                        cb_ = cvb[gu]
                        DMA("sp", cfT[:, :, gu, :], cf_in[l, :, :, chunk, :].rearrange("r p b -> p r b"), [RC],
                            [("cfT", gu)], ("cfT", gu))
                        for t, (t0, tn) in enumerate(TILES):
                            pu, puk = PS()
                            for k in range(8):
                                MM(pu[:, 0:tn], ww[:, k, cl], xb[:, k, t0:t0 + tn], k == 0, k == 7,
                                   [wwk, ("xb", k, t), RC], [puk])
                            CP(ub[:, 2 + t0:2 + t0 + tn], pu[:, 0:tn], [puk, ("upb0", gu), RC], [("upb", gu, t)], eng="act")
                        fc = lambda jtap: pp[:, PP_FC + (l * 3 + jtap) * 44 + chunk:PP_FC + (l * 3 + jtap) * 44 + chunk + 1]
                        ubk = [("upb", gu, t) for t in range(5)] + [("upb0", gu)]
                        cvk = ("cvb", gu)
                        ACT(cb_[:, 0:SEQ], ub[:, 0:SEQ], AF.Identity, ubk + ["pp", RC], [cvk], scale=fc(0))
                        STT(cb_[:, 0:SEQ], ub[:, 1:SEQ + 1], fc(1), cb_[:, 0:SEQ], ALU.mult, ALU.add, ubk + [cvk], [cvk])
                        STT(cb_[:, 0:SEQ], ub[:, 2:SEQ + 2], fc(2), cb_[:, 0:SEQ], ALU.mult, ALU.add, ubk + [cvk], [cvk])
                        cvks = ("cvbs", gu)
                        TS(cb_[:, SEQ:NT], cfT[:, 0, gu, :], fc(0), None, ALU.mult, None, [("cfT", gu), cvk], [cvks])
                        STT(cb_[:, SEQ:NT], cfT[:, 1, gu, :], fc(1), cb_[:, SEQ:NT], ALU.mult, ALU.add, [("cfT", gu), cvks], [cvks])
                        STT(cb_[:, SEQ:NT], ub[:, 2 + SEQ:2 + NT], fc(2), cb_[:, SEQ:NT], ALU.mult, ALU.add, ubk + [cvks], [cvks])
                        CP(ffp[:, chunk, :], ub[:, SEQ:SEQ + 2], ubk, [("ffp", chunk)], eng="act")
                        CP(ffs[:, chunk, :], ub[:, 2 + SEQ:2 + NT], ubk, [("ffs", chunk)], eng="act")
                    ACT(cvb[0], cvb[0], AF.Silu, [("cvb", 0), ("cvbs", 0)], [("cvb", 0), ("cvbs", 0)])
                    for t, (t0, tn) in enumerate(TILES):
                        TT(hb[:, jj, t0:t0 + tn], cvb[0][:, t0:t0 + tn], cvb[1][:, t0:t0 + tn], ALU.mult,
                           [("cvb", 0), ("cvbs", 0), ("cvb", 1), ("cvbs", 1), RC], [hk(jj, t)])
            kcn = len(grp)
            for half in range(2):
                wd, wdk = WLOAD(w_dn[l][grp[0] * 128:(grp[0] + kcn) * 128, half * 512:(half + 1) * 512], kcn, 512)
                for o4 in range(4):
                    oc = half * 4 + o4
                    for t, (t0, tn) in enumerate(TILES):
                        p6, p6k = PS()
                        for k in range(kcn):
                            MM(p6[:, 0:tn], wd[:, k, o4 * 128:(o4 + 1) * 128], hb[:, k, t0:t0 + tn], k == 0, k == kcn - 1,
                               [wdk, hk(k, t), RC], [p6k])
                        xk = ("x32", oc, t)
                        if gi == 0:
                            STT(x32[:, oc, t0:t0 + tn], x32[:, oc, t0:t0 + tn], ALPHA, p6[:, 0:tn], ALU.mult, ALU.add,
                                [xk, p6k], [xk])
                        else:
                            TT(x32[:, oc, t0:t0 + tn], x32[:, oc, t0:t0 + tn], p6[:, 0:tn], ALU.add, [xk, p6k], [xk])
        DMA("sp", ff_p[l], ffp, [("ffp", c) for c in range(44)], [], "ff_p")
        DMA("sp", ff_s[l, 1], ffs, [("ffs", c) for c in range(44)], [], "ff_s1")
        layer_norm(PP_LN + 64, PP_LN + 96, RC, t512c, lnscc)

    yTv = yT.rearrange("(c p) n -> p c n", p=128)
    for c in range(8):
        DMA("sp", yTv[:, c, :], x32[:, c, :], [("x32", c, t) for t in range(5)], [], ("yT", c))

    P.resolve()
    with nc.Block() as block:
        P.emit(block)
    return nc


_NC_CACHE = {}


def _consts():
    c = np.zeros((128, C_N), np.float32)
    idx = np.arange(128)
    c[:, C_ID:C_ID + 128] = np.eye(128, dtype=np.float32)
    c[:, C_U:C_U + 128] = (idx[:, None] <= idx[None, :]).astype(np.float32)
    c[:, C_MTS:C_MTS + 128] = np.where(idx[None, :] <= idx[:, None], 0.0, NEG)
    c[127, C_S127:C_S127 + 128] = 1.0
    c[:, C_MST:C_MST + 128] = np.where(idx[:, None] <= idx[None, :], 0.0, NEG)
    c[:, C_I16:C_I16 + 256] = np.eye(16, dtype=np.float32).reshape(1, 256)
    return c


def kernel(x_prompt, x_sample, cache_sconv, state_mlstm_C, state_mlstm_n, state_mlstm_m, cache_ffn_conv,
           w_in, b_igate, b_fgate, w_conv_mix, mhln_g, w_proj_a, w_proj_b, w_mix_out, ln1_g, ln1_b,
           w_ffn_up, w_ffn_conv, w_ffn_down, ln2_g, ln2_b):
    f = lambda a: np.ascontiguousarray(np.asarray(a, dtype=np.float32))
    x_prompt, x_sample = f(x_prompt), f(x_sample)
    cache_sconv, cache_ffn_conv = f(cache_sconv), f(cache_ffn_conv)
    state_mlstm_C, state_mlstm_n, state_mlstm_m = f(state_mlstm_C), f(state_mlstm_n), f(state_mlstm_m)
    NCORE = 8
    pp = np.zeros((128, PP_N), np.float32)
    for i, a in enumerate((ln1_g, ln1_b, ln2_g, ln2_b)):
        pp[:, PP_LN + i * 32:PP_LN + (i + 1) * 32] = f(a).reshape(L, 8, 128).transpose(2, 0, 1).reshape(128, 32)
    pp[:, PP_CM:PP_CM + 48] = f(w_conv_mix).reshape(L, 3, 4, 128).transpose(3, 0, 1, 2).reshape(128, 48)
    pp[:, PP_FC:PP_FC + 528] = f(w_ffn_conv).reshape(L, 3, 44, 128).transpose(3, 0, 1, 2).reshape(128, 528)
    gb = np.concatenate([f(b_igate), f(b_fgate)], axis=1).reshape(1, L * 8)
    pp[:, PP_GB:PP_GB + 32] = np.broadcast_to(gb, (128, 32))
    cst = _consts()
    shared = {"pp": pp, "cst": cst, "mhln_g": f(mhln_g), "w_in": f(w_in), "w_proj_a": f(w_proj_a),
              "w_proj_b": f(w_proj_b), "w_mix_out": f(w_mix_out), "w_ffn_up": f(w_ffn_up), "w_ffn_down": f(w_ffn_down)}
    in_maps = []
    for c in range(NCORE):
        sl = slice(c * NS_, (c + 1) * NS_)
        xT = np.ascontiguousarray(np.concatenate([x_prompt[c].T, x_sample[sl, 0, :].T], axis=1))
        cs = np.ascontiguousarray(cache_sconv[:, sl].reshape(L, NS_, 2, 4, 128).transpose(0, 2, 4, 3, 1))
        cf = np.ascontiguousarray(cache_ffn_conv[:, sl].reshape(L, NS_, 2, 44, 128).transpose(0, 2, 4, 3, 1))
        m = dict(shared)
        m.update({"xT": xT, "cs_in": cs, "cf_in": cf,
                  "Cs_in": np.ascontiguousarray(state_mlstm_C[:, sl]),
                  "ns_in": np.ascontiguousarray(state_mlstm_n[:, sl].reshape(L, NS_, H * DH)),
                  "ms_in": np.ascontiguousarray(state_mlstm_m[:, sl])})
        in_maps.append(m)
    if "nc" not in _NC_CACHE:
        _NC_CACHE["nc"] = build_nc()
    nc = _NC_CACHE["nc"]
    res = run_bass_kernel_spmd(nc, in_maps, core_ids=list(range(NCORE)))
    R = res.results
    y_p = np.stack([R[c]["yT"][:, :SEQ].T for c in range(NCORE)])
    y_s = np.concatenate([R[c]["yT"][:, SEQ:].T for c in range(NCORE)])[:, None, :]
    def conv_p(key, C):
        return np.stack([R[c][key].transpose(0, 3, 2, 1).reshape(L, 2, C * 128) for c in range(NCORE)], axis=1)

    def conv_s(key, C):
        return np.concatenate([R[c][key].transpose(0, 4, 1, 3, 2).reshape(L, NS_, 2, C * 128) for c in range(NCORE)], axis=1)

    sp_ = conv_p("sc_p", 4)
    ss_ = conv_s("sc_s", 4)
    fp_ = conv_p("ff_p", 44)
    fs_ = conv_s("ff_s", 44)
    Cp = np.stack([R[c]["Cp_o"] for c in range(NCORE)], axis=1)
    Cs = np.concatenate([R[c]["Cs_o"] for c in range(NCORE)], axis=1)
    np_ = np.stack([R[c]["np_o"].transpose(0, 1, 3, 2).reshape(L, H, DH) for c in range(NCORE)], axis=1)
    ns_ = np.concatenate([R[c]["ns_o"].reshape(L, NS_, H, DH) for c in range(NCORE)], axis=1)
    mp_ = np.stack([R[c]["mp_o"] for c in range(NCORE)], axis=1)
    ms_ = np.concatenate([R[c]["ms_o"] for c in range(NCORE)], axis=1)
    out = (y_p, y_s, sp_, ss_, Cp, Cs, np_, ns_, mp_, ms_, fp_, fs_)
    return tuple(np.ascontiguousarray(o, dtype=np.float32) for o in out)
```

```python
import numpy as np
import concourse.bass as bass
import concourse.mybir as mybir
from concourse.bass_utils import run_bass_kernel_spmd

F32 = mybir.dt.float32
BF16 = mybir.dt.bfloat16
AF = mybir.ActivationFunctionType
ALU = mybir.AluOpType
AX = mybir.AxisListType

L = 4
D = 1024
SEQ = 2048
NS_ = 16
NT = SEQ + NS_
H = 4
DH = 256
DFF = 2816
DIN = 7688
EPS = 1e-5
ALPHA = (2.0 * L) ** 0.25
TILES = [(0, 512), (512, 512), (1024, 512), (1536, 512), (2048, 16)]
O_B, O_C, O_XV, O_Q, O_K, O_V, O_O, O_IG, O_GA, O_GB = 0, 512, 1024, 1536, 2560, 3584, 4608, 5632, 5640, 6664
NSLOT = 5
NEG = -1.0e30

PP_LN = 0
PP_CM = 128
PP_FC = 176
PP_GB = 704
PP_N = 736
C_ID, C_U, C_MTS, C_S127, C_MST, C_I16, C_N = 0, 128, 256, 384, 512, 640, 896

ENG = ("pe", "act", "dve", "pool", "sp")


class Op:
    __slots__ = ("eng", "fn", "r", "w", "dma", "chain", "waits", "inc", "idx")

    def __init__(self, eng, fn, r, w, dma, chain):
        self.eng, self.fn, self.r, self.w, self.dma, self.chain = eng, fn, r, w, dma, chain
        self.waits = []
        self.inc = None
        self.idx = -1


class Prog:
    def __init__(self, nc):
        self.nc = nc
        self.ops = []

    def op(self, eng, fn, r=(), w=(), dma=False, chain=None):
        o = Op(eng, fn, tuple(r), tuple(w), dma, chain)
        o.idx = len(self.ops)
        self.ops.append(o)
        return o

    def resolve(self):
        last_w, readers = {}, {}
        n = len(self.ops)
        deps = [None] * n
        has_dep = [False] * n
        for o in self.ops:
            d = set()
            for k in o.r:
                lw = last_w.get(k)
                if lw is not None:
                    d.add(lw)
            for k in o.w:
                lw = last_w.get(k)
                if lw is not None:
                    d.add(lw)
                rl = readers.get(k)
                if rl:
                    d.update(rl)
            d.discard(o.idx)
            if o.eng == "pe" and not o.dma:
                d = {x for x in d if not (self.ops[x].eng == "pe" and not self.ops[x].dma)}
            latest = {}
            for x in d:
                ox = self.ops[x]
                sk = ("c", ox.chain) if ox.dma else ("e", ox.eng)
                if x > latest.get(sk, -1):
                    latest[sk] = x
            d = set(latest.values())
            deps[o.idx] = d
            for x in d:
                has_dep[x] = True
            for k in o.w:
                last_w[k] = o.idx
                readers[k] = []
            for k in o.r:
                readers.setdefault(k, []).append(o.idx)
        eng_cnt = {e: 0 for e in ENG}
        chain_cnt = {}
        self.chains = []
        ms = [None] * n
        for o in self.ops:
            if o.dma:
                c = chain_cnt.get(o.chain, 0) + 16
                chain_cnt[o.chain] = c
                if c == 16:
                    self.chains.append(o.chain)
                ms[o.idx] = (("c", o.chain), c)
                o.inc = ("c", o.chain)
            elif has_dep[o.idx]:
                eng_cnt[o.eng] += 1
                ms[o.idx] = (("e", o.eng), eng_cnt[o.eng])
                o.inc = ("e", o.eng)
        self.eng_cnt, self.chain_cnt = eng_cnt, chain_cnt
        seen = {e: {} for e in ENG}
        for o in self.ops:
            need = {}
            for x in deps[o.idx]:
                s, v = ms[x]
                if v > need.get(s, 0):
                    need[s] = v
            sn = seen[o.eng]
            for s, v in need.items():
                if sn.get(s, 0) >= v:
                    continue
                sn[s] = v
                o.waits.append((s, v))
        return self

    def emit(self, block):
        nc = self.nc
        sems = {}
        for e in ENG:
            if self.eng_cnt[e] > 0:
                sems[("e", e)] = nc.alloc_semaphore("se_" + e)
        for i, c in enumerate(self.chains):
            sems[("c", c)] = nc.alloc_semaphore("sc_%d" % i)
        per = {e: [o for o in self.ops if o.eng == e] for e in ENG}

        def run(eh, ename):
            for o in per[ename]:
                for s, v in o.waits:
                    eh.wait_ge(sems[s], v)
                ins = o.fn(eh)
                if o.inc is not None:
                    ins.then_inc(sems[o.inc], 16 if o.dma else 1)
            if ename == "sp":
                for c in self.chains:
                    eh.wait_ge(sems[("c", c)], self.chain_cnt[c])

        @block.tensor
        def _(e):
            run(e, "pe")

        @block.scalar
        def _(e):
            run(e, "act")

        @block.vector
        def _(e):
            run(e, "dve")

        @block.gpsimd
        def _(e):
            run(e, "pool")

        @block.sync
        def _(e):
            run(e, "sp")


def build_nc():
    nc = bass.Bass("TRN2", target_bir_lowering=False)

    def din(name, shape):
        return nc.dram_tensor(name, list(shape), F32, kind="ExternalInput").ap()

    def dout(name, shape):
        return nc.dram_tensor(name, list(shape), F32, kind="ExternalOutput").ap()

    xT = din("xT", [D, NT])
    pp_d = din("pp", [128, PP_N])
    cst_d = din("cst", [128, C_N])
    mh_d = din("mhln_g", [L, D])
    w_in = din("w_in", [L, D, DIN])
    w_pa = din("w_proj_a", [L, 512, D])
    w_pb = din("w_proj_b", [L, D, D])
    w_mx = din("w_mix_out", [L, D, D])
    w_up = din("w_ffn_up", [L, D, 2 * DFF])
    w_dn = din("w_ffn_down", [L, DFF, D])
    cs_in = din("cs_in", [L, 2, 128, 4, NS_])
    cf_in = din("cf_in", [L, 2, 128, 44, NS_])
    Cs_in = din("Cs_in", [L, NS_, H, DH, DH])
    ns_in = din("ns_in", [L, NS_, H * DH])
    ms_in = din("ms_in", [L, NS_, H])

    yT = dout("yT", [D, NT])
    sc_p = dout("sc_p", [L, 128, 4, 2])
    sc_s = dout("sc_s", [L, 2, 128, 4, NS_])
    Cp_o = dout("Cp_o", [L, H, DH, DH])
    Cs_o = dout("Cs_o", [L, NS_, H, DH, DH])
    np_o = dout("np_o", [L, H, 128, 2])
    ns_o = dout("ns_o", [L, NS_, H * DH])
    mp_o = dout("mp_o", [L, H])
    ms_o = dout("ms_o", [L, NS_, H])
    ff_p = dout("ff_p", [L, 128, 44, 2])
    ff_s = dout("ff_s", [L, 2, 128, 44, NS_])

    P = Prog(nc)

    def sb(name, shape, dt=F32):
        return nc.alloc_sbuf_tensor("sb_" + name, list(shape), dt).ap()

    x32 = sb("x32", [128, 8, NT])
    xb = sb("xb", [128, 8, NT], BF16)
    slots = [sb("ws%d" % i, [128, 2048], BF16) for i in range(NSLOT)]
    S = nc.alloc_sbuf_tensor("S", [128, 79 * 1024], mybir.dt.uint8)
    pp = sb("pp", [128, PP_N])
    cst = sb("cst", [128, C_N])
    identb = sb("identb", [128, 128], BF16)
    mstb = sb("mstb", [128, 128], BF16)
    onesb = sb("onesb", [128, 128], BF16)
    ones32 = sb("ones32", [128, 128])
    gbc = sb("gbc", [128, DH])
    psum = [nc.alloc_psum_tensor("ps%d" % i, [128, 512], F32).ap() for i in range(8)]

    class Carver:
        def __init__(self):
            self.off = 0

        def take(self, shape, dt):
            esz = 2 if dt == BF16 else 4
            n = 1
            for s in shape[1:]:
                n *= s
            nbytes = (n * esz + 31) // 32 * 32
            assert self.off + nbytes <= 79 * 1024, (self.off, nbytes)
            v = S[:, self.off:self.off + nbytes].bitcast(dt)[:, 0:n]
            self.off += nbytes
            if len(shape) == 3:
                v = v.rearrange("p (a b) -> p a b", b=shape[2])
            elif len(shape) == 4:
                v = v.rearrange("p (a b c) -> p a b c", b=shape[2], c=shape[3])
            return v[0:shape[0]]

    state = {"ps": 0, "ws": 0, "ev": 0, "tok": None}
    _raw_op = P.op

    def _op(eng, fn, r=(), w=(), dma=False, chain=None):
        r = list(r)
        if state["tok"] is not None and eng != "pool":
            r.append(state["tok"])
        return _raw_op(eng, fn, r, w, dma, chain)

    P.op = _op

    def PS():
        i = state["ps"] % 7
        state["ps"] += 1
        return psum[i], ("ps", i)

    def PSL():
        return psum[7], ("ps", 7)

    def MM(out, lhsT, rhs, start, stop, r, w):
        P.op("pe", lambda e: e.matmul(out, lhsT=lhsT, rhs=rhs, start=start, stop=stop), r=r, w=w)

    def TR(out, in_, ident, r, w):
        P.op("pe", lambda e: e.transpose(out, in_, ident), r=r, w=w)

    def ACT(out, in_, func, r, w, bias=0.0, scale=1.0):
        P.op("act", lambda e: e.activation(out=out, in_=in_, func=func, bias=bias, scale=scale), r=r, w=w)

    def TT(out, in0, in1, op, r, w, eng="dve"):
        P.op(eng, lambda e: e.tensor_tensor(out=out, in0=in0, in1=in1, op=op), r=r, w=w)

    def TS(out, in0, s1, s2, op0, op1, r, w):
        if s2 is None:
            P.op("dve", lambda e: e.tensor_scalar(out=out, in0=in0, scalar1=s1, scalar2=None, op0=op0), r=r, w=w)
        else:
            P.op("dve", lambda e: e.tensor_scalar(out=out, in0=in0, scalar1=s1, scalar2=s2, op0=op0, op1=op1), r=r, w=w)

    def STT(out, in0, scalar, in1, op0, op1, r, w):
        P.op("dve", lambda e: e.scalar_tensor_tensor(out=out, in0=in0, scalar=scalar, in1=in1, op0=op0, op1=op1),
             r=r, w=w)

    def CP(out, in_, r, w, eng=None):
        if eng is None:
            eng = "act" if state["ev"] % 2 == 0 else "dve"
            state["ev"] += 1
        if eng == "act":
            P.op("act", lambda e: e.activation(out=out, in_=in_, func=AF.Copy), r=r, w=w)
        else:
            P.op("dve", lambda e: e.tensor_copy(out=out, in_=in_), r=r, w=w)

    def PTS(out, in0, s1, r, w):
        P.op("act", lambda e: e.activation(out=out, in_=in0, func=AF.Identity, scale=s1), r=r, w=w)

    def PCP(out, in_, r, w):
        P.op("act", lambda e: e.activation(out=out, in_=in_, func=AF.Copy), r=r, w=w)

    def MSET(ap, val, w, eng="dve"):
        P.op(eng, lambda e: e.memset(ap, val), w=w)

    def DMA(q, out, in_, r, w, chain):
        P.op(q, lambda e: e.dma_start(out=out, in_=in_), r=r, w=w, dma=True, chain=chain)

    def WLOAD(src2d, KC, ncols):
        i = state["ws"] % NSLOT
        state["ws"] += 1
        v = slots[i][:, 0:KC * ncols].rearrange("p (k n) -> p k n", n=ncols)
        key = ("ws", i)
        DMA("pool", v, src2d.rearrange("(k p) n -> p k n", p=128), r=[], w=[key], chain=key)
        return v, key

    def xbk(t):
        return [("xb", k, t) for k in range(8)]

    def barrier(newtok, scratch):
        old = state["tok"]
        wk = [newtok] + ([old] if old is not None else [])
        _raw_op("dve", lambda e: e.memset(scratch, 0.0), [], wk, False, None)
        state["tok"] = newtok

    DMA("sp", pp, pp_d, [], ["pp"], "pp")
    DMA("sp", cst, cst_d, [], ["cst"], "cst")
    CP(identb, cst[:, C_ID:C_ID + 128], ["cst"], ["identb"], eng="dve")
    CP(mstb, cst[:, C_MST:C_MST + 128], ["cst"], ["mstb"], eng="dve")
    MSET(onesb, 1.0, ["onesb"])
    MSET(ones32, 1.0, ["ones32"])
    ident32 = cst[:, C_ID:C_ID + 128]
    Utri = cst[:, C_U:C_U + 128]
    mts = cst[:, C_MTS:C_MTS + 128]
    sel127 = cst[:, C_S127:C_S127 + 128]
    I16bc = cst[:, C_I16:C_I16 + 256].rearrange("p (a b) -> p a b", b=16)
    xTv = xT.rearrange("(c p) n -> p c n", p=128)
    for c in range(8):
        DMA("sp", x32[:, c, :], xTv[:, c, :], [], [("x32", c, t) for t in range(5)], ("x32i", c))
        DMA("pool", xb[:, c, :], xTv[:, c, :], [], [("xb", c, t) for t in range(5)], ("xbi", c))
    bscr = sb("bscr", [128, 8])

    def ppc(base, l, i):
        return pp[:, base + l * 8 + i: base + l * 8 + i + 1]

    for l in range(L):
        cv = Carver()
        hmT = cv.take([128, 8, NT], BF16)
        qT = cv.take([128, 2, 512], BF16)
        kT = cv.take([128, 2, 512], BF16)
        ktok = cv.take([128, 4, DH], BF16)
        vtok = cv.take([128, 4, DH + 2], BF16)
        gsig = cv.take([128, 4, DH], F32)
        Et = cv.take([128, 128], F32)
        PTt = cv.take([128, 128], BF16)
        diag = cv.take([128, 128], F32)
        tmp128 = Et
        numB = cv.take([128, DH + 1], F32)
        num = cv.take([128, DH + 1], F32)
        hn = cv.take([128, DH], F32)
        hg = cv.take([128, DH], BF16)
        Kw = cv.take([128, DH], BF16)
        C32 = cv.take([128, 2, DH], F32)
        n32 = cv.take([128, 2], F32)
        Cb = cv.take([128, 2, DH + 2], BF16)
        st6 = cv.take([128, 6], F32)
        mv = cv.take([128, 2], F32)
        sc4 = cv.take([128, 4], F32)
        G_ = {}
        for nm in ("gp", "li", "sp", "bloc", "tot", "ginc", "G", "a", "cm", "cmx", "M", "nM", "fl", "sI", "wS",
                   "dec", "t1"):
            G_[nm] = cv.take([128, 16, 4], F32)
        MT = cv.take([128, 17, 4], F32)
        q_s = cv.take([NS_, DH], F32)
        k_s = cv.take([NS_, DH], F32)
        v_s = cv.take([NS_, DH], F32)
        g_s = cv.take([NS_, DH], F32)
        t_s_full = cv.take([128, DH], F32)[:, 0:128]
        kw_s_full = cv.take([128, DH], F32)[:, 0:128]
        vm_s_full = cv.take([128, DH], F32)[:, 0:128]
        cv.off -= 3 * 1024
        t_s = cv.take([NS_, DH], F32)
        kw_s = cv.take([NS_, DH], F32)
        vm_s = cv.take([NS_, DH], BF16)
        kwb_s = cv.take([NS_, DH], BF16)
        nold_s = cv.take([NS_, DH], F32)
        nnew_s = cv.take([NS_, DH], F32)
        num_s = cv.take([NS_, DH], F32)
        hg_s = cv.take([NS_, DH], BF16)
        sg = {}
        for nm in ("gps", "li", "lf", "m", "mn", "dec", "wg", "fl", "qk", "qn", "s", "den", "rdn", "t"):
            sg[nm] = cv.take([NS_, 8], F32)
        Dm = cv.take([NS_, 4, NS_], F32)
        decs_bc = cv.take([128, 4, NS_], F32)
        qTs = cv.take([128, 2, NS_], F32)
        Qm = cv.take([128, 2, NS_, NS_], F32)
        Cst = [cv.take([128, 2, DH], F32) for _ in range(3)]
        st6s = cv.take([NS_, 6], F32)
        mvs = cv.take([NS_, 2], F32)
        RA = ("RA", l)
        barrier(RA, bscr)

        gw, gwk = WLOAD(w_in[l][:, O_IG:O_IG + 8], 8, 8)
        gps, gpk = PS()
        gpsv = gps[:, 0:128].rearrange("p (c g) -> p c g", g=8)
        for c in range(16):
            for k in range(8):
                MM(gpsv[:, c, :], xb[:, k, c * 128:(c + 1) * 128], gw[:, k, :], k == 0, k == 7,
                   [gwk, ("xb", k, c // 4), RA], [gpk])
        bi_bc = pp[:, PP_GB + l * 8:PP_GB + l * 8 + 4]
        bf_bc = pp[:, PP_GB + l * 8 + 4:PP_GB + l * 8 + 8]
        g = G_
        K = lambda nm: ("g", nm)
        TT(g["li"], gpsv[:, :, 0:4], bi_bc.unsqueeze(1).broadcast_to([128, 16, 4]), ALU.add, [gpk, "pp", RA], [K("li")])
        TT(g["t1"], gpsv[:, :, 4:8], bf_bc.unsqueeze(1).broadcast_to([128, 16, 4]), ALU.add, [gpk, "pp", RA], [K("t1")])
        ACT(g["sp"], g["t1"], AF.Exp, [K("t1")], [K("sp")], scale=-1.0)
        ACT(g["sp"], g["sp"], AF.Ln, [K("sp")], [K("sp")], bias=1.0)
        f64 = lambda t: t.rearrange("p c h -> p (c h)")
        ps1, pk1 = PS()
        MM(ps1[:, 0:64], Utri, f64(g["sp"]), True, True, [K("sp"), "cst"], [pk1])
        CP(f64(g["bloc"]), ps1[:, 0:64], [pk1], [K("bloc")], eng="dve")
        ps2, pk2 = PS()
        MM(ps2[:, 0:64], ones32, f64(g["sp"]), True, True, [K("sp"), "ones32"], [pk2])
        CP(f64(g["tot"]), ps2[:, 0:64], [pk2], [K("tot")], eng="dve")
        for h in range(H):
            P.op("dve", lambda e, h=h: e.tensor_tensor_scan(out=g["ginc"][:, :, h], data0=g["tot"][:, :, h],
                                                            data1=g["tot"][:, :, h], initial=0.0,
                                                            op0=ALU.add, op1=ALU.bypass),
                 r=[K("tot")], w=[K("ginc")])
        TT(g["G"], g["ginc"], g["tot"], ALU.subtract, [K("ginc"), K("tot")], [K("G")])
        TT(g["G"], g["G"], g["bloc"], ALU.add, [K("G"), K("bloc")], [K("G")])
        TT(g["a"], g["li"], g["G"], ALU.add, [K("li"), K("G")], [K("a")])
        dgs = [(diag, "diag"), (t_s_full, "t_s"), (kw_s_full, "kw_s"), (vm_s_full, "vm_s")]
        items = [(c, hh_) for c in range(16) for hh_ in range(H)]

        def cm_front(i):
            c, hh_ = items[i]
            dg, dk = dgs[i % 4]
            TS(dg, ident32, g["a"][:, c, hh_:hh_ + 1], None, ALU.mult, None, [K("a"), "cst"], [dk])

        LOOK = 3
        for i in range(min(LOOK, len(items))):
            cm_front(i)
        for i, (c, hh_) in enumerate(items):
            dg, dk = dgs[i % 4]
            psd, pkd = PS()
            MM(psd[:, 0:128], ones32, dg, True, True, [dk, "ones32"], [pkd])
            if i + LOOK < len(items):
                cm_front(i + LOOK)
            TT(tmp128, psd[:, 0:128], mts, ALU.add, [pkd, "cst"], ["Et"])
            P.op("dve", lambda e, c=c, hh_=hh_: e.tensor_reduce(out=g["cm"][:, c, hh_:hh_ + 1], in_=tmp128, axis=AX.X,
                                                                op=ALU.max), r=["Et"], w=[K("cm")])
        ps3, pk3 = PS()
        MM(ps3[:, 0:64], sel127, f64(g["cm"]), True, True, [K("cm"), "cst"], [pk3])
        CP(f64(g["cmx"]), ps3[:, 0:64], [pk3], [K("cmx")], eng="dve")
        MSET(MT[:, 0, :], 0.0, [K("MT")])
        for h in range(H):
            P.op("dve", lambda e, h=h: e.tensor_tensor_scan(out=MT[:, 1:17, h], data0=g["cmx"][:, :, h],
                                                            data1=g["cmx"][:, :, h], initial=0.0,
                                                            op0=ALU.max, op1=ALU.bypass),
                 r=[K("cmx"), K("MT")], w=[K("MT")])
        Mprev = MT[:, 0:16, :]
        Mend = MT[:, 1:17, :]
        TT(g["M"], g["cm"], Mprev, ALU.max, [K("cm"), K("MT")], [K("M")])
        TS(g["nM"], g["M"], -1.0, None, ALU.mult, None, [K("M")], [K("nM")])
        TT(g["t1"], g["G"], g["M"], ALU.subtract, [K("G"), K("M"), K("sp")], [K("t1")])
        ACT(g["fl"], g["t1"], AF.Exp, [K("t1")], [K("fl")])
        TT(g["sI"], Mprev, g["M"], ALU.subtract, [K("MT"), K("M")], [K("sI")])
        ACT(g["sI"], g["sI"], AF.Exp, [K("sI")], [K("sI")])
        TT(g["wS"], g["a"], Mend, ALU.subtract, [K("a"), K("MT")], [K("wS")])
        ACT(g["wS"], g["wS"], AF.Exp, [K("wS")], [K("wS")])
        TT(g["dec"], Mprev, Mend, ALU.subtract, [K("MT")], [K("dec")])
        ACT(g["dec"], g["dec"], AF.Exp, [K("dec")], [K("dec")])
        TT(sc4, MT[:, 16, :], g["ginc"][:, 15, :], ALU.subtract, [K("MT"), K("ginc")], ["sc4"])
        DMA("sp", mp_o[l:l + 1, :], sc4[0:1, :], ["sc4"], [], "mp_o")

        s = sg
        SK = lambda nm: ("sg", nm)
        psg, pkg = PS()
        for k in range(8):
            MM(psg[0:NS_, 0:8], xb[:, k, SEQ:NT], gw[:, k, :], k == 0, k == 7, [gwk, ("xb", k, 4), RA], [pkg])
        DMA("sp", s["m"][:, 0:4], ms_in[l], [RA], [SK("m")], "ms_in")
        TT(s["li"][:, 0:4], psg[0:NS_, 0:4], bi_bc[0:NS_], ALU.add, [pkg, "pp"], [SK("li")])
        TT(s["t"][:, 0:4], psg[0:NS_, 4:8], bf_bc[0:NS_], ALU.add, [pkg, "pp"], [SK("t")])
        ACT(s["lf"][:, 0:4], s["t"][:, 0:4], AF.Exp, [SK("t")], [SK("lf")], scale=-1.0)
        ACT(s["lf"][:, 0:4], s["lf"][:, 0:4], AF.Ln, [SK("lf")], [SK("lf")], bias=1.0)
        TT(s["t"][:, 0:4], s["m"][:, 0:4], s["lf"][:, 0:4], ALU.subtract, [SK("m"), SK("lf"), SK("t")], [SK("t")])
        TT(s["mn"][:, 0:4], s["t"][:, 0:4], s["li"][:, 0:4], ALU.max, [SK("t"), SK("li")], [SK("mn")])
        TT(s["dec"][:, 0:4], s["t"][:, 0:4], s["mn"][:, 0:4], ALU.subtract, [SK("t"), SK("mn")], [SK("dec")])
        ACT(s["dec"][:, 0:4], s["dec"][:, 0:4], AF.Exp, [SK("dec")], [SK("dec")])
        TT(s["wg"][:, 0:4], s["li"][:, 0:4], s["mn"][:, 0:4], ALU.subtract, [SK("li"), SK("mn")], [SK("wg")])
        ACT(s["wg"][:, 0:4], s["wg"][:, 0:4], AF.Exp, [SK("wg")], [SK("wg")])
        ACT(s["fl"][:, 0:4], s["mn"][:, 0:4], AF.Exp, [SK("mn")], [SK("fl")], scale=-1.0)
        DMA("sp", ms_o[l], s["mn"][:, 0:4], [SK("mn")], [], "ms_o")
        TT(Dm, s["dec"][:, 0:4].unsqueeze(2).broadcast_to([NS_, 4, NS_]),
           ident32[0:NS_, 0:NS_].unsqueeze(1).broadcast_to([NS_, 4, NS_]), ALU.mult, [SK("dec"), "cst"], ["Dm"])
        psb, pkb = PS()
        MM(psb[:, 0:64], ones32[0:NS_, :], Dm.rearrange("p h b -> p (h b)"), True, True, ["Dm", "ones32"], [pkb])
        CP(decs_bc.rearrange("p h b -> p (h b)"), psb[:, 0:64], [pkb], ["decs_bc"], eng="dve")

        for h in range(H):
            wq, wqk = WLOAD(w_in[l][:, O_Q + h * DH:O_Q + (h + 1) * DH], 8, DH)
            wk, wkk = WLOAD(w_in[l][:, O_K + h * DH:O_K + (h + 1) * DH], 8, DH)
            wv, wvk = WLOAD(w_in[l][:, O_V + h * DH:O_V + (h + 1) * DH], 8, DH)
            wo, wok = WLOAD(w_in[l][:, O_O + h * DH:O_O + (h + 1) * DH], 8, DH)
            DMA("sp", gbc, mh_d[l:l + 1, h * DH:(h + 1) * DH].broadcast_to([128, DH]), [RA], ["gbc"], "gbc")
            MSET(C32, 0.0, ["C32"])
            MSET(n32, 0.0, ["n32"])
            MSET(Cb, 0.0, ["Cb"], eng="dve")
            MSET(vtok[:, :, DH:DH + 2], 1.0, ["vtok1"])
            hh = slice(h, h + 1)
            qk_, kk_, vk_ = ("s", id(q_s)), ("s", id(k_s)), ("s", id(v_s))
            pqc, pqck = PSL()

            def sample_prep():
                for (dst, wv_, wk_, sc_) in ((q_s, wq, wqk, 1.0), (k_s, wk, wkk, 1.0 / 16.0), (v_s, wv, wvk, 1.0)):
                    pp_, ppk = PS()
                    for k in range(8):
                        MM(pp_[0:NS_, 0:DH], xb[:, k, SEQ:NT], wv_[:, k, :], k == 0, k == 7, [wk_, ("xb", k, 4), RA], [ppk])
                    ACT(dst, pp_[0:NS_, 0:DH], AF.Identity, [ppk], [("s", id(dst))], scale=sc_)
                pp_, ppk = PS()
                for k in range(8):
                    MM(pp_[0:NS_, 0:DH], xb[:, k, SEQ:NT], wo[:, k, :], k == 0, k == 7, [wok, ("xb", k, 4), RA], [ppk])
                ACT(g_s, pp_[0:NS_, 0:DH], AF.Sigmoid, [ppk], ["g_s"])
                TT(g_s, g_s, gbc[0:NS_], ALU.mult, ["g_s", "gbc"], ["g_s"])
                qk_, kk_, vk_ = ("s", id(q_s)), ("s", id(k_s)), ("s", id(v_s))
                for dc in range(2):
                    pq, pqk = PS()
                    for k in range(8):
                        MM(pq[:, 0:NS_], wq[:, k, dc * 128:(dc + 1) * 128], xb[:, k, SEQ:NT], k == 0, k == 7,
                           [wqk, ("xb", k, 4), RA], [pqk])
                    CP(qTs[:, dc, :], pq[:, 0:NS_], [pqk], [("qTs", dc)], eng="dve")
                    TT(Qm[:, dc], qTs[:, dc, :].unsqueeze(2).broadcast_to([128, NS_, NS_]), I16bc, ALU.mult,
                       [("qTs", dc), "cst"], [("Qm", dc)])
                DMA("sp", nold_s, ns_in[l][:, h * DH:(h + 1) * DH], [RA], ["nold_s"], "nold_s")
                TT(t_s, q_s, k_s, ALU.mult, [qk_, kk_], ["t_s"])
                P.op("dve", lambda e, h=h: e.tensor_reduce(out=s["qk"][:, h:h + 1], in_=t_s, axis=AX.X, op=ALU.add),
                     r=["t_s"], w=[SK("qk")])
                TT(t_s, q_s, nold_s, ALU.mult, [qk_, "nold_s", SK("qk")], ["t_s"])
                P.op("dve", lambda e, h=h: e.tensor_reduce(out=s["qn"][:, h:h + 1], in_=t_s, axis=AX.X, op=ALU.add),
                     r=["t_s"], w=[SK("qn")])
                hh = slice(h, h + 1)
                TT(s["s"][:, hh], s["qk"][:, hh], s["wg"][:, hh], ALU.mult, [SK("qk"), SK("wg")], [SK("s")])
                TT(s["den"][:, hh], s["dec"][:, hh], s["qn"][:, hh], ALU.mult, [SK("dec"), SK("qn")], [SK("den")])
                TT(s["den"][:, hh], s["den"][:, hh], s["s"][:, hh], ALU.add, [SK("den"), SK("s")], [SK("den")])
                STT(s["rdn"][:, hh], s["den"][:, hh], -1.0, s["den"][:, hh], ALU.mult, ALU.max, [SK("den")], [SK("rdn")])
                TS(s["rdn"][:, hh], s["rdn"][:, hh], s["fl"][:, hh], None, ALU.max, None, [SK("rdn"), SK("fl")], [SK("rdn")])
                P.op("dve", lambda e, hh=hh: e.reciprocal(out=s["rdn"][:, hh], in_=s["rdn"][:, hh]), r=[SK("rdn")], w=[SK("rdn")])
                TS(kw_s, k_s, s["wg"][:, hh], None, ALU.mult, None, [kk_, SK("wg")], ["kw_s"])
                CP(kwb_s, kw_s, ["kw_s"], ["kwb_s"], eng="act")
                STT(nnew_s, nold_s, s["dec"][:, hh], kw_s, ALU.mult, ALU.add, ["nold_s", "kw_s", SK("dec"), "t_s"], ["nnew_s"])
                DMA("sp", ns_o[l][:, h * DH:(h + 1) * DH], nnew_s, ["nnew_s"], [], "ns_o")


            def sample_step(b):
                ct = Cst[b % 3]
                ck = ("Cst", b % 3)
                if b == 0:
                    for b2 in range(2):
                        DMA("sp", Cst[b2], Cs_in[l, b2, h].rearrange("(kc p) v -> p kc v", p=128), [RA],
                            [("Cst", b2)], ("Cst_i", b2))
                if b + 2 < NS_:
                    b2 = b + 2
                    DMA("sp", Cst[b2 % 3], Cs_in[l, b2, h].rearrange("(kc p) v -> p kc v", p=128), [RA],
                        [("Cst", b2 % 3)], ("Cst_i", b2 % 3))
                for kc in range(2):
                    MM(pqc[0:NS_, 0:DH], Qm[:, kc, b, :], ct[:, kc, :], b == 0 and kc == 0, b == NS_ - 1 and kc == 1,
                       [("Qm", kc), ck], [pqck])
                TS(vm_s, v_s, ident32[0:NS_, b:b + 1], None, ALU.mult, None, [vk_, "cst"], ["vm_s"])
                pU, pUk = PS()
                for kc in range(2):
                    MM(pU[:, kc * DH:(kc + 1) * DH], kwb_s[:, kc * 128:(kc + 1) * 128], vm_s, True, True,
                       ["kwb_s", "vm_s"], [pUk])
                STT(ct.rearrange("p a b -> p (a b)"), ct.rearrange("p a b -> p (a b)"), decs_bc[:, h, b:b + 1], pU,
                    ALU.mult, ALU.add, [ck, pUk, "decs_bc"], [ck])
                DMA("act", Cs_o[l, b, h].rearrange("(kc p) v -> p kc v", p=128), ct, [ck], [], ("Cst_o", b % 3))

            def sample_tail():
                TS(num_s, v_s, s["s"][:, hh], None, ALU.mult, None, [vk_, SK("s")], ["num_s"])
                STT(num_s, pqc[0:NS_, 0:DH], s["dec"][:, hh], num_s, ALU.mult, ALU.add, [pqck, "num_s", SK("dec")], ["num_s"])
                TS(num_s, num_s, s["rdn"][:, hh], None, ALU.mult, None, ["num_s", SK("rdn")], ["num_s"])
                P.op("dve", lambda e: e.bn_stats(out=st6s, in_=num_s), r=["num_s"], w=["st6s"])
                P.op("dve", lambda e: e.bn_aggr(out=mvs, in_=st6s), r=["st6s"], w=["mvs"])
                ACT(s["t"][:, 4:5], mvs[:, 1:2], AF.Ln, ["mvs"], [SK("t2")], bias=EPS)
                ACT(s["t"][:, 4:5], s["t"][:, 4:5], AF.Exp, [SK("t2")], [SK("t2")], scale=-0.5)
                TS(num_s, num_s, mvs[:, 0:1], s["t"][:, 4:5], ALU.subtract, ALU.mult, ["num_s", "mvs", SK("t2")], ["num_s"])
                TT(hg_s, num_s, g_s, ALU.mult, ["num_s", "g_s"], ["hg_s"])
                ptp, ptpk = PS()
                ptb = ptp.bitcast(BF16)
                for dc in range(2):
                    TR(ptb[:, dc * NS_:(dc + 1) * NS_], hg_s[:, dc * 128:(dc + 1) * 128], identb[0:NS_, 0:NS_],
                       ["hg_s", "identb"], [ptpk])
                CP(hmT[:, 2 * h:2 * h + 2, SEQ:NT], ptb[:, 0:2 * NS_].rearrange("p (a b) -> p a b", b=NS_), [ptpk, RA],
                   [("hmT", 2 * h, 4), ("hmT", 2 * h + 1, 4)])
            sample_prep()
            for tt in range(4):
                t0 = tt * 512
                for dc in range(2):
                    pq, pqk = PS()
                    for k in range(8):
                        MM(pq, wq[:, k, dc * 128:(dc + 1) * 128], xb[:, k, t0:t0 + 512], k == 0, k == 7,
                           [wqk, ("xb", k, tt), RA], [pqk])
                    CP(qT[:, dc, :], pq, [pqk], [("qT", dc)])
                    pk_, pkk = PS()
                    for k in range(8):
                        MM(pk_, wk[:, k, dc * 128:(dc + 1) * 128], xb[:, k, t0:t0 + 512], k == 0, k == 7,
                           [wkk, ("xb", k, tt), RA], [pkk])
                    ACT(kT[:, dc, :], pk_, AF.Identity, [pkk], [("kT", dc)], scale=1.0 / 16.0)
                for ci in range(4):
                    c0 = t0 + ci * 128
                    pa, pak = PS()
                    for k in range(8):
                        MM(pa[:, 0:DH], xb[:, k, c0:c0 + 128], wk[:, k, :], k == 0, k == 7,
                           [wkk, ("xb", k, tt), RA], [pak])
                    ACT(ktok[:, ci, :], pa[:, 0:DH], AF.Identity, [pak], [("ktok", ci)], scale=1.0 / 16.0)
                    pb_, pbk = PS()
                    for k in range(8):
                        MM(pb_[:, 0:DH], xb[:, k, c0:c0 + 128], wv[:, k, :], k == 0, k == 7,
                           [wvk, ("xb", k, tt), RA], [pbk])
                    CP(vtok[:, ci, 0:DH], pb_[:, 0:DH], [pbk], [("vtok", ci)], eng="dve")
                    pc_, pck = PS()
                    for k in range(8):
                        MM(pc_[:, 0:DH], xb[:, k, c0:c0 + 128], wo[:, k, :], k == 0, k == 7,
                           [wok, ("xb", k, tt), RA], [pck])
                    ACT(gsig[:, ci, :], pc_[:, 0:DH], AF.Sigmoid, [pck], [("gsig", ci)])
                    TT(gsig[:, ci, :], gsig[:, ci, :], gbc, ALU.mult, [("gsig", ci), "gbc"], [("gsig", ci)])
                XB, XK = psum[4], ("ps", 4)
                UB, UK = psum[5], ("ps", 5)
                TB, TK = psum[6], ("ps", 6)
                SB_, SK_ = XB, XK

                def front_a(ci):
                    c = tt * 4 + ci
                    cs = slice(ci * 128, (ci + 1) * 128)
                    pB, pBk = psum[2 + c % 2], ("ps", 2 + c % 2)
                    PTS(diag, ident32, g["nM"][:, c, h:h + 1], [K("nM"), "cst"], ["diag"])
                    PTS(Kw, ktok[:, ci, :], g["wS"][:, c, h:h + 1], [("ktok", ci), K("wS")], ["Kw"])
                    for dc in range(2):
                        MM(SB_[:, 0:128], kT[:, dc, cs], qT[:, dc, cs], dc == 0, dc == 1, [("kT", dc), ("qT", dc)], [SK_])
                    MM(XB[:, 128:256], ones32, diag, True, False, ["diag", "ones32"], [XK])
                    MM(XB[:, 128:256], identb, mstb, False, True, ["identb", "mstb"], [XK])
                    for kc in range(2):
                        MM(XB[:, 256 + kc:257 + kc], Kw[:, kc * 128:(kc + 1) * 128], vtok[:, ci, DH:DH + 1], True, True,
                           ["Kw", "vtok1"], [XK])
                    for kc in range(2):
                        MM(UB[:, kc * DH:(kc + 1) * DH], Kw[:, kc * 128:(kc + 1) * 128], vtok[:, ci, 0:DH], True, True,
                           ["Kw", ("vtok", ci)], [UK])

                def front_a2(ci):
                    c = tt * 4 + ci
                    cs = slice(ci * 128, (ci + 1) * 128)
                    pB, pBk = psum[2 + c % 2], ("ps", 2 + c % 2)
                    for kc in range(2):
                        MM(pB[:, 0:DH + 1], qT[:, kc, cs], Cb[:, kc, 0:DH + 1], kc == 0, kc == 1, [("qT", kc), "Cb"], [pBk])
                    STT(C32.rearrange("p a b -> p (a b)"), C32.rearrange("p a b -> p (a b)"), g["dec"][:, c, h:h + 1],
                        UB, ALU.mult, ALU.add, ["C32", UK, K("dec")], ["C32"])
                    STT(n32, n32, g["dec"][:, c, h:h + 1], XB[:, 256:258], ALU.mult, ALU.add, ["n32", XK, K("dec")], ["n32"])
                    PCP(Cb[:, :, 0:DH], C32, ["C32"], ["Cb"])
                    PCP(Cb[:, :, DH:DH + 1], n32.unsqueeze(2), ["n32", "Cb"], ["Cb"])
                    ACT(Et, XB[:, 128:256], AF.Exp, [XK, K("a")], ["Et"], bias=g["a"][:, c, h:h + 1])

                def front_b(ci):
                    c = tt * 4 + ci
                    pA, pAk = psum[c % 2], ("ps", c % 2)
                    TT(PTt, SB_[:, 0:128], Et, ALU.mult, [SK_, "Et"], ["PTt"])
                    MM(pA[:, 0:DH + 1], PTt, vtok[:, ci, 0:DH + 1], True, True, ["PTt", ("vtok", ci), "vtok1"], [pAk])

                def tail_a(ci):
                    c = tt * 4 + ci
                    pA, pAk = psum[c % 2], ("ps", c % 2)
                    pB, pBk = psum[2 + c % 2], ("ps", 2 + c % 2)
                    ACT(numB, pB[:, 0:DH + 1], AF.Identity, [pBk, K("sI")], ["numB"], scale=g["sI"][:, c, h:h + 1])
                    TT(num, pA[:, 0:DH + 1], numB, ALU.add, [pAk, "numB"], ["num"])
                    STT(mv[:, 0:1], num[:, DH:DH + 1], -1.0, num[:, DH:DH + 1], ALU.mult, ALU.max, ["num"], ["rdn"])
                    TS(mv[:, 0:1], mv[:, 0:1], g["fl"][:, c, h:h + 1], None, ALU.max, None, ["rdn", K("fl")], ["rdn"])
                    P.op("dve", lambda e: e.bn_stats(out=st6, in_=num[:, 0:DH]), r=["num"], w=["st6"])
                    P.op("dve", lambda e: e.bn_aggr(out=sc4[:, 0:2], in_=st6), r=["st6"], w=["sc4"])
                    TT(sc4[:, 2:3], mv[:, 0:1], mv[:, 0:1], ALU.mult, ["rdn", "sc4"], ["sc4b"])
                    STT(sc4[:, 2:3], sc4[:, 2:3], EPS, sc4[:, 1:2], ALU.mult, ALU.add, ["sc4b", "sc4"], ["sc4b"])
                    ACT(sc4[:, 2:3], sc4[:, 2:3], AF.Ln, ["sc4b"], ["sc4b"])
                    ACT(sc4[:, 2:3], sc4[:, 2:3], AF.Exp, ["sc4b"], ["sc4b"], scale=-0.5)

                def tail_b(ci):
                    c = tt * 4 + ci
                    TS(hn, num[:, 0:DH], sc4[:, 0:1], sc4[:, 2:3], ALU.subtract, ALU.mult, ["num", "sc4", "sc4b"], ["hn"])
                    TT(hg, hn, gsig[:, ci, :], ALU.mult, ["hn", ("gsig", ci)], ["hg"])
                    ptb = TB.bitcast(BF16)
                    for dc in range(2):
                        TR(ptb[:, dc * 128:(dc + 1) * 128], hg[:, dc * 128:(dc + 1) * 128], identb, ["hg", "identb"], [TK])
                    CP(hmT[:, 2 * h:2 * h + 2, c * 128:(c + 1) * 128],
                       ptb[:, 0:256].rearrange("p (a b) -> p a b", b=128), [TK, RA],
                       [("hmT", 2 * h, tt), ("hmT", 2 * h + 1, tt)])

                front_a(0)
                front_a2(0)
                front_b(0)
                for ci in range(1, 4):
                    front_a(ci)
                    tail_a(ci - 1)
                    front_a2(ci)
                    front_b(ci)
                    tail_b(ci - 1)
                tail_a(3)
                tail_b(3)
                for b_ in range(4 * tt, 4 * tt + 4):
                    sample_step(b_)
            DMA("sp", Cp_o[l, h].rearrange("(kc p) v -> p kc v", p=128), C32, ["C32"], [], "Cp_o")
            DMA("sp", np_o[l, h], n32, ["n32"], [], "np_o")
            sample_tail()


        cv = Carver()
        hmT = cv.take([128, 8, NT], BF16)
        u_ = cv.take([128, 4, NT], BF16)
        cur = cv.take([128, 2 + NT], F32)
        cvt = cv.take([128, NT], F32)
        t512 = [cv.take([128, 512], F32) for _ in range(1)]
        csT = cv.take([128, 2, 4, NS_], F32)
        scp = cv.take([128, 4, 2], F32)
        scs = cv.take([128, 4, NS_], F32)
        RB = ("RB", l)
        barrier(RB, bscr)
        DMA("sp", csT[:, 0], cs_in[l, 0], [RB], [("csT", 0)], ("csT", 0))
        DMA("sp", csT[:, 1], cs_in[l, 1], [RB], [("csT", 1)], ("csT", 1))
        DMA("sp", sc_s[l, 0], cs_in[l, 1], [], [], "sc_s0")
        MSET(cur[:, 0:2], 0.0, ["cur0"])
        for jp in range(2):
            wB, wBk = WLOAD(w_in[l][:, O_B + jp * 256:O_B + (jp + 1) * 256], 8, 256)
            wC, wCk = WLOAD(w_in[l][:, O_C + jp * 256:O_C + (jp + 1) * 256], 8, 256)
            wX, wXk = WLOAD(w_in[l][:, O_XV + jp * 256:O_XV + (jp + 1) * 256], 8, 256)
            for jj in range(2):
                j = jp * 2 + jj
                cl = slice(jj * 128, (jj + 1) * 128)
                for t, (t0, tn) in enumerate(TILES):
                    pc_, pck = PS()
                    for k in range(8):
                        MM(pc_[:, 0:tn], wC[:, k, cl], xb[:, k, t0:t0 + tn], k == 0, k == 7, [wCk, ("xb", k, t), RB], [pck])
                    px_, pxk = PS()
                    for k in range(8):
                        MM(px_[:, 0:tn], wX[:, k, cl], xb[:, k, t0:t0 + tn], k == 0, k == 7, [wXk, ("xb", k, t), RB], [pxk])
                    ACT(t512[0][:, 0:tn], pc_[:, 0:tn], AF.Copy, [pck], ["t512_0"])
                    TT(cur[:, 2 + t0:2 + t0 + tn], t512[0][:, 0:tn], px_[:, 0:tn], ALU.mult, ["t512_0", pxk, "cur0", RB],
                       [("cur", t)])
                cm_ = lambda jtap: pp[:, PP_CM + (l * 3 + jtap) * 4 + j:PP_CM + (l * 3 + jtap) * 4 + j + 1]
                curk = [("cur", t) for t in range(5)]
                ACT(cvt[:, 0:SEQ], cur[:, 0:SEQ], AF.Identity, curk + ["cur0", "pp"], ["cvt"], scale=cm_(0))
                STT(cvt[:, 0:SEQ], cur[:, 1:SEQ + 1], cm_(1), cvt[:, 0:SEQ], ALU.mult, ALU.add, curk + ["cvt"], ["cvt"])
                STT(cvt[:, 0:SEQ], cur[:, 2:SEQ + 2], cm_(2), cvt[:, 0:SEQ], ALU.mult, ALU.add, curk + ["cvt"], ["cvt"])
                TS(cvt[:, SEQ:NT], csT[:, 0, j, :], cm_(0), None, ALU.mult, None, [("csT", 0), "cvt"], ["cvts"])
                STT(cvt[:, SEQ:NT], csT[:, 1, j, :], cm_(1), cvt[:, SEQ:NT], ALU.mult, ALU.add, [("csT", 1), "cvts"], ["cvts"])
                STT(cvt[:, SEQ:NT], cur[:, 2 + SEQ:2 + NT], cm_(2), cvt[:, SEQ:NT], ALU.mult, ALU.add, curk + ["cvts"], ["cvts"])
                CP(scp[:, j, :], cur[:, SEQ:SEQ + 2], curk, [("scp", j)], eng="act")
                CP(scs[:, j, :], cur[:, 2 + SEQ:2 + NT], curk, [("scs", j)], eng="act")
                for t, (t0, tn) in enumerate(TILES):
                    pb_, pbk = PS()
                    for k in range(8):
                        MM(pb_[:, 0:tn], wB[:, k, cl], xb[:, k, t0:t0 + tn], k == 0, k == 7, [wBk, ("xb", k, t), RB], [pbk])
                    TT(u_[:, j, t0:t0 + tn], pb_[:, 0:tn], cvt[:, t0:t0 + tn], ALU.mult, [pbk, "cvt", "cvts", RB], [("u", j, t)])
        DMA("sp", sc_p[l], scp, [("scp", j) for j in range(4)], [], "sc_p")
        DMA("sp", sc_s[l, 1], scs, [("scs", j) for j in range(4)], [], "sc_s1")
        cv = Carver()
        hmT = cv.take([128, 8, NT], BF16)
        u_ = cv.take([128, 4, NT], BF16)
        mgs = [cv.take([128, 2, NT], BF16) for _ in range(2)]
        t512 = [cv.take([128, 512], F32) for _ in range(3)]
        lnsc = [cv.take([128, 512], F32) for _ in range(2)]
        RB = ("RB2", l)
        barrier(RB, bscr)
        for gq in range(4):
            wa, wak = WLOAD(w_pa[l][:, gq * 256:(gq + 1) * 256], 4, 256)
            wb_, wbk = WLOAD(w_pb[l][:, gq * 256:(gq + 1) * 256], 8, 256)
            wga, wgak = WLOAD(w_in[l][:, O_GA + gq * 256:O_GA + (gq + 1) * 256], 8, 256)
            wgb, wgbk = WLOAD(w_in[l][:, O_GB + gq * 256:O_GB + (gq + 1) * 256], 8, 256)
            for oc in range(2):
                cl = slice(oc * 128, (oc + 1) * 128)
                for t, (t0, tn) in enumerate(TILES):
                    p1, p1k = PS()
                    for k in range(8):
                        MM(p1[:, 0:tn], wga[:, k, cl], xb[:, k, t0:t0 + tn], k == 0, k == 7, [wgak, ("xb", k, t), RB], [p1k])
                    ACT(t512[0][:, 0:tn], p1[:, 0:tn], AF.Sigmoid, [p1k], ["t512_0"])
                    p2, p2k = PS()
                    for k in range(4):
                        MM(p2[:, 0:tn], wa[:, k, cl], u_[:, k, t0:t0 + tn], k == 0, k == 3, [wak, ("u", k, t), RB], [p2k])
                    TT(t512[1][:, 0:tn], t512[0][:, 0:tn], p2[:, 0:tn], ALU.mult, ["t512_0", p2k], ["t512_1"])
                    p3, p3k = PS()
                    for k in range(8):
                        MM(p3[:, 0:tn], wgb[:, k, cl], xb[:, k, t0:t0 + tn], k == 0, k == 7, [wgbk, ("xb", k, t), RB], [p3k])
                    ACT(t512[2][:, 0:tn], p3[:, 0:tn], AF.Sigmoid, [p3k], ["t512_2"])
                    p4, p4k = PS()
                    for k in range(8):
                        MM(p4[:, 0:tn], wb_[:, k, cl], hmT[:, k, t0:t0 + tn], k == 0, k == 7, [wbk, ("hmT", k, t), RB], [p4k])
                    TT(t512[2][:, 0:tn], t512[2][:, 0:tn], p4[:, 0:tn], ALU.mult, ["t512_2", p4k], ["t512_2"])
                    TT(mgs[gq % 2][:, oc, t0:t0 + tn], t512[1][:, 0:tn], t512[2][:, 0:tn], ALU.add, ["t512_1", "t512_2", RB],
                       [("mg", gq % 2, oc, t)])
            wm, wmk = WLOAD(w_mx[l][gq * 256:(gq + 1) * 256, :], 2, 1024)
            for oc in range(8):
                for t, (t0, tn) in enumerate(TILES):
                    p5, p5k = PS()
                    for k in range(2):
                        MM(p5[:, 0:tn], wm[:, k, oc * 128:(oc + 1) * 128], mgs[gq % 2][:, k, t0:t0 + tn], k == 0, k == 1,
                           [wmk, ("mg", gq % 2, k, t), RB], [p5k])
                    xk = ("x32", oc, t)
                    if gq == 0:
                        STT(x32[:, oc, t0:t0 + tn], x32[:, oc, t0:t0 + tn], ALPHA, p5[:, 0:tn], ALU.mult, ALU.add,
                            [xk, p5k], [xk])
                    else:
                        TT(x32[:, oc, t0:t0 + tn], x32[:, oc, t0:t0 + tn], p5[:, 0:tn], ALU.add, [xk, p5k], [xk])

        def layer_norm(goff, boff, RK, t512, lnsc):
            for t, (t0, tn) in enumerate(TILES):
                ps_s, pssk = PS()
                ps_q, psqk = PS()
                for oc in range(8):
                    xk = ("x32", oc, t)
                    bsel = oc % 2
                    yb_ = lnsc[bsel].bitcast(BF16)[:, 0:tn]
                    yq_ = lnsc[bsel].bitcast(BF16)[:, 512:512 + tn]
                    CP(yb_, x32[:, oc, t0:t0 + tn], [xk, RK], [("lnyb", bsel)], eng="dve")
                    ACT(yq_, x32[:, oc, t0:t0 + tn], AF.Square, [xk, RK], [("lnyq", bsel)])
                    MM(ps_s[:, 0:tn], onesb, yb_, oc == 0, oc == 7, [("lnyb", bsel), "onesb"], [pssk])
                    MM(ps_q[:, 0:tn], onesb, yq_, oc == 0, oc == 7, [("lnyq", bsel), "onesb"], [psqk])
                mean = t512[1][:, 0:tn]
                rstd = t512[2][:, 0:tn]
                ACT(mean, ps_s[:, 0:tn], AF.Identity, [pssk], ["lnmean"], scale=1.0 / D)
                TT(rstd, mean, mean, ALU.mult, ["lnmean"], ["lnrstd"])
                STT(rstd, ps_q[:, 0:tn], 1.0 / D, rstd, ALU.mult, ALU.subtract, [psqk, "lnrstd"], ["lnrstd"])
                ACT(rstd, rstd, AF.Ln, ["lnrstd"], ["lnrstd"], bias=EPS)
                ACT(rstd, rstd, AF.Exp, ["lnrstd"], ["lnrstd"], scale=-0.5)
                tmp = t512[0][:, 0:tn]
                for oc in range(8):
                    xk = ("x32", oc, t)
                    TT(tmp, x32[:, oc, t0:t0 + tn], mean, ALU.subtract, [xk, "lnmean"], ["lntmp"])
                    TT(tmp, tmp, rstd, ALU.mult, ["lntmp", "lnrstd"], ["lntmp"])
                    ACT(x32[:, oc, t0:t0 + tn], tmp, AF.Identity, ["lntmp", "pp"], [xk],
                        bias=ppc(boff, l, oc), scale=ppc(goff, l, oc))
                    ACT(xb[:, oc, t0:t0 + tn], tmp, AF.Identity, ["lntmp", "pp"], [("xb", oc, t)],
                        bias=ppc(boff, l, oc), scale=ppc(goff, l, oc))

        layer_norm(PP_LN, PP_LN + 32, RB, t512, lnsc)

        cv = Carver()
        hbuf = [cv.take([128, 4, NT], BF16) for _ in range(2)]
        upb = [cv.take([128, 2 + NT], F32) for _ in range(2)]
        cvb = [cv.take([128, NT], F32) for _ in range(2)]
        t512c = [cv.take([128, 512], F32) for _ in range(3)]
        lnscc = [cv.take([128, 512], F32) for _ in range(2)]
        cfT = cv.take([128, 2, 2, NS_], F32)
        ffp = cv.take([128, 44, 2], F32)
        ffs = cv.take([128, 44, NS_], F32)
        RC = ("RC", l)
        barrier(RC, bscr)
        DMA("sp", ff_s[l, 0], cf_in[l, 1], [], [], "ff_s0")
        for gu in range(2):
            MSET(upb[gu][:, 0:2], 0.0, [("upb0", gu)])
        groups = [list(range(a, min(a + 4, 22))) for a in range(0, 22, 4)]
        for gi, grp in enumerate(groups):
            hb = hbuf[gi % 2]
            hk = lambda jj, t: ("h", gi % 2, jj, t)
            for pi in range(0, len(grp), 2):
                prs = grp[pi:pi + 2]
                j0 = prs[0]
                ncol = 128 * len(prs)
                wg_, wgk_ = WLOAD(w_up[l][:, j0 * 128:j0 * 128 + ncol], 8, ncol)
                wu_, wuk_ = WLOAD(w_up[l][:, DFF + j0 * 128:DFF + j0 * 128 + ncol], 8, ncol)
                for pj, j in enumerate(prs):
                    jj = j - grp[0]
                    cl = slice(pj * 128, (pj + 1) * 128)
                    for gu, (ww, wwk) in enumerate(((wg_, wgk_), (wu_, wuk_))):
                        chunk = j + gu * 22
                        ub = upb[gu]
                        cb_ = cvb[gu]
                        DMA("sp", cfT[:, :, gu, :], cf_in[l, :, :, chunk, :].rearrange("r p b -> p r b"), [RC],
                            [("cfT", gu)], ("cfT", gu))
                        for t, (t0, tn) in enumerate(TILES):
                            pu, puk = PS()
                            for k in range(8):
                                MM(pu[:, 0:tn], ww[:, k, cl], xb[:, k, t0:t0 + tn], k == 0, k == 7,
                                   [wwk, ("xb", k, t), RC], [puk])
                            CP(ub[:, 2 + t0:2 + t0 + tn], pu[:, 0:tn], [puk, ("upb0", gu), RC], [("upb", gu, t)], eng="act")
                        fc = lambda jtap: pp[:, PP_FC + (l * 3 + jtap) * 44 + chunk:PP_FC + (l * 3 + jtap) * 44 + chunk + 1]
                        ubk = [("upb", gu, t) for t in range(5)] + [("upb0", gu)]
                        cvk = ("cvb", gu)
                        ACT(cb_[:, 0:SEQ], ub[:, 0:SEQ], AF.Identity, ubk + ["pp", RC], [cvk], scale=fc(0))
                        STT(cb_[:, 0:SEQ], ub[:, 1:SEQ + 1], fc(1), cb_[:, 0:SEQ], ALU.mult, ALU.add, ubk + [cvk], [cvk])
                        STT(cb_[:, 0:SEQ], ub[:, 2:SEQ + 2], fc(2), cb_[:, 0:SEQ], ALU.mult, ALU.add, ubk + [cvk], [cvk])
                        cvks = ("cvbs", gu)
                        TS(cb_[:, SEQ:NT], cfT[:, 0, gu, :], fc(0), None, ALU.mult, None, [("cfT", gu), cvk], [cvks])
                        STT(cb_[:, SEQ:NT], cfT[:, 1, gu, :], fc(1), cb_[:, SEQ:NT], ALU.mult, ALU.add, [("cfT", gu), cvks], [cvks])
                        STT(cb_[:, SEQ:NT], ub[:, 2 + SEQ:2 + NT], fc(2), cb_[:, SEQ:NT], ALU.mult, ALU.add, ubk + [cvks], [cvks])
                        CP(ffp[:, chunk, :], ub[:, SEQ:SEQ + 2], ubk, [("ffp", chunk)], eng="act")
                        CP(ffs[:, chunk, :], ub[:, 2 + SEQ:2 + NT], ubk, [("ffs", chunk)], eng="act")
                    ACT(cvb[0], cvb[0], AF.Silu, [("cvb", 0), ("cvbs", 0)], [("cvb", 0), ("cvbs", 0)])
                    for t, (t0, tn) in enumerate(TILES):
                        TT(hb[:, jj, t0:t0 + tn], cvb[0][:, t0:t0 + tn], cvb[1][:, t0:t0 + tn], ALU.mult,
                           [("cvb", 0), ("cvbs", 0), ("cvb", 1), ("cvbs", 1), RC], [hk(jj, t)])
            kcn = len(grp)
            for half in range(2):
                wd, wdk = WLOAD(w_dn[l][grp[0] * 128:(grp[0] + kcn) * 128, half * 512:(half + 1) * 512], kcn, 512)
                for o4 in range(4):
                    oc = half * 4 + o4
                    for t, (t0, tn) in enumerate(TILES):
                        p6, p6k = PS()
                        for k in range(kcn):
                            MM(p6[:, 0:tn], wd[:, k, o4 * 128:(o4 + 1) * 128], hb[:, k, t0:t0 + tn], k == 0, k == kcn - 1,
                               [wdk, hk(k, t), RC], [p6k])
                        xk = ("x32", oc, t)
                        if gi == 0:
                            STT(x32[:, oc, t0:t0 + tn], x32[:, oc, t0:t0 + tn], ALPHA, p6[:, 0:tn], ALU.mult, ALU.add,
                                [xk, p6k], [xk])
                        else:
                            TT(x32[:, oc, t0:t0 + tn], x32[:, oc, t0:t0 + tn], p6[:, 0:tn], ALU.add, [xk, p6k], [xk])
        DMA("sp", ff_p[l], ffp, [("ffp", c) for c in range(44)], [], "ff_p")
        DMA("sp", ff_s[l, 1], ffs, [("ffs", c) for c in range(44)], [], "ff_s1")
        layer_norm(PP_LN + 64, PP_LN + 96, RC, t512c, lnscc)

    yTv = yT.rearrange("(c p) n -> p c n", p=128)
    for c in range(8):
        DMA("sp", yTv[:, c, :], x32[:, c, :], [("x32", c, t) for t in range(5)], [], ("yT", c))

    P.resolve()
    with nc.Block() as block:
        P.emit(block)
    return nc


_NC_CACHE = {}


def _consts():
    c = np.zeros((128, C_N), np.float32)
    idx = np.arange(128)
    c[:, C_ID:C_ID + 128] = np.eye(128, dtype=np.float32)
    c[:, C_U:C_U + 128] = (idx[:, None] <= idx[None, :]).astype(np.float32)
    c[:, C_MTS:C_MTS + 128] = np.where(idx[None, :] <= idx[:, None], 0.0, NEG)
    c[127, C_S127:C_S127 + 128] = 1.0
    c[:, C_MST:C_MST + 128] = np.where(idx[:, None] <= idx[None, :], 0.0, NEG)
    c[:, C_I16:C_I16 + 256] = np.eye(16, dtype=np.float32).reshape(1, 256)
    return c


def kernel(x_prompt, x_sample, cache_sconv, state_mlstm_C, state_mlstm_n, state_mlstm_m, cache_ffn_conv,
           w_in, b_igate, b_fgate, w_conv_mix, mhln_g, w_proj_a, w_proj_b, w_mix_out, ln1_g, ln1_b,
           w_ffn_up, w_ffn_conv, w_ffn_down, ln2_g, ln2_b):
    f = lambda a: np.ascontiguousarray(np.asarray(a, dtype=np.float32))
    x_prompt, x_sample = f(x_prompt), f(x_sample)
    cache_sconv, cache_ffn_conv = f(cache_sconv), f(cache_ffn_conv)
    state_mlstm_C, state_mlstm_n, state_mlstm_m = f(state_mlstm_C), f(state_mlstm_n), f(state_mlstm_m)
    NCORE = 8
    pp = np.zeros((128, PP_N), np.float32)
    for i, a in enumerate((ln1_g, ln1_b, ln2_g, ln2_b)):
        pp[:, PP_LN + i * 32:PP_LN + (i + 1) * 32] = f(a).reshape(L, 8, 128).transpose(2, 0, 1).reshape(128, 32)
    pp[:, PP_CM:PP_CM + 48] = f(w_conv_mix).reshape(L, 3, 4, 128).transpose(3, 0, 1, 2).reshape(128, 48)
    pp[:, PP_FC:PP_FC + 528] = f(w_ffn_conv).reshape(L, 3, 44, 128).transpose(3, 0, 1, 2).reshape(128, 528)
    gb = np.concatenate([f(b_igate), f(b_fgate)], axis=1).reshape(1, L * 8)
    pp[:, PP_GB:PP_GB + 32] = np.broadcast_to(gb, (128, 32))
    cst = _consts()
    shared = {"pp": pp, "cst": cst, "mhln_g": f(mhln_g), "w_in": f(w_in), "w_proj_a": f(w_proj_a),
              "w_proj_b": f(w_proj_b), "w_mix_out": f(w_mix_out), "w_ffn_up": f(w_ffn_up), "w_ffn_down": f(w_ffn_down)}
    in_maps = []
    for c in range(NCORE):
        sl = slice(c * NS_, (c + 1) * NS_)
        xT = np.ascontiguousarray(np.concatenate([x_prompt[c].T, x_sample[sl, 0, :].T], axis=1))
        cs = np.ascontiguousarray(cache_sconv[:, sl].reshape(L, NS_, 2, 4, 128).transpose(0, 2, 4, 3, 1))
        cf = np.ascontiguousarray(cache_ffn_conv[:, sl].reshape(L, NS_, 2, 44, 128).transpose(0, 2, 4, 3, 1))
        m = dict(shared)
        m.update({"xT": xT, "cs_in": cs, "cf_in": cf,
                  "Cs_in": np.ascontiguousarray(state_mlstm_C[:, sl]),
                  "ns_in": np.ascontiguousarray(state_mlstm_n[:, sl].reshape(L, NS_, H * DH)),
                  "ms_in": np.ascontiguousarray(state_mlstm_m[:, sl])})
        in_maps.append(m)
    if "nc" not in _NC_CACHE:
        _NC_CACHE["nc"] = build_nc()
    nc = _NC_CACHE["nc"]
    res = run_bass_kernel_spmd(nc, in_maps, core_ids=list(range(NCORE)))
    R = res.results
    y_p = np.stack([R[c]["yT"][:, :SEQ].T for c in range(NCORE)])
    y_s = np.concatenate([R[c]["yT"][:, SEQ:].T for c in range(NCORE)])[:, None, :]
    def conv_p(key, C):
        return np.stack([R[c][key].transpose(0, 3, 2, 1).reshape(L, 2, C * 128) for c in range(NCORE)], axis=1)

    def conv_s(key, C):
        return np.concatenate([R[c][key].transpose(0, 4, 1, 3, 2).reshape(L, NS_, 2, C * 128) for c in range(NCORE)], axis=1)

    sp_ = conv_p("sc_p", 4)
    ss_ = conv_s("sc_s", 4)
    fp_ = conv_p("ff_p", 44)
    fs_ = conv_s("ff_s", 44)
    Cp = np.stack([R[c]["Cp_o"] for c in range(NCORE)], axis=1)
    Cs = np.concatenate([R[c]["Cs_o"] for c in range(NCORE)], axis=1)
    np_ = np.stack([R[c]["np_o"].transpose(0, 1, 3, 2).reshape(L, H, DH) for c in range(NCORE)], axis=1)
    ns_ = np.concatenate([R[c]["ns_o"].reshape(L, NS_, H, DH) for c in range(NCORE)], axis=1)
    mp_ = np.stack([R[c]["mp_o"] for c in range(NCORE)], axis=1)
    ms_ = np.concatenate([R[c]["ms_o"] for c in range(NCORE)], axis=1)
    out = (y_p, y_s, sp_, ss_, Cp, Cs, np_, ns_, mp_, ms_, fp_, fs_)
    return tuple(np.ascontiguousarray(o, dtype=np.float32) for o in out)
```
